# Optimizing a Trainium2 kernel written in Bass

```python
import math, functools
import jax, jax.numpy as jnp
from jax import lax
import numpy as np

D_MODEL = 1024
BATCH = 2
SEQ = 8192
DEPTH = 1
DEC_BATCH = 128
DEC_SEQ = 4
PAST_LEN = 2048
PAGE_SIZE = 128

MIX_A = D_MODEL // 2
MIX_H = D_MODEL - MIX_A
A_HEAD_DIM = 64
A_HEADS = MIX_A // (2 * A_HEAD_DIM)
ROT_DIM = A_HEAD_DIM // 4
ROPE_THETA = 500000.0
H_EXPAND = 128
H_HEADS = MIX_H // H_EXPAND
D_FF = -(-8 * D_MODEL // (3 * 256)) * 256
N_IN = 3 * MIX_A + 4 * MIX_H
SPLITS = [MIX_A, 2 * MIX_A, 3 * MIX_A, 3 * MIX_A + MIX_H, 3 * MIX_A + 2 * MIX_H, 3 * MIX_A + 3 * MIX_H]
QBLK = 128
CHUNK = 64
EPS = 1e-6
SUBLN_EPS = 1e-5

kernel_name = "hymba_style_diffattn_hgrn2_step"


def rmsnorm(x, w, eps=EPS):
    xf = x.astype(jnp.float32)
    y = xf * lax.rsqrt(jnp.mean(xf * xf, axis=-1, keepdims=True) + eps)
    return (y * w.astype(jnp.float32)).astype(x.dtype)


def rope_partial(x, pos):
    half = ROT_DIM // 2
    inv_freq = 1.0 / (ROPE_THETA ** (jnp.arange(half, dtype=jnp.float32) * 2.0 / ROT_DIM))
    ang = pos.astype(jnp.float32)[:, None] * inv_freq[None, :]
    cos = jnp.cos(ang)[None, :, None, None, :]
    sin = jnp.sin(ang)[None, :, None, None, :]
    x1 = x[..., :half].astype(jnp.float32)
    x2 = x[..., half:ROT_DIM].astype(jnp.float32)
    rot = jnp.concatenate([x1 * cos - x2 * sin, x2 * cos + x1 * sin], axis=-1).astype(x.dtype)
    return jnp.concatenate([rot, x[..., ROT_DIM:]], axis=-1)


def diff_attn_prompt(q, k, v, lam):
    B, L = q.shape[0], q.shape[1]
    n_blk = L // QBLK
    scale = A_HEAD_DIM ** -0.5
    kpos = jnp.arange(L)

    def blk(i):
        qb = lax.dynamic_slice_in_dim(q, i * QBLK, QBLK, axis=1)
        s = jnp.einsum("bqhcd,bkhcd->bhcqk", qb, k).astype(jnp.float32) * scale
        qpos = i * QBLK + jnp.arange(QBLK)
        s = jnp.where(kpos[None, :] <= qpos[:, None], s, -jnp.inf)
        p = jax.nn.softmax(s, axis=-1)
        pd = p[:, :, 0] - lam * p[:, :, 1]
        return jnp.einsum("bhqk,bkhe->bqhe", pd.astype(v.dtype), v)

    o = lax.map(blk, jnp.arange(n_blk))
    return jnp.moveaxis(o, 0, 1).reshape(B, L, A_HEADS, 2 * A_HEAD_DIM)


def diff_attn_sample(q, k, v, lam, k_past, v_past):
    T = q.shape[1]
    P = k_past.shape[1]
    scale = A_HEAD_DIM ** -0.5
    s_past = jnp.einsum("bqhcd,bkhcd->bhcqk", q, k_past).astype(jnp.float32) * scale
    s_new = jnp.einsum("bqhcd,bkhcd->bhcqk", q, k).astype(jnp.float32) * scale
    s_new = jnp.where(jnp.tril(jnp.ones((T, T), bool)), s_new, -jnp.inf)
    p = jax.nn.softmax(jnp.concatenate([s_past, s_new], axis=-1), axis=-1)
    pd = (p[:, :, 0] - lam * p[:, :, 1]).astype(v.dtype)
    return (jnp.einsum("bhqk,bkhe->bqhe", pd[..., :P], v_past)
            + jnp.einsum("bhqk,bkhe->bqhe", pd[..., P:], v))


def gla_chunk_scan(q, k, v, logf, s0):
    B, L, H, DK = q.shape
    DV = v.shape[-1]
    C = math.gcd(L, CHUNK)
    n = L // C
    tri = jnp.tril(jnp.ones((C, C), bool))

    def to_chunks(t):
        return jnp.moveaxis(t.reshape(B, n, C, H, t.shape[-1]), 1, 0)

    def step(S, inp):
        qc, kc, vc, gc = inp
        b = jnp.cumsum(gc, axis=1)
        o_inter = jnp.einsum("bthk,bhkv->bthv", qc * jnp.exp(b), S)
        diff = jnp.where(tri[None, :, :, None, None], b[:, :, None] - b[:, None, :], -jnp.inf)
        A = jnp.einsum("bthk,btshk->btsh", qc, jnp.exp(diff) * kc[:, None])
        o_intra = jnp.einsum("btsh,bshv->bthv", A, vc)
        b_last = b[:, -1]
        S = S * jnp.exp(b_last)[..., None] + jnp.einsum(
            "bshk,bshv->bhkv", kc * jnp.exp(b_last[:, None] - b), vc)
        return S, o_inter + o_intra

    S, o = lax.scan(step, s0, (to_chunks(q), to_chunks(k), to_chunks(v), to_chunks(logf)))
    return jnp.moveaxis(o, 0, 1).reshape(B, L, H, DV), S


def trunk_layer(x, pos, s0, attend, attn_norm, w_in, w_out, subln_w, lam, lam_init, lb,
                hgrn_norm, ffn_norm, w_gate, w_up, w_down):
    B, L, _ = x.shape
    h = rmsnorm(x, attn_norm)
    z = h @ w_in
    qa, ka, va, qh, fh, ih, gh = jnp.split(z, SPLITS, axis=-1)
    qa = rope_partial(qa.reshape(B, L, A_HEADS, 2, A_HEAD_DIM), pos)
    ka = rope_partial(ka.reshape(B, L, A_HEADS, 2, A_HEAD_DIM), pos)
    va = va.reshape(B, L, A_HEADS, 2 * A_HEAD_DIM)
    oa = attend(qa, ka, va, lam)
    oa = (rmsnorm(oa, subln_w, SUBLN_EPS).astype(jnp.float32) * (1.0 - lam_init)).reshape(B, L, MIX_A)
    f = lb + (1.0 - lb) * jax.nn.sigmoid(fh.astype(jnp.float32))
    heads = lambda t: t.reshape(B, L, H_HEADS, H_EXPAND)
    oh, s_new = gla_chunk_scan(heads(jax.nn.silu(qh.astype(jnp.float32))), heads(1.0 - f),
                               heads(ih.astype(jnp.float32)), heads(jnp.log(f)), s0)
    oh = rmsnorm(oh.reshape(B, L, MIX_H), hgrn_norm) * jax.nn.silu(gh.astype(jnp.float32))
    mix = jnp.concatenate([oa.astype(x.dtype), oh.astype(x.dtype)], axis=-1)
    x = x + mix @ w_out
    h = rmsnorm(x, ffn_norm)
    x = x + (jax.nn.silu(h @ w_gate) * (h @ w_up)) @ w_down
    return x, ka, va, s_new


def setup_inputs(seed: int = 0) -> dict:
    key = jax.random.key(seed)
    ks = jax.random.split(key, 24)
    n_pages = PAST_LEN // PAGE_SIZE
    n_used = DEC_BATCH * n_pages
    n_pool = n_used + n_used // 4
    nrm = lambda k, shape, s: jax.random.normal(k, shape, jnp.float32) * s
    gain = lambda k, shape: 1.0 + 0.01 * jax.random.normal(k, shape, jnp.float32)
    page_table = jax.random.permutation(ks[5], n_pool)[:n_used].reshape(DEC_BATCH, n_pages).astype(jnp.int32)
    return {
        "x_prompt": nrm(ks[0], (BATCH, SEQ, D_MODEL), 1.0),
        "x_sample": nrm(ks[1], (DEC_BATCH, DEC_SEQ, D_MODEL), 1.0),
        "cache_k": nrm(ks[2], (DEPTH, n_pool, PAGE_SIZE, A_HEADS, 2, A_HEAD_DIM), 1.0),
        "cache_v": nrm(ks[3], (DEPTH, n_pool, PAGE_SIZE, A_HEADS, 2 * A_HEAD_DIM), 1.0),
        "state_hgrn": nrm(ks[4], (DEPTH, DEC_BATCH, H_HEADS, H_EXPAND, H_EXPAND), 0.3),
        "page_table": page_table,
        "attn_norm": gain(ks[6], (DEPTH, D_MODEL)),
        "w_in": nrm(ks[7], (DEPTH, D_MODEL, N_IN), D_MODEL ** -0.5),
        "w_out": nrm(ks[8], (DEPTH, MIX_A + MIX_H, D_MODEL), (MIX_A + MIX_H) ** -0.5),
        "lambda_q1": nrm(ks[9], (DEPTH, A_HEAD_DIM), 0.1),
        "lambda_k1": nrm(ks[10], (DEPTH, A_HEAD_DIM), 0.1),
        "lambda_q2": nrm(ks[11], (DEPTH, A_HEAD_DIM), 0.1),
        "lambda_k2": nrm(ks[12], (DEPTH, A_HEAD_DIM), 0.1),
        "subln_w": gain(ks[13], (DEPTH, 2 * A_HEAD_DIM)),
        "hgrn_lb_logits": nrm(ks[14], (DEPTH + 1, MIX_H), 0.5),
        "hgrn_norm": gain(ks[15], (DEPTH, MIX_H)),
        "ffn_norm": gain(ks[16], (DEPTH, D_MODEL)),
        "w_gate": nrm(ks[17], (DEPTH, D_MODEL, D_FF), D_MODEL ** -0.5),
        "w_up": nrm(ks[18], (DEPTH, D_MODEL, D_FF), D_MODEL ** -0.5),
        "w_down": nrm(ks[19], (DEPTH, D_FF, D_MODEL), D_FF ** -0.5),
        "final_norm": gain(ks[20], (D_MODEL,)),
    }


def reference(x_prompt, x_sample, cache_k, cache_v, state_hgrn, page_table, attn_norm, w_in, w_out,
              lambda_q1, lambda_k1, lambda_q2, lambda_k2, subln_w, hgrn_lb_logits, hgrn_norm,
              ffn_norm, w_gate, w_up, w_down, final_norm):
    Bd, n_pages = page_table.shape
    past_len = n_pages * cache_k.shape[2]
    pos_p = jnp.arange(x_prompt.shape[1])
    pos_s = past_len + jnp.arange(x_sample.shape[1])
    lb_all = jnp.cumsum(jax.nn.softmax(hgrn_lb_logits.astype(jnp.float32), axis=0), axis=0)
    xp, xs = x_prompt, x_sample
    kp, vp, sp, ks_, vs_, ss = [], [], [], [], [], []
    for l in range(DEPTH):
        lam_init = 0.8 - 0.6 * math.exp(-0.3 * l)
        lam = (jnp.exp(jnp.sum(lambda_q1[l].astype(jnp.float32) * lambda_k1[l].astype(jnp.float32)))
               - jnp.exp(jnp.sum(lambda_q2[l].astype(jnp.float32) * lambda_k2[l].astype(jnp.float32)))
               + lam_init)
        lw = (attn_norm[l], w_in[l], w_out[l], subln_w[l], lam, lam_init, lb_all[l],
              hgrn_norm[l], ffn_norm[l], w_gate[l], w_up[l], w_down[l])
        s0_p = jnp.zeros((xp.shape[0], H_HEADS, H_EXPAND, H_EXPAND), jnp.float32)
        xp, k_new, v_new, s_new = trunk_layer(xp, pos_p, s0_p, diff_attn_prompt, *lw)
        kp.append(k_new); vp.append(v_new); sp.append(s_new)
        k_past = cache_k[l][page_table].reshape(Bd, past_len, A_HEADS, 2, A_HEAD_DIM)
        v_past = cache_v[l][page_table].reshape(Bd, past_len, A_HEADS, 2 * A_HEAD_DIM)
        attend_s = functools.partial(diff_attn_sample, k_past=k_past, v_past=v_past)
        xs, k_new, v_new, s_new = trunk_layer(xs, pos_s, state_hgrn[l].astype(jnp.float32), attend_s, *lw)
        ks_.append(k_new); vs_.append(v_new); ss.append(s_new)
    y_prompt = rmsnorm(xp, final_norm)
    y_sample = rmsnorm(xs, final_norm)
    return (y_prompt, y_sample, jnp.stack(kp), jnp.stack(vp), jnp.stack(sp),
            jnp.stack(ks_), jnp.stack(vs_), jnp.stack(ss))
```

```python
import numpy as np
from contextlib import ExitStack
import concourse.bass as bass
import concourse.mybir as mybir
from concourse.bass_utils import run_bass_kernel_spmd

F32 = mybir.dt.float32
BF16 = mybir.dt.bfloat16
I32 = mybir.dt.int32
AF = mybir.ActivationFunctionType
ALU = mybir.AluOpType
AX = mybir.AxisListType

D = 1024
NIN = 3584
DFF = 2816
NFF = DFF // 128
ROPE_THETA = 500000.0
PAST = 2048
NPAGE = 16
EPS = 1e-6
SUBLN_EPS = 1e-5
LAM_INIT = 0.2
STOP = 99


class Buf:
    __slots__ = ("w", "r")

    def __init__(self):
        self.w = None
        self.r = {}


def bufs(n):
    return [Buf() for _ in range(n)]


class KB:
    def __init__(self, nc, es):
        self.nc = nc
        self.es = es
        self.eng = dict(pe=nc.tensor, act=nc.scalar, dve=nc.vector, pool=nc.gpsimd, sp=nc.sync)
        self.sem = {}
        self.cnt = {}
        self.seen = {}
        for k in self.eng:
            self._mksem(k)

    def _mksem(self, k):
        self.sem[k] = self.es.enter_context(self.nc.semaphore("s_" + k))
        self.cnt[k] = 0

    def op(self, e, fn, r=(), w=(), dsem=None, inc=16):
        need = {}

        def add(t):
            if t is None:
                return
            kk, v = t
            if need.get(kk, 0) < v:
                need[kk] = v

        for b in r:
            add(b.w)
        for b in w:
            add(b.w)
            for kk, v in b.r.items():
                add((kk, v))
        E = self.eng[e]
        for kk, v in need.items():
            if kk == e and e == "pe":
                continue
            if self.seen.get((e, kk), 0) >= v:
                continue
            E.wait_ge(self.sem[kk], v)
            self.seen[(e, kk)] = v
        inst = fn(E)
        if dsem is not None:
            if dsem not in self.sem:
                self._mksem(dsem)
            self.cnt[dsem] += inc
            inst.then_inc(self.sem[dsem], inc)
            tag = (dsem, self.cnt[dsem])
        else:
            self.cnt[e] += 1
            inst.then_inc(self.sem[e], 1)
            tag = (e, self.cnt[e])
        for b in r:
            if b.r.get(tag[0], 0) < tag[1]:
                b.r[tag[0]] = tag[1]
        for b in w:
            b.w = tag
            b.r = {}
        return inst

    def barrier(self, engines=("pe", "act", "dve", "pool", "sp")):
        for e in engines:
            E = self.eng[e]
            for kk, v in self.cnt.items():
                if v == 0 or kk == e:
                    continue
                if self.seen.get((e, kk), 0) >= v:
                    continue
                E.wait_ge(self.sem[kk], v)
                self.seen[(e, kk)] = v


def rsqrt_col(k, src, srcb, dst, dstb, n, eps):
    k.op("act", lambda E: E.activation(out=dst, in_=src, func=AF.Ln, scale=1.0 / n, bias=k.epsc[eps][0:src.shape[0], :]),
         r=[srcb], w=[dstb])
    k.op("act", lambda E: E.activation(out=dst, in_=dst, func=AF.Exp, scale=-0.5), r=[dstb], w=[dstb])


def load_consts(k, es, dr, L, do_sample):
    nc = k.nc
    NT = L // 128
    c = {}

    def sb(name, shape, dt):
        return es.enter_context(nc.sbuf_tensor("cs_" + name, shape, dt))

    cb = Buf()
    c["b"] = cb
    stage = sb("c_stage", [128, 256], F32)
    c["ident_f"] = sb("ident_f", [128, 128], F32)
    c["ident_b"] = sb("ident_b", [128, 128], BF16)
    c["mask_b"] = sb("mask_b", [128, 128], BF16)
    c["mask64_f"] = sb("mask64_f", [128, 64], F32)
    c["bmask_f"] = sb("bmask_f", [64, 64], F32)
    c["bmask_b"] = sb("bmask_b", [64, 64], BF16)
    c["sel"] = sb("sel", [64, 16], F32)
    c["ones_f"] = sb("ones_f", [128, 512], F32)
    c["ones_b"] = sb("ones_b", [128, 1], BF16)
    c["cosp"] = sb("cosp", [128, NT, 8], F32)
    c["sinp"] = sb("sinp", [128, NT, 8], F32)
    c["coss"] = sb("coss", [64, 8], F32)
    c["sins"] = sb("sins", [64, 8], F32)
    c["an"] = sb("an", [128, 8], F32)
    c["fnc"] = sb("fnc", [128, 8], F32)
    c["lamv"] = sb("lamv", [128, 256], F32)
    c["lbl_h"] = sb("lbl_h", [128, 2], F32)
    c["lbl_a"] = sb("lbl_a", [128, 8], F32)
    c["hn_h"] = sb("hn_h", [128, 1], F32)
    c["hn_a"] = sb("hn_a", [128, 4], F32)
    c["subln"] = sb("subln", [128, 1], F32)
    c["fin"] = sb("fin", [128, 1024], F32)
    c["iota_p"] = sb("iota_p", [128, 1], F32)
    c["small"] = sb("c_small", [128, 32], F32)
    epst = sb("epst", [128, 2], F32)

    def ld(dst, src):
        k.op("sp", lambda E: E.dma_start(out=dst, in_=src), w=[cb], dsem="cst")

    ld(c["ident_f"][:], dr["c_ident"])
    ld(stage[:, 0:128], dr["c_mask"])
    ld(c["mask64_f"][:], dr["c_mask64"])
    ld(c["bmask_f"][:], dr["c_bmask"])
    ld(c["sel"][:], dr["c_sel"])
    ld(c["cosp"][:], dr["c_cosp"])
    ld(c["sinp"][:], dr["c_sinp"])
    ld(c["coss"][:], dr["c_coss"])
    ld(c["sins"][:], dr["c_sins"])
    ld(c["an"][:], dr["an"])
    ld(c["fnc"][:], dr["fnc"])
    ld(c["lamv"][:], dr["lamv"].partition_broadcast(128))
    ld(c["lbl_h"][:], dr["lbl_h"])
    ld(c["lbl_a"][:], dr["lbl_a"])
    ld(c["hn_h"][:], dr["hn_h"])
    ld(c["hn_a"][:], dr["hn_a"])
    ld(c["subln"][:], dr["subln"])
    ld(c["fin"][:], dr["fin"].partition_broadcast(128))
    ld(c["iota_p"][:], dr["c_iota"])
    V = lambda fn: k.op("dve", fn, r=[cb], w=[cb])
    V(lambda E: E.memset(c["ones_f"][:], 1.0))
    V(lambda E: E.memset(c["ones_b"][:], 1.0))
    V(lambda E: E.memset(epst[:, 0:1], EPS))
    V(lambda E: E.memset(epst[:, 1:2], SUBLN_EPS))
    k.epsc = {EPS: epst[:, 0:1], SUBLN_EPS: epst[:, 1:2]}
    V(lambda E: E.tensor_copy(out=c["ident_b"][:], in_=c["ident_f"][:]))
    V(lambda E: E.tensor_copy(out=c["mask_b"][:], in_=stage[:, 0:128]))
    V(lambda E: E.tensor_copy(out=c["bmask_b"][:], in_=c["bmask_f"][:]))
    sm = c["small"]
    lv = c["lamv"]
    V(lambda E: E.scalar_tensor_tensor(out=stage[:, 0:64], in0=lv[:, 0:64], scalar=1.0, in1=lv[:, 64:128],
                                       op0=ALU.mult, op1=ALU.mult, accum_out=sm[:, 0:1]))
    V(lambda E: E.scalar_tensor_tensor(out=stage[:, 0:64], in0=lv[:, 128:192], scalar=1.0, in1=lv[:, 192:256],
                                       op0=ALU.mult, op1=ALU.mult, accum_out=sm[:, 1:2]))
    k.op("act", lambda E: E.activation(out=sm[:, 2:4], in_=sm[:, 0:2], func=AF.Exp), r=[cb], w=[cb])
    V(lambda E: E.tensor_tensor(out=sm[:, 4:5], in0=sm[:, 2:3], in1=sm[:, 3:4], op=ALU.subtract))
    V(lambda E: E.tensor_scalar(out=sm[:, 5:6], in0=sm[:, 4:5], scalar1=LAM_INIT, scalar2=-1.0,
                                op0=ALU.add, op1=ALU.mult))
    c["neglam"] = sm[:, 5:6]
    V(lambda E: E.tensor_tensor(out=sm[:, 6:7], in0=c["lbl_h"][:, 0:1], in1=c["lbl_h"][:, 1:2], op=ALU.subtract))
    V(lambda E: E.tensor_tensor(out=sm[:, 8:12], in0=c["lbl_a"][:, 0:4], in1=c["lbl_a"][:, 4:8], op=ALU.subtract))
    k.op("act", lambda E: E.activation(out=sm[:, 6:7], in_=sm[:, 6:7], func=AF.Sigmoid), r=[cb], w=[cb])
    k.op("act", lambda E: E.activation(out=sm[:, 8:12], in_=sm[:, 8:12], func=AF.Sigmoid), r=[cb], w=[cb])
    V(lambda E: E.tensor_scalar(out=sm[:, 7:8], in0=sm[:, 6:7], scalar1=-1.0, scalar2=1.0, op0=ALU.mult, op1=ALU.add))
    V(lambda E: E.tensor_scalar(out=sm[:, 12:16], in0=sm[:, 8:12], scalar1=-1.0, scalar2=1.0, op0=ALU.mult, op1=ALU.add))
    c["lb_h"] = sm[:, 6:7]
    c["oml_h"] = sm[:, 7:8]
    c["lb_a"] = sm[:, 8:12]
    c["oml_a"] = sm[:, 12:16]
    return c


def rope_inplace(k, v4, vb, cos, sin, tmp, tmpb, npart, ng):
    x1 = v4[:, :, 0:8]
    x2 = v4[:, :, 8:16]
    cb_ = cos.unsqueeze(1).to_broadcast([npart, ng, 8])
    sb_ = sin.unsqueeze(1).to_broadcast([npart, ng, 8])
    t = [tmp[0:npart, j * ng * 8:(j + 1) * ng * 8].rearrange("p (g d) -> p g d", d=8) for j in range(4)]
    V = lambda fn, r, w: k.op("dve", fn, r=r, w=w)
    V(lambda E: E.tensor_tensor(out=t[0], in0=x1, in1=cb_, op=ALU.mult), [vb], [tmpb])
    V(lambda E: E.tensor_tensor(out=t[1], in0=x2, in1=sb_, op=ALU.mult), [vb], [tmpb])
    V(lambda E: E.tensor_tensor(out=t[2], in0=x2, in1=cb_, op=ALU.mult), [vb], [tmpb])
    V(lambda E: E.tensor_tensor(out=t[3], in0=x1, in1=sb_, op=ALU.mult), [vb], [tmpb])
    V(lambda E: E.tensor_tensor(out=x1, in0=t[0], in1=t[1], op=ALU.subtract), [tmpb], [vb])
    V(lambda E: E.tensor_tensor(out=x2, in0=t[2], in1=t[3], op=ALU.add), [tmpb], [vb])


def phase1(k, c, dr, L):
    nc = k.nc
    NT = L // 128
    NG = L // 512
    cb = c["b"]
    with ExitStack() as ph, ExitStack() as ip:
        cur = [ph]

        def sb(name, shape, dt):
            return cur[0].enter_context(nc.sbuf_tensor(name, shape, dt))

        def ps(name, shape, dt):
            return cur[0].enter_context(nc.psum_tensor(name, shape, dt))

        QKT = sb("QKT", [128, 2, L], BF16)
        QKTb = bufs(NT)
        Vaug = sb("Vaug", [128, NT, 130], BF16)
        Vaugb = bufs(NT)
        cur[0] = ip
        Vh = sb("Vh", [128, NT, 128], BF16)
        Vhb = bufs(NT)
        w1 = sb("w1", [128, 8, 896], BF16)
        w1b = Buf()
        SSQ = sb("SSQ", [128, NT], F32)
        SSQb = Buf()
        with ExitStack() as wph:
            wst = [wph.enter_context(nc.sbuf_tensor(f"wst{i}", [128, 896], F32)) for i in range(2)]
            wstb = bufs(2)
            for kc in range(8):
                s = kc % 2
                k.op("sp", lambda E: E.dma_start(out=wst[s][:], in_=dr["w_in_h"][kc * 128:(kc + 1) * 128, :]),
                     w=[wstb[s]], dsem=f"w{s}")
                k.op("dve", lambda E: E.tensor_scalar(out=w1[:, kc, :], in0=wst[s][:], scalar1=c["an"][:, kc:kc + 1],
                                                      scalar2=None, op0=ALU.mult), r=[wstb[s], cb], w=[w1b])
            k.barrier()
        k.op("pool", lambda E: E.memset(Vaug[:, :, 128:130], 1.0), w=Vaugb)
        if STOP <= 1:
            return

        NX = 3
        xs = [sb(f"xs{i}", [128, 1024], F32) for i in range(NX)]
        xsb = bufs(NX)
        junk = sb("junk", [128, 1024], BF16)
        junkb = Buf()
        cols = sb("cols", [128, 16], F32)
        colb = bufs(16)
        xn = [sb(f"xn{i}", [128, 1024], BF16) for i in range(2)]
        xnb = bufs(2)
        hT = [sb(f"hT{i}", [128, 8, 512], BF16) for i in range(2)]
        hTb = [bufs(4) for _ in range(2)]
        qkvf = [sb(f"qkvf{i}", [128, 384], F32) for i in range(2)]
        qkvfb = bufs(2)
        rtmp = sb("rtmp", [128, 128], F32)
        rtmpb = Buf()
        qkb = [sb(f"qkb{i}", [128, 256], BF16) for i in range(2)]
        qkbb = bufs(2)
        tp = [ps(f"tp{i}", [128, 8, 128], BF16) for i in range(2)]
        tpb = bufs(2)
        ztm = [ps(f"ztm{i}", [128, 512], F32) for i in range(2)]
        ztmb = bufs(2)
        tq = ps("tq", [128, 8, 128], BF16)
        tqb = Buf()
        zfm = [ps(f"zfm{i}", [128, 512], F32) for i in range(3)]
        zfmb = bufs(3)
        pOT, pOTb = zfm[0], zfmb[0]
        pmisc, pAb = zfm[1], zfmb[1]
        pSb = pAb
        pU, pUb = zfm[2], zfmb[2]
        hg = {n: sb("hg_" + n, [128, 512], F32) for n in ("F", "G", "KT", "GC", "D", "QS", "SG", "E", "E1")}
        hgb = {n: Buf() for n in hg}
        GS = sb("GS", [128, 8], F32)
        GSb = Buf()
        hb = {n: sb("hb_" + n, [128, 512], BF16) for n in ("Qh", "Kh", "Kb", "Qb", "Y")}
        hbb = {n: Buf() for n in hb}
        KbT = sb("KbT", [128, 4, 128], BF16)
        KbTb = Buf()
        Am = [sb(f"Am{i}", [128, 64], BF16) for i in range(2)]
        Amb = bufs(2)
        S = [sb(f"S{i}", [128, 128], F32) for i in range(2)]
        Sb = bufs(2)
        Sbf = [sb(f"Sbf{i}", [128, 128], BF16) for i in range(2)]
        Sbfb = bufs(2)
        k.op("dve", lambda E: E.memset(S[0][:], 0.0), w=[Sb[0]])
        k.op("dve", lambda E: E.memset(Sbf[0][:], 0.0), w=[Sbfb[0]])
        scur = 0

        def hgrn(g, gs):
            nonlocal scur
            for m in range(3):
                for kc in range(8):
                    k.op("pe", lambda E: E.matmul(out=zfm[m][:, :], lhsT=w1[:, kc, 512 + m * 128:512 + (m + 1) * 128],
                                                  rhs=hT[gs][:, kc, :], start=(kc == 0), stop=(kc == 7)),
                         r=hTb[gs] + [w1b], w=[zfmb[m]])
            yield
            zq, zf, zg = zfm
            ACT = lambda fn, r, w: k.op("act", fn, r=r, w=w)
            DVE = lambda fn, r, w: k.op("dve", fn, r=r, w=w)
            POOL = lambda fn, r, w: k.op("pool", fn, r=r, w=w)
            ACT(lambda E: E.activation(out=hg["F"][:], in_=zf[:], func=AF.Sigmoid), [zfmb[1]], [hgb["F"]])
            ACT(lambda E: E.activation(out=hg["QS"][:], in_=zq[:], func=AF.Silu), [zfmb[0]], [hgb["QS"]])
            ACT(lambda E: E.activation(out=hg["SG"][:], in_=zg[:], func=AF.Silu), [zfmb[2]], [hgb["SG"]])
            DVE(lambda E: E.tensor_scalar(out=hg["F"][:], in0=hg["F"][:], scalar1=c["oml_h"], scalar2=c["lb_h"],
                                          op0=ALU.mult, op1=ALU.add), [hgb["F"], cb], [hgb["F"]])
            ACT(lambda E: E.activation(out=hg["G"][:], in_=hg["F"][:], func=AF.Ln), [hgb["F"]], [hgb["G"]])
            POOL(lambda E: E.tensor_scalar(out=hg["KT"][:], in0=hg["F"][:], scalar1=-1.0, scalar2=1.0,
                                           op0=ALU.mult, op1=ALU.add), [hgb["F"]], [hgb["KT"]])
            DVE(lambda E: E.tensor_tensor_scan(out=hg["GC"][:], data0=c["ones_f"][:], data1=hg["G"][:], initial=0.0,
                                               op0=ALU.mult, op1=ALU.add), [hgb["G"], cb], [hgb["GC"]])
            GC3 = hg["GC"][:].rearrange("p (c t) -> p c t", t=64)
            D3 = hg["D"][:].rearrange("p (c t) -> p c t", t=64)
            DVE(lambda E: E.memset(GS[:, 0:1], 0.0), [], [GSb])
            DVE(lambda E: E.tensor_copy(out=GS[:, 1:8], in_=GC3[:, 0:7, 63]), [hgb["GC"]], [GSb])
            bc = lambda ap: ap.to_broadcast([128, 8, 64])
            DVE(lambda E: E.tensor_tensor(out=D3, in0=GC3, in1=bc(GC3[:, :, 31:32]), op=ALU.subtract),
                [hgb["GC"]], [hgb["D"]])
            ACT(lambda E: E.activation(out=hg["E"][:], in_=hg["D"][:], func=AF.Exp), [hgb["D"]], [hgb["E"]])
            DVE(lambda E: E.tensor_tensor(out=hb["Qh"][:], in0=hg["QS"][:], in1=hg["E"][:], op=ALU.mult),
                [hgb["QS"], hgb["E"]], [hbb["Qh"]])
            ACT(lambda E: E.activation(out=hg["E"][:], in_=hg["D"][:], func=AF.Exp, scale=-1.0), [hgb["D"]], [hgb["E"]])
            DVE(lambda E: E.tensor_tensor(out=hb["Kh"][:], in0=hg["KT"][:], in1=hg["E"][:], op=ALU.mult),
                [hgb["KT"], hgb["E"]], [hbb["Kh"]])
            DVE(lambda E: E.tensor_tensor(out=D3, in0=GC3, in1=bc(GC3[:, :, 63:64]), op=ALU.subtract),
                [hgb["GC"]], [hgb["D"]])
            ACT(lambda E: E.activation(out=hg["E"][:], in_=hg["D"][:], func=AF.Exp, scale=-1.0), [hgb["D"]], [hgb["E"]])
            DVE(lambda E: E.tensor_tensor(out=hb["Kb"][:], in0=hg["KT"][:], in1=hg["E"][:], op=ALU.mult),
                [hgb["KT"], hgb["E"]], [hbb["Kb"]])
            DVE(lambda E: E.tensor_tensor(out=D3, in0=GC3, in1=bc(GS[:].unsqueeze(2)), op=ALU.subtract),
                [hgb["GC"], GSb], [hgb["D"]])
            ACT(lambda E: E.activation(out=hg["E1"][:], in_=hg["D"][:], func=AF.Exp), [hgb["D"]], [hgb["E1"]])
            DVE(lambda E: E.tensor_tensor(out=hb["Qb"][:], in0=hg["QS"][:], in1=hg["E1"][:], op=ALU.mult),
                [hgb["QS"], hgb["E1"]], [hbb["Qb"]])
            for bl in range(4):
                k.op("pe", lambda E: E.transpose(out=tp[0][:, bl, :], in_=hb["Kb"][:, bl * 128:(bl + 1) * 128],
                                                 identity=c["ident_b"][:]), r=[hbb["Kb"], cb], w=[tpb[0]])
            ACT(lambda E: E.activation(out=KbT[:], in_=tp[0][:, 0:4, :], func=AF.Copy), [tpb[0]], [KbTb])
            yield
            for cc in range(8):
                bl, half = divmod(cc, 2)
                tile_i = g * 4 + bl
                p0 = half * 64
                csl = slice(cc * 64, (cc + 1) * 64)
                am_t, am_b = Am[cc % 2], Amb[cc % 2]
                k.op("pe", lambda E: E.matmul(out=pmisc[p0:p0 + 64, 0:64], lhsT=hb["Kh"][:, csl], rhs=hb["Qh"][:, csl],
                                              start=True, stop=True), r=[hbb["Kh"], hbb["Qh"]], w=[pAb])
                DVE(lambda E: E.tensor_tensor(out=am_t[p0:p0 + 64, :], in0=pmisc[p0:p0 + 64, 0:64],
                                              in1=c["mask64_f"][p0:p0 + 64, :], op=ALU.mult), [pAb, cb], [am_b])
                k.op("pe", lambda E: E.matmul(out=pOT[:, csl], lhsT=Sbf[scur][:], rhs=hb["Qb"][:, csl],
                                              start=True, stop=False), r=[Sbfb[scur], hbb["Qb"]], w=[pOTb])
                k.op("pe", lambda E: E.matmul(out=pOT[:, csl], lhsT=Vh[p0:p0 + 64, tile_i, :], rhs=am_t[p0:p0 + 64, :],
                                              start=False, stop=True), r=[Vhb[tile_i], am_b], w=[pOTb])
                k.op("pe", lambda E: E.matmul(out=pU[:, 0:128], lhsT=KbT[p0:p0 + 64, bl, :],
                                              rhs=Vh[p0:p0 + 64, tile_i, :], start=True, stop=True),
                     r=[KbTb, Vhb[tile_i]], w=[pUb])
                nxt = 1 - scur
                dcol = hg["E1"][:, cc * 64 + 63:cc * 64 + 64]
                DVE(lambda E: E.scalar_tensor_tensor(out=S[nxt][:], in0=S[scur][:], scalar=dcol, in1=pU[:, 0:128],
                                                     op0=ALU.mult, op1=ALU.add), [Sb[scur], hgb["E1"], pUb], [Sb[nxt]])
                ACT(lambda E: E.activation(out=Sbf[nxt][:], in_=S[nxt][:], func=AF.Copy), [Sb[nxt]], [Sbfb[nxt]])
                scur = nxt
                yield
            ACT(lambda E: E.activation(out=hg["E"][:], in_=pOT[:], func=AF.Square), [pOTb], [hgb["E"]])
            for bl in range(4):
                k.op("pe", lambda E: E.matmul(out=pmisc[:, 256 + bl:257 + bl], lhsT=hg["E"][:, bl * 128:(bl + 1) * 128],
                                              rhs=c["ones_f"][:, 0:1], start=True, stop=True), r=[hgb["E"], cb], w=[pSb])
            DVE(lambda E: E.tensor_copy(out=SSQ[:, g * 4:(g + 1) * 4], in_=pmisc[:, 256:260]), [pSb], [SSQb])
            DVE(lambda E: E.scalar_tensor_tensor(out=hb["Y"][:], in0=pOT[:], scalar=c["hn_h"][:, 0:1], in1=hg["SG"][:],
                                                 op0=ALU.mult, op1=ALU.mult), [pOTb, hgb["SG"], hgb["E"], cb], [hbb["Y"]])
            T2_ = L // 4
            rr_, co_ = divmod(g * 512, T2_)
            k.op("pool", lambda E: E.dma_start(out=dr["xch"][rr_ * 256 + 128:rr_ * 256 + 256, co_:co_ + 512], in_=hb["Y"][:]),
                 r=[hbb["Y"]], dsem="yo")

        pending = []

        def advance(n):
            for _ in range(n):
                if not pending:
                    return
                try:
                    next(pending[0])
                except StopIteration:
                    pending.pop(0)

        for i in range(NT):
            g, j = divmod(i, 4)
            gs = g % 2
            advance(3)
            xslot = i % NX
            x_t, x_b = xs[xslot], xsb[xslot]
            k.op("sp", lambda E: E.dma_start(out=x_t[:], in_=dr["xp"][i * 128:(i + 1) * 128, :]),
                 w=[x_b], dsem=f"x{xslot}")
            ci = (i % 8) * 2
            ssc, ssb_ = cols[:, ci:ci + 1], colb[ci]
            rsc, rsb_ = cols[:, ci + 1:ci + 2], colb[ci + 1]
            k.op("act", lambda E: E.activation(out=junk[:], in_=x_t[:], func=AF.Square, accum_out=ssc),
                 r=[x_b], w=[junkb, ssb_])
            rsqrt_col(k, ssc, ssb_, rsc, rsb_, 1024.0, EPS)
            if STOP <= 2.1:
                continue
            xn_t, xn_b = xn[i % 2], xnb[i % 2]
            k.op("dve", lambda E: E.tensor_scalar(out=xn_t[:], in0=x_t[:], scalar1=rsc, scalar2=None, op0=ALU.mult),
                 r=[x_b, rsb_], w=[xn_b])
            for kc in range(8):
                k.op("pe", lambda E: E.transpose(out=tp[i % 2][:, kc, :], in_=xn_t[:, kc * 128:(kc + 1) * 128],
                                                 identity=c["ident_b"][:]), r=[xn_b, cb], w=[tpb[i % 2]])
            k.op("act", lambda E: E.activation(out=hT[gs][:, :, j * 128:(j + 1) * 128], in_=tp[i % 2][:], func=AF.Copy),
                 r=[tpb[i % 2]], w=[hTb[gs][j]])
            if STOP <= 2.2:
                continue
            for kc in range(8):
                k.op("pe", lambda E: E.matmul(out=ztm[i % 2][:, :], lhsT=hT[gs][:, kc, j * 128:(j + 1) * 128],
                                              rhs=w1[:, kc, 0:512], start=(kc == 0), stop=(kc == 7)),
                     r=[hTb[gs][j], w1b], w=[ztmb[i % 2]])
            qs = i % 2
            qf, qfb = qkvf[qs], qkvfb[qs]
            if STOP <= 2.25:
                continue
            k.op("act", lambda E: E.activation(out=qf[:], in_=ztm[i % 2][:, 0:384], func=AF.Copy), r=[ztmb[i % 2]], w=[qfb])
            if STOP <= 2.27:
                continue
            k.op("act", lambda E: E.activation(out=Vh[:, i, :], in_=ztm[i % 2][:, 384:512], func=AF.Copy), r=[ztmb[i % 2]], w=[Vhb[i]])
            if STOP <= 2.3:
                continue
            v4 = qf[:, 0:256].rearrange("p (g d) -> p g d", d=64)
            rope_inplace(k, v4, qfb, c["cosp"][:, i, :], c["sinp"][:, i, :], rtmp, rtmpb, 128, 4)
            if STOP <= 2.4:
                continue
            k.op("pool", lambda E: E.dma_start(out=dr["k_out"][i * 128:(i + 1) * 128, :], in_=qf[:, 128:256]),
                 r=[qfb], dsem=f"kvo{qs}")
            k.op("pool", lambda E: E.dma_start(out=dr["v_out"][i * 128:(i + 1) * 128, :], in_=qf[:, 256:384]),
                 r=[qfb], dsem=f"kvo{qs}")
            if STOP <= 2.5:
                continue
            qb_t, qb_b = qkb[qs], qkbb[qs]
            k.op("act", lambda E: E.activation(out=qb_t[:, 0:128], in_=qf[:, 0:128], func=AF.Copy, scale=0.125),
                 r=[qfb], w=[qb_b])
            k.op("pool", lambda E: E.tensor_copy(out=qb_t[:, 128:256], in_=qf[:, 128:256]), r=[qfb], w=[qb_b])
            k.op("pool", lambda E: E.tensor_copy(out=Vaug[:, i, 0:128], in_=qf[:, 256:384]), r=[qfb], w=[Vaugb[i]])
            if STOP <= 2.6:
                continue
            for h_ in range(2):
                k.op("pe", lambda E: E.transpose(out=tq[:, h_, :], in_=qb_t[:, h_ * 128:(h_ + 1) * 128],
                                                 identity=c["ident_b"][:]), r=[qb_b, cb], w=[tqb])
            k.op("dve", lambda E: E.tensor_copy(out=QKT[:, :, i * 128:(i + 1) * 128], in_=tq[:, 0:2, :]), r=[tqb], w=[QKTb[i]])
            if j != 3 or STOP <= 2:
                continue
            pending.append(hgrn(g, gs))
            advance(1)
        while pending:
            advance(1)
        k.op("pool", lambda E: E.dma_start(out=dr["s_out"][:, :], in_=S[scur][:]), r=[Sb[scur]], dsem="so")
        NT2_ = NT // 4
        for rr_ in range(4):
            k.op("pool", lambda E: E.dma_start(out=dr["xssq"][rr_ * 128:(rr_ + 1) * 128, :],
                                               in_=SSQ[:, rr_ * NT2_:(rr_ + 1) * NT2_]), r=[SSQb], dsem="so")
        k.barrier()
        ip.close()
        if STOP <= 3:
            return
        attention_prompt(k, c, dr, L, ph, QKT, QKTb, Vaug, Vaugb)
        k.barrier()


def attention_prompt(k, c, dr, L, ph, QKT, QKTb, Vaug, Vaugb):
    nc = k.nc
    NT = L // 128
    cb = c["b"]
    with ExitStack() as ap:
        def sb(name, shape, dt):
            return ap.enter_context(nc.sbuf_tensor(name, shape, dt))

        def ps(name, shape, dt):
            return ap.enter_context(nc.psum_tensor(name, shape, dt))
        NS = 2
        ST = [[ps(f"ST{cc}_{s}", [128, 512], F32) for s in range(NS)] for cc in range(2)]
        STb = [bufs(NS) for _ in range(2)]
        PT = [[sb(f"PT{cc}_{s}", [128, 512], BF16) for s in range(NS)] for cc in range(2)]
        PTb = [bufs(NS) for _ in range(2)]
        O = ps("Oacc", [128, 2, 512], F32)
        Ob = Buf()
        tpo = ps("tpo", [128, 128], BF16)
        tpob = Buf()
        ec = sb("ecols", [128, 8], F32)
        ecb = Buf()
        Osb = sb("Osb", [128, 2, 129], F32)
        Osbb = Buf()
        a_t = sb("a_t", [128, 128], F32)
        oa_t = sb("oa_t", [128, 128], F32)
        ej = sb("ejunk", [128, 128], F32)
        oan = sb("oan", [128, 128], BF16)
        eb = Buf()
        ostage = [sb(f"ostage{i}", [128, 512], BF16) for i in range(2)]
        ostb = bufs(2)

        items = []
        for qb in range(NT):
            ng = qb // 4 + 1
            for g in range(ng):
                kbs = list(range(4 * g, min(4 * g + 4, qb + 1)))
                for cc in range(2):
                    items.append((qb, g, cc, kbs))
        slot_ctr = [0, 0]
        slots = []
        for it in items:
            slots.append(slot_ctr[it[2]] % NS)
            slot_ctr[it[2]] += 1

        def emit_qk(n):
            qb, g, cc, kbs = items[n]
            s = slots[n]
            for jj, kb in enumerate(kbs):
                k.op("pe", lambda E: E.matmul(out=ST[cc][s][:, jj * 128:(jj + 1) * 128],
                                              lhsT=QKT[cc * 64:(cc + 1) * 64, 1, kb * 128:(kb + 1) * 128],
                                              rhs=QKT[cc * 64:(cc + 1) * 64, 0, qb * 128:(qb + 1) * 128],
                                              start=True, stop=True), r=[QKTb[kb], QKTb[qb]], w=[STb[cc][s]])

        emit_qk(0)
        for n, (qb, g, cc, kbs) in enumerate(items):
            if n + 1 < len(items):
                emit_qk(n + 1)
            s = slots[n]
            nk = len(kbs)
            k.op("act", lambda E: E.activation(out=PT[cc][s][:, 0:nk * 128], in_=ST[cc][s][:, 0:nk * 128], func=AF.Exp),
                 r=[STb[cc][s]], w=[PTb[cc][s]])
            if kbs[-1] == qb:
                jd = nk - 1
                k.op("pool", lambda E: E.tensor_tensor(out=PT[cc][s][:, jd * 128:(jd + 1) * 128],
                                                       in0=PT[cc][s][:, jd * 128:(jd + 1) * 128], in1=c["mask_b"][:],
                                                       op=ALU.mult), r=[PTb[cc][s], cb], w=[PTb[cc][s]])
            for jj, kb in enumerate(kbs):
                k.op("pe", lambda E: E.matmul(out=O[:, cc, 0:129], lhsT=PT[cc][s][:, jj * 128:(jj + 1) * 128],
                                              rhs=Vaug[:, kb, 0:129], start=(kb == 0), stop=(kb == qb)),
                     r=[PTb[cc][s], Vaugb[kb]], w=[Ob])
            if not (cc == 1 and kbs[-1] == qb):
                continue
            DVE = lambda fn, r, w: k.op("dve", fn, r=r, w=w)
            DVE(lambda E: E.tensor_copy(out=Osb[:], in_=O[:, :, 0:129]), [Ob], [Osbb])
            DVE(lambda E: E.reciprocal(out=ec[:, 0:2], in_=Osb[:, :, 128]), [Osbb], [ecb])
            DVE(lambda E: E.tensor_tensor(out=ec[:, 2:3], in0=ec[:, 1:2], in1=c["neglam"], op=ALU.mult), [ecb, cb], [ecb])
            DVE(lambda E: E.tensor_scalar(out=a_t[:], in0=Osb[:, 0, 0:128], scalar1=ec[:, 0:1], scalar2=None, op0=ALU.mult),
                [Osbb, ecb], [eb])
            DVE(lambda E: E.scalar_tensor_tensor(out=oa_t[:], in0=Osb[:, 1, 0:128], scalar=ec[:, 2:3], in1=a_t[:],
                                                 op0=ALU.mult, op1=ALU.add), [Osbb, ecb, eb], [eb])
            DVE(lambda E: E.scalar_tensor_tensor(out=ej[:], in0=oa_t[:], scalar=1.0, in1=oa_t[:], op0=ALU.mult,
                                                 op1=ALU.mult, accum_out=ec[:, 3:4]), [eb], [eb, ecb])
            rsqrt_col(k, ec[:, 3:4], ecb, ec[:, 4:5], ecb, 128.0, SUBLN_EPS)
            DVE(lambda E: E.tensor_scalar(out=oan[:], in0=oa_t[:], scalar1=ec[:, 4:5], scalar2=None, op0=ALU.mult),
                [eb, ecb], [eb])
            k.op("pe", lambda E: E.transpose(out=tpo[:], in_=oan[:], identity=c["ident_b"][:]), r=[eb, cb], w=[tpob])
            og, oj = divmod(qb, 4)
            os_ = og % 2
            DVE(lambda E: E.tensor_copy(out=ostage[os_][:, oj * 128:(oj + 1) * 128], in_=tpo[:]), [tpob], [ostb[os_]])
            if oj == 3 or qb == NT - 1:
                rr_, co_ = divmod(og * 512, L // 4)
                k.op("pool", lambda E: E.dma_start(out=dr["xch"][rr_ * 256:rr_ * 256 + 128, co_:co_ + (oj + 1) * 128],
                                                   in_=ostage[os_][:, 0:(oj + 1) * 128]), r=[ostb[os_]], dsem=f"oo{os_}")


def phase2(k, c, dr, L, smp):
    nc = k.nc
    T2 = L // 4
    NT2 = T2 // 128
    cb = c["b"]
    tiles = [(128, t) for t in range(NT2)]
    if smp is not None:
        tiles.append((64, NT2))
    with ExitStack() as p2:
        def mk(stack):
            def sb(name, shape, dt):
                return stack.enter_context(nc.sbuf_tensor(name, shape, dt))

            def ps(name, shape, dt):
                return stack.enter_context(nc.psum_tensor(name, shape, dt))
            return sb, ps
        sb, ps = mk(p2)
        X1 = sb("X1", [128, NT2 + 1, 1024], F32)
        X1b = bufs(NT2 + 1)
        cols = sb("p2cols", [128, 16], F32)
        colb = bufs(16)
        junk = sb("p2junk", [128, 1024], BF16)
        junkb = Buf()
        with ExitStack() as pa:
            sba, psa = mk(pa)
            mixT = sba("mixT", [128, 8, T2], BF16)
            mixb = bufs(8)
            idxm = sba("idxm", [128, 8], I32)
            idxs = sba("idxs", [128, 4], I32)
            idxb = Buf()
            ssqg = sba("ssqg", [128, 4, NT2], F32)
            ssqb = Buf()
            rsth = sba("rsth", [128, NT2], F32)
            wout = sba("wout", [128, 8, 1024], BF16)
            woutb = Buf()
            wst = [sba(f"wost{i}", [128, 1024], F32) for i in range(2)]
            wstb = bufs(2)
            xt = [sba(f"p2x{i}", [128, 1024], F32) for i in range(2)]
            xtb = bufs(2)
            pA = psa("pA", [128, 2, 512], F32)
            pH = psa("pH", [128, 2, 512], F32)
            pAb, pHb = Buf(), Buf()
            k.op("sp", lambda E: E.dma_start(out=idxm[:], in_=dr["idx_mix"]), w=[idxb], dsem="p2i")
            k.op("sp", lambda E: E.dma_start(out=idxs[:], in_=dr["idx_ssq"]), w=[idxb], dsem="p2i")
            for kc in range(8):
                k.op("pool", lambda E: E.indirect_dma_start(
                    out=mixT[:, kc, :], out_offset=None, in_=dr["xg"][:, :],
                    in_offset=bass.IndirectOffsetOnAxis(ap=idxm[:, kc:kc + 1], axis=0)),
                    r=[idxb, dr["xgb"]], w=[mixb[kc]], dsem="p2g")
            for h in range(4):
                k.op("pool", lambda E: E.indirect_dma_start(
                    out=ssqg[:, h, :], out_offset=None, in_=dr["xgs"][:, :],
                    in_offset=bass.IndirectOffsetOnAxis(ap=idxs[:, h:h + 1], axis=0)),
                    r=[idxb, dr["xgsb"]], w=[ssqb], dsem="p2g")
            for b_ in mixb + [ssqb]:
                b_.w = ("p2g", k.cnt["p2g"])
            for kc in range(8):
                s_ = kc % 2
                k.op("sp", lambda E: E.dma_start(out=wst[s_][:], in_=dr["w_out"][kc * 128:(kc + 1) * 128, :]),
                     w=[wstb[s_]], dsem=f"wo{s_}")
                if kc < 4:
                    k.op("dve", lambda E: E.tensor_scalar(out=wout[:, kc, :], in0=wst[s_][:], scalar1=c["subln"][:, 0:1],
                                                          scalar2=1.0 - LAM_INIT, op0=ALU.mult, op1=ALU.mult),
                         r=[wstb[s_], cb], w=[woutb])
                else:
                    k.op("dve", lambda E: E.tensor_copy(out=wout[:, kc, :], in_=wst[s_][:]), r=[wstb[s_]], w=[woutb])
            k.op("dve", lambda E: E.tensor_tensor(out=rsth[:], in0=ssqg[:, 0, :], in1=ssqg[:, 1, :], op=ALU.add), r=[ssqb], w=[ssqb])
            k.op("dve", lambda E: E.tensor_tensor(out=rsth[:], in0=rsth[:], in1=ssqg[:, 2, :], op=ALU.add), r=[ssqb], w=[ssqb])
            k.op("dve", lambda E: E.tensor_tensor(out=rsth[:], in0=rsth[:], in1=ssqg[:, 3, :], op=ALU.add), r=[ssqb], w=[ssqb])
            rsqrt_col(k, rsth[:], ssqb, rsth[:], ssqb, 512.0, EPS)
            for ti, (nt, t) in enumerate(tiles):
                xs_ = ti % 2
                src = dr["xp2"][t * 128:(t + 1) * 128, :] if nt == 128 else dr["xs"][:, :]
                k.op("sp", lambda E: E.dma_start(out=xt[xs_][0:nt, :], in_=src), w=[xtb[xs_]], dsem=f"p2x{xs_}")
                for half in range(2):
                    for kc in range(8):
                        if nt == 128:
                            lhs, lb_ = mixT[:, kc, t * 128:(t + 1) * 128], mixb[kc]
                        else:
                            lhs, lb_ = smp["mixT"][:, kc, :], smp["b"]
                        dst, db = (pA, pAb) if kc < 4 else (pH, pHb)
                        k.op("pe", lambda E: E.matmul(out=dst[0:nt, half, :], lhsT=lhs, rhs=wout[:, kc, half * 512:(half + 1) * 512],
                                                      start=(kc % 4 == 0), stop=(kc % 4 == 3)), r=[lb_, woutb], w=[db])
                rcol = rsth[:, t:t + 1] if nt == 128 else smp["rstdh"][0:64, 0:1]
                rb = ssqb if nt == 128 else smp["b"]
                for half in range(2):
                    hs = slice(half * 512, (half + 1) * 512)
                    k.op("dve", lambda E: E.tensor_tensor(out=xt[xs_][0:nt, hs], in0=xt[xs_][0:nt, hs], in1=pA[0:nt, half, :],
                                                          op=ALU.add), r=[xtb[xs_], pAb], w=[xtb[xs_]])
                    k.op("dve", lambda E: E.scalar_tensor_tensor(out=X1[0:nt, t, hs], in0=pH[0:nt, half, :], scalar=rcol,
                                                                 in1=xt[xs_][0:nt, hs], op0=ALU.mult, op1=ALU.add),
                         r=[pHb, rb, xtb[xs_]], w=[X1b[t]])
            k.barrier()
        with ExitStack() as pb:
            sbb, psb = mk(pb)
            wd = sbb("wd", [128, NFF, 1024], BF16)
            wdb = Buf()
            with ExitStack() as pw:
                wdst = [pw.enter_context(nc.sbuf_tensor(f"wdst{i}", [128, 1024], F32)) for i in range(2)]
                wdstb = bufs(2)
                for f in range(NFF):
                    s_ = f % 2
                    k.op("sp", lambda E: E.dma_start(out=wdst[s_][:], in_=dr["w_down"][f * 128:(f + 1) * 128, :]),
                         w=[wdstb[s_]], dsem=f"wd{s_}")
                    k.op("pool", lambda E: E.tensor_copy(out=wd[:, f, :], in_=wdst[s_][:]), r=[wdstb[s_]], w=[wdb])
                k.barrier()
            GW = 576
            h2 = sbb("h2", [128, 1024], BF16)
            h2b = Buf()
            h2T = sbb("h2T", [128, 8, GW], BF16)
            h2Tb = Buf()
            actT = sbb("actT", [128, NFF, GW], BF16)
            actTb = Buf()
            wgs = [sbb(f"wgs{i}", [128, 8, 128], F32) for i in range(3)]
            wgsb = bufs(3)
            wgb_ = [sbb(f"wgb{i}", [128, 8, 128], BF16) for i in range(4)]
            wgbb = bufs(4)
            sg = [sbb(f"sg{i}", [128, 512], F32) for i in range(2)]
            sgb = bufs(2)
            x2 = [sbb(f"x2_{i}", [128, 1024], F32) for i in range(2)]
            x2b = bufs(2)
            yt = [sbb(f"yt_{i}", [128, 1024], F32) for i in range(2)]
            ytb = bufs(2)
            tp2 = psb("tp2", [128, 8, 128], BF16)
            tp2b = Buf()
            pG = [psb(f"pG{i}", [128, 512], F32) for i in range(2)]
            pGb = bufs(2)
            pUp = [psb(f"pUp{i}", [128, 512], F32) for i in range(2)]
            pUpb = bufs(2)
            pD = [psb(f"pD{i}", [128, 512], F32) for i in range(2)]
            pDb = bufs(2)
            groups = [tiles[i:i + 4] for i in range(0, NT2, 4)]
            if smp is not None:
                groups[-1] = groups[-1] + [tiles[-1]]
            fnb = c["fnc"][:].unsqueeze(2).to_broadcast([128, 8, 128])
            blkctr = 0
            dctr = 0
            tctr = 0
            for gt in groups:
                ncols = sum(nt for nt, _ in gt)
                col = 0
                for (nt, t) in gt:
                    ci = (tctr % 8) * 2
                    tctr += 1
                    k.op("act", lambda E: E.activation(out=junk[0:nt, :], in_=X1[0:nt, t, :], func=AF.Square,
                                                       accum_out=cols[0:nt, ci:ci + 1]), r=[X1b[t]], w=[junkb, colb[ci]])
                    rsqrt_col(k, cols[0:nt, ci:ci + 1], colb[ci], cols[0:nt, ci + 1:ci + 2], colb[ci + 1], 1024.0, EPS)
                    k.op("dve", lambda E: E.tensor_scalar(out=h2[0:nt, :], in0=X1[0:nt, t, :], scalar1=cols[0:nt, ci + 1:ci + 2],
                                                          scalar2=None, op0=ALU.mult), r=[X1b[t], colb[ci + 1]], w=[h2b])
                    for kc in range(8):
                        k.op("pe", lambda E: E.transpose(out=tp2[:, kc, 0:nt], in_=h2[0:nt, kc * 128:(kc + 1) * 128],
                                                         identity=c["ident_b"][0:nt, 0:nt]), r=[h2b, cb], w=[tp2b])
                    k.op("act", lambda E: E.activation(out=h2T[:, :, col:col + nt], in_=tp2[:, :, 0:nt], func=AF.Copy),
                         r=[tp2b], w=[h2Tb])
                    col += nt
                blocks = [(0, min(512, ncols))]
                if ncols > 512:
                    blocks.append((512, ncols - 512))
                for f in range(NFF):
                    ws = []
                    for wi, wn in enumerate(("w_gate", "w_up")):
                        s_ = (f * 2 + wi) % 4
                        s3 = (f * 2 + wi) % 3
                        src = dr[wn][f]
                        k.op("sp", lambda E: E.dma_start(out=wgs[s3][:], in_=src), w=[wgsb[s3]], dsem=f"wg{s3}")
                        k.op("pool", lambda E: E.tensor_tensor(out=wgb_[s_][:], in0=wgs[s3][:], in1=fnb, op=ALU.mult),
                             r=[wgsb[s3], cb], w=[wgbb[s_]])
                        ws.append(s_)
                    for (c0, cn) in blocks:
                        bs = blkctr % 2
                        blkctr += 1
                        for kc in range(8):
                            k.op("pe", lambda E: E.matmul(out=pG[bs][:, 0:cn], lhsT=wgb_[ws[0]][:, kc, :], rhs=h2T[:, kc, c0:c0 + cn],
                                                          start=(kc == 0), stop=(kc == 7)), r=[wgbb[ws[0]], h2Tb], w=[pGb[bs]])
                        for kc in range(8):
                            k.op("pe", lambda E: E.matmul(out=pUp[bs][:, 0:cn], lhsT=wgb_[ws[1]][:, kc, :], rhs=h2T[:, kc, c0:c0 + cn],
                                                          start=(kc == 0), stop=(kc == 7)), r=[wgbb[ws[1]], h2Tb], w=[pUpb[bs]])
                        k.op("act", lambda E: E.activation(out=sg[bs][:, 0:cn], in_=pG[bs][:, 0:cn], func=AF.Silu),
                             r=[pGb[bs]], w=[sgb[bs]])
                        k.op("dve", lambda E: E.tensor_tensor(out=actT[:, f, c0:c0 + cn], in0=sg[bs][:, 0:cn], in1=pUp[bs][:, 0:cn],
                                                              op=ALU.mult), r=[sgb[bs], pUpb[bs]], w=[actTb])
                col = 0
                for (nt, t) in gt:
                    xs_ = t % 2
                    for half in range(2):
                        ds = dctr % 2
                        dctr += 1
                        hs = slice(half * 512, (half + 1) * 512)
                        for f in range(NFF):
                            k.op("pe", lambda E: E.matmul(out=pD[ds][0:nt, :], lhsT=actT[:, f, col:col + nt], rhs=wd[:, f, hs],
                                                          start=(f == 0), stop=(f == NFF - 1)), r=[actTb, wdb], w=[pDb[ds]])
                        k.op("dve", lambda E: E.tensor_tensor(out=x2[xs_][0:nt, hs], in0=X1[0:nt, t, hs], in1=pD[ds][0:nt, :],
                                                              op=ALU.add), r=[X1b[t], pDb[ds]], w=[x2b[xs_]])
                    ci = (tctr % 8) * 2
                    tctr += 1
                    k.op("act", lambda E: E.activation(out=junk[0:nt, :], in_=x2[xs_][0:nt, :], func=AF.Square,
                                                       accum_out=cols[0:nt, ci:ci + 1]), r=[x2b[xs_]], w=[junkb, colb[ci]])
                    rsqrt_col(k, cols[0:nt, ci:ci + 1], colb[ci], cols[0:nt, ci + 1:ci + 2], colb[ci + 1], 1024.0, EPS)
                    k.op("dve", lambda E: E.scalar_tensor_tensor(out=yt[xs_][0:nt, :], in0=x2[xs_][0:nt, :],
                                                                 scalar=cols[0:nt, ci + 1:ci + 2], in1=c["fin"][0:nt, :],
                                                                 op0=ALU.mult, op1=ALU.mult), r=[x2b[xs_], colb[ci + 1], cb], w=[ytb[xs_]])
                    dst = dr["y_p"][t * 128:(t + 1) * 128, :] if nt == 128 else dr["y_s"][:, :]
                    k.op("pool", lambda E: E.dma_start(out=dst, in_=yt[xs_][0:nt, :]), r=[ytb[xs_]], dsem=f"yo{xs_}")
                    col += nt
            k.barrier()


def phaseS(k, c, dr, es):
    nc = k.nc
    cb = c["b"]
    smp = {"b": Buf()}
    smp["mixT"] = es.enter_context(nc.sbuf_tensor("smixT", [128, 8, 64], BF16))
    smp["rstdh"] = es.enter_context(nc.sbuf_tensor("srstdh", [128, 1], F32))
    mb = smp["b"]
    qkT = es.enter_context(nc.sbuf_tensor("s_qkT", [128, 8, 64], BF16))
    Vn = es.enter_context(nc.sbuf_tensor("s_Vn", [64, 4, 130], BF16))
    with ExitStack() as st:
        def sb(name, shape, dt):
            return st.enter_context(nc.sbuf_tensor("ps_" + name, shape, dt))

        def ps(name, shape, dt):
            return st.enter_context(nc.psum_tensor("pp_" + name, shape, dt))
        ACT = lambda fn, r, w: k.op("act", fn, r=r, w=w)
        DVE = lambda fn, r, w: k.op("dve", fn, r=r, w=w)
        POOL = lambda fn, r, w: k.op("pool", fn, r=r, w=w)
        PE = lambda fn, r, w: k.op("pe", fn, r=r, w=w)
        Z = sb("Z", [64, NIN], F32)
        Zb = Buf()
        ZT = [sb(f"ZT{i}", [128, 4, 64], F32) for i in range(3)]
        ZTb = Buf()
        gb = Buf()
        with ExitStack() as s0:
            xs_t = s0.enter_context(nc.sbuf_tensor("sx", [64, 1024], F32))
            xn = s0.enter_context(nc.sbuf_tensor("sxn", [64, 1024], BF16))
            junk = s0.enter_context(nc.sbuf_tensor("sjunk", [64, 1024], BF16))
            cl = s0.enter_context(nc.sbuf_tensor("scl", [64, 2], F32))
            hTs = s0.enter_context(nc.sbuf_tensor("shT", [128, 8, 64], BF16))
            wst = [s0.enter_context(nc.sbuf_tensor(f"swst{i}", [128, 512], F32)) for i in range(3)]
            wstb = bufs(3)
            wb = [s0.enter_context(nc.sbuf_tensor(f"swb{i}", [128, 512], BF16)) for i in range(3)]
            wbb = bufs(3)
            tps = s0.enter_context(nc.psum_tensor("stp", [128, 8, 128], BF16))
            pz = [s0.enter_context(nc.psum_tensor(f"spz{i}", [128, 512], F32)) for i in range(2)]
            pzb = bufs(2)
            pfh = [s0.enter_context(nc.psum_tensor(f"spfh{i}", [128, 512], F32)) for i in range(4)]
            pfhb = bufs(4)
            k.op("sp", lambda E: E.dma_start(out=xs_t[:], in_=dr["xs"][:, :]), w=[gb], dsem="sx")
            ACT(lambda E: E.activation(out=junk[:], in_=xs_t[:], func=AF.Square, accum_out=cl[:, 0:1]), [gb], [gb])
            rsqrt_col(k, cl[:, 0:1], gb, cl[:, 1:2], gb, 1024.0, EPS)
            DVE(lambda E: E.tensor_scalar(out=xn[:], in0=xs_t[:], scalar1=cl[:, 1:2], scalar2=None, op0=ALU.mult), [gb], [gb])
            for kc in range(8):
                PE(lambda E: E.transpose(out=tps[:, kc, 0:64], in_=xn[:, kc * 128:(kc + 1) * 128], identity=c["ident_b"][0:64, 0:64]),
                   [gb, cb], [gb])
            ACT(lambda E: E.activation(out=hTs[:], in_=tps[:, :, 0:64], func=AF.Copy), [gb], [gb])
            n = 0
            for cg in range(7):
                for kc in range(8):
                    s_ = n % 3
                    n += 1
                    k.op("sp", lambda E: E.dma_start(out=wst[s_][:], in_=dr["w_in"][kc * 128:(kc + 1) * 128, cg * 512:(cg + 1) * 512]),
                         w=[wstb[s_]], dsem=f"sw{s_}")
                    DVE(lambda E: E.tensor_scalar(out=wb[s_][:], in0=wst[s_][:], scalar1=c["an"][:, kc:kc + 1], scalar2=None,
                                                  op0=ALU.mult), [wstb[s_], cb], [wbb[s_]])
                    PE(lambda E: E.matmul(out=pz[cg % 2][0:64, :], lhsT=hTs[:, kc, :], rhs=wb[s_][:], start=(kc == 0), stop=(kc == 7)),
                       [gb, wbb[s_]], [pzb[cg % 2]])
                    if cg in (3, 4, 6):
                        for h in range(4):
                            PE(lambda E: E.matmul(out=pfh[h][:, 0:64], lhsT=wb[s_][:, h * 128:(h + 1) * 128], rhs=hTs[:, kc, :],
                                                  start=(kc == 0), stop=(kc == 7)), [gb, wbb[s_]], [pfhb[h]])
                ACT(lambda E: E.activation(out=Z[:, cg * 512:(cg + 1) * 512], in_=pz[cg % 2][0:64, :], func=AF.Copy),
                    [pzb[cg % 2]], [Zb])
                if cg in (3, 4, 6):
                    qi_ = {3: 0, 4: 1, 6: 2}[cg]
                    for h in range(4):
                        ACT(lambda E: E.activation(out=ZT[qi_][:, h, :], in_=pfh[h][:, 0:64], func=AF.Copy), [pfhb[h]], [ZTb])
            k.barrier()
        rtmp = sb("rtmp", [64, 512], F32)
        rope_inplace(k, Z[:, 0:1024].rearrange("p (g d) -> p g d", d=64), Zb, c["coss"][:, :], c["sins"][:, :], rtmp, gb, 64, 16)
        k.op("pool", lambda E: E.dma_start(out=dr["ks_out"][:, :], in_=Z[:, 512:1024]), r=[Zb], dsem="so2")
        k.op("pool", lambda E: E.dma_start(out=dr["vs_out"][:, :], in_=Z[:, 1024:1536]), r=[Zb], dsem="so2")
        if STOP <= 1:
            k.barrier()
            return smp
        qkb = sb("qkb", [64, 1024], BF16)
        Vhs = sb("Vhs", [64, 4, 128], BF16)
        tpb_ = ps("tpb", [128, 8, 128], BF16)
        ACT(lambda E: E.activation(out=qkb[:, 0:512], in_=Z[:, 0:512], func=AF.Copy, scale=0.125), [Zb], [gb])
        DVE(lambda E: E.tensor_copy(out=qkb[:, 512:1024], in_=Z[:, 512:1024]), [Zb], [gb])
        DVE(lambda E: E.memset(Vn[:, :, 128:130], 1.0), [], [gb])
        DVE(lambda E: E.tensor_copy(out=Vn[:, :, 0:128], in_=Z[:, 1024:1536].rearrange("p (h e) -> p h e", e=128)), [Zb], [gb])
        DVE(lambda E: E.tensor_copy(out=Vhs[:], in_=Z[:, 2560:3072].rearrange("p (h e) -> p h e", e=128)), [Zb], [gb])
        for i8 in range(8):
            PE(lambda E: E.transpose(out=tpb_[:, i8, 0:64], in_=qkb[:, i8 * 128:(i8 + 1) * 128], identity=c["ident_b"][0:64, 0:64]),
               [gb, cb], [gb])
        ACT(lambda E: E.activation(out=qkT[:], in_=tpb_[:, :, 0:64], func=AF.Copy), [gb], [gb])
        if STOP <= 1.3:
            k.barrier()
            return smp
        pf = ZT
        hs = {n_: sb("h_" + n_, [128, 4, 64], F32) for n_ in ("F", "G", "KT", "QS", "SG", "GC", "E1", "E", "D")}
        hbf = {n_: sb("hb_" + n_, [128, 4, 64], BF16) for n_ in ("Qb", "Kh", "Kb")}
        ACT(lambda E: E.activation(out=hs["F"][:], in_=pf[1][:], func=AF.Sigmoid), [gb, ZTb], [gb])
        ACT(lambda E: E.activation(out=hs["QS"][:], in_=pf[0][:], func=AF.Silu), [gb, ZTb], [gb])
        ACT(lambda E: E.activation(out=hs["SG"][:], in_=pf[2][:], func=AF.Silu), [gb, ZTb], [gb])
        for h in range(4):
            DVE(lambda E: E.tensor_scalar(out=hs["F"][:, h, :], in0=hs["F"][:, h, :], scalar1=c["oml_a"][:, h:h + 1],
                                          scalar2=c["lb_a"][:, h:h + 1], op0=ALU.mult, op1=ALU.add), [gb, cb], [gb])
        ACT(lambda E: E.activation(out=hs["GC"][:], in_=hs["F"][:], func=AF.Ln), [gb], [gb])
        DVE(lambda E: E.tensor_scalar(out=hs["KT"][:], in0=hs["F"][:], scalar1=-1.0, scalar2=1.0, op0=ALU.mult, op1=ALU.add), [gb], [gb])
        GC4 = hs["GC"][:].rearrange("p h (s t) -> p (h s) t", t=4)
        for t_ in range(1, 4):
            DVE(lambda E: E.tensor_tensor(out=GC4[:, :, t_:t_ + 1], in0=GC4[:, :, t_:t_ + 1], in1=GC4[:, :, t_ - 1:t_], op=ALU.add), [gb], [gb])
        ACT(lambda E: E.activation(out=hs["E1"][:], in_=hs["GC"][:], func=AF.Exp), [gb], [gb])
        DVE(lambda E: E.tensor_tensor(out=hbf["Qb"][:], in0=hs["QS"][:], in1=hs["E1"][:], op=ALU.mult), [gb], [gb])
        ACT(lambda E: E.activation(out=hs["E"][:], in_=hs["GC"][:], func=AF.Exp, scale=-1.0), [gb], [gb])
        DVE(lambda E: E.tensor_tensor(out=hbf["Kh"][:], in0=hs["KT"][:], in1=hs["E"][:], op=ALU.mult), [gb], [gb])
        D4 = hs["D"][:].rearrange("p h (s t) -> p (h s) t", t=4)
        DVE(lambda E: E.tensor_tensor(out=D4, in0=GC4[:, :, 3:4].to_broadcast([128, 64, 4]), in1=GC4, op=ALU.subtract), [gb], [gb])
        ACT(lambda E: E.activation(out=hs["E"][:], in_=hs["D"][:], func=AF.Exp), [gb], [gb])
        DVE(lambda E: E.tensor_tensor(out=hbf["Kb"][:], in0=hs["KT"][:], in1=hs["E"][:], op=ALU.mult), [gb], [gb])
        KbTok = sb("KbTok", [64, 4, 128], BF16)
        KbSel = sb("KbSel", [64, 4, 16, 128], BF16)
        for h in range(4):
            PE(lambda E: E.transpose(out=tpb_[0:64, h, :], in_=hbf["Kb"][:, h, :], identity=c["ident_b"][:]), [gb, cb], [gb])
        ACT(lambda E: E.activation(out=KbTok[:], in_=tpb_[0:64, 0:4, :], func=AF.Copy), [gb], [gb])
        for h in range(4):
            DVE(lambda E: E.tensor_tensor(out=KbSel[:, h, :, :], in0=KbTok[:, h, :].unsqueeze(1).to_broadcast([64, 16, 128]),
                                           in1=c["sel"][:].unsqueeze(2).to_broadcast([64, 16, 128]), op=ALU.mult), [gb, cb], [gb])
        pAs = ps("pAs", [128, 4, 64], F32)
        Am = sb("Am", [64, 4, 64], BF16)
        for h in range(4):
            PE(lambda E: E.matmul(out=pAs[0:64, h, :], lhsT=hbf["Kh"][:, h, :], rhs=hbf["Qb"][:, h, :], start=True, stop=True), [gb], [gb])
        DVE(lambda E: E.tensor_tensor(out=Am[:], in0=pAs[0:64, :, :], in1=c["bmask_f"][:].unsqueeze(1).to_broadcast([64, 4, 64]),
                                      op=ALU.mult), [gb, cb], [gb])
        if STOP <= 1.5:
            k.barrier()
            return smp
        pO = ps("pO", [128, 4, 64], F32)
        pOb = Buf()
        pUs = [ps(f"pUs{i}", [128, 4, 128], F32) for i in range(2)]
        pUsb = bufs(2)
        S0 = [sb(f"S0_{i}", [128, 4, 128], F32) for i in range(2)]
        S0b = bufs(2)
        S0bf = [sb(f"S0bf_{i}", [128, 4, 128], BF16) for i in range(2)]
        S0bfb = bufs(2)
        Sn = [sb(f"Sn_{i}", [128, 4, 128], F32) for i in range(2)]
        Snb = bufs(2)
        pOi = ps("pOi", [128, 4, 64], F32)
        Oin = sb("Oin", [128, 4, 64], F32)
        for h in range(4):
            PE(lambda E: E.matmul(out=pOi[:, h, :], lhsT=Vhs[:, h, :], rhs=Am[:, h, :], start=True, stop=True), [gb], [gb])
        ACT(lambda E: E.activation(out=Oin[:], in_=pOi[:], func=AF.Copy), [gb], [gb])
        for s_ in range(16):
            sl = s_ % 2
            k.op("sp", lambda E: E.dma_start(out=S0[sl][:], in_=dr["state"][s_].rearrange("h k v -> k h v")), w=[S0b[sl]], dsem=f"st{sl}")
            ACT(lambda E: E.activation(out=S0bf[sl][:], in_=S0[sl][:], func=AF.Copy), [S0b[sl]], [S0bfb[sl]])
            for h in range(4):
                PE(lambda E: E.matmul(out=pO[:, h, s_ * 4:(s_ + 1) * 4], lhsT=S0bf[sl][:, h, :], rhs=hbf["Qb"][:, h, s_ * 4:(s_ + 1) * 4],
                                      start=True, stop=True), [S0bfb[sl], gb], [pOb])
            for h in range(4):
                PE(lambda E: E.matmul(out=pUs[sl][:, h, :], lhsT=KbSel[:, h, s_, :], rhs=Vhs[:, h, :], start=True, stop=True),
                   [gb], [pUsb[sl]])
            for h in range(4):
                DVE(lambda E: E.scalar_tensor_tensor(out=Sn[sl][:, h, :], in0=S0[sl][:, h, :], scalar=hs["E1"][:, h, s_ * 4 + 3:s_ * 4 + 4],
                                                     in1=pUs[sl][:, h, :], op0=ALU.mult, op1=ALU.add), [S0b[sl], pUsb[sl], gb], [Snb[sl]])
            k.op("pool", lambda E: E.dma_start(out=dr["ss_out"][s_].rearrange("h k v -> k h v"), in_=Sn[sl][:]), r=[Snb[sl]], dsem=f"sso{sl}")
        if STOP <= 1.7:
            k.barrier()
            return smp
        SQ = sb("SQ", [128, 4, 64], BF16)
        pSs = pAs[:, 0, 0:4]
        DVE(lambda E: E.tensor_tensor(out=Oin[:], in0=Oin[:], in1=pO[:], op=ALU.add), [pOb, gb], [gb])
        ACT(lambda E: E.activation(out=SQ[:], in_=Oin[:], func=AF.Square), [gb], [gb])
        for h in range(4):
            PE(lambda E: E.matmul(out=pSs[0:64, h:h + 1], lhsT=SQ[:, h, :], rhs=c["ones_b"][:, 0:1], start=True, stop=True), [gb, cb], [gb])
        sc = sb("sc", [64, 4], F32)
        DVE(lambda E: E.tensor_reduce(out=sc[:, 0:1], in_=pSs[0:64, 0:4], axis=AX.X, op=ALU.add), [gb], [gb])
        rsqrt_col(k, sc[:, 0:1], gb, smp["rstdh"][0:64, 0:1], mb, 512.0, EPS)
        for h in range(4):
            DVE(lambda E: E.scalar_tensor_tensor(out=smp["mixT"][:, 4 + h, :], in0=Oin[:, h, :], scalar=c["hn_a"][:, h:h + 1],
                                                 in1=hs["SG"][:, h, :], op0=ALU.mult, op1=ALU.mult), [pOb, gb, cb], [mb])
        k.barrier()
    with ExitStack() as sa:
        def sb(name, shape, dt):
            return sa.enter_context(nc.sbuf_tensor("pa_" + name, shape, dt))

        def ps(name, shape, dt):
            return sa.enter_context(nc.psum_tensor("ppa_" + name, shape, dt))
        ACT = lambda fn, r, w: k.op("act", fn, r=r, w=w)
        DVE = lambda fn, r, w: k.op("dve", fn, r=r, w=w)
        POOL = lambda fn, r, w: k.op("pool", fn, r=r, w=w)
        PE = lambda fn, r, w: k.op("pe", fn, r=r, w=w)
        gb2 = Buf()
        pScm = [ps(f"pSc{i}", [128, 512], F32) for i in range(2)]
        pScb = Buf()
        pNm = [pScm[i][0:64, 0:256].rearrange("p (h t) -> p h t", t=64) for i in range(2)]
        PnT = sb("PnT", [64, 8, 64], BF16)
        Pn32 = sb("Pn32", [64, 8, 64], F32)
        for h in range(4):
            for cm in range(2):
                PE(lambda E: E.matmul(out=pNm[cm][:, h, :], lhsT=qkT[cm * 64:(cm + 1) * 64, 4 + h, :],
                                      rhs=qkT[cm * 64:(cm + 1) * 64, h, :], start=True, stop=True), [gb], [pScb])
        Pn4 = Pn32[:].rearrange("p (h c) t -> p h c t", c=2)
        for cm in range(2):
            ACT(lambda E: E.activation(out=Pn4[:, :, cm, :], in_=pNm[cm], func=AF.Exp), [pScb], [gb2])
        DVE(lambda E: E.tensor_tensor(out=PnT[:], in0=Pn32[:], in1=c["bmask_f"][:].unsqueeze(1).to_broadcast([64, 8, 64]), op=ALU.mult),
            [gb2, cb], [gb2])
        ptb = sb("ptb", [128, 256], I32)
        idxp = sb("idxp", [128, 256], I32)
        if STOP <= 2:
            k.barrier()
            return smp
        k.op("sp", lambda E: E.dma_start(out=ptb[:], in_=dr["pt"].partition_broadcast(128)), w=[gb2], dsem="spt")
        DVE(lambda E: E.tensor_scalar(out=idxp[:], in0=ptb[:], scalar1=128.0, scalar2=c["iota_p"][:, 0:1], op0=ALU.mult, op1=ALU.add),
            [gb2, cb], [gb2])
        NPG = 4
        OA = sb("OA", [4, 16, 8, 130], F32)
        OAb = Buf()
        pOs = ps("pOs", [128, 3, 512], F32)
        pOsb = Buf()
        lp = ExitStack()
        _sb_outer = sb
        sb = lambda name, shape, dt: lp.enter_context(nc.sbuf_tensor("pl_" + name, shape, dt))
        Kpg = [sb(f"Kpg{i}", [128, 512], F32) for i in range(NPG)]
        Kpgb = bufs(NPG)
        Vpg = [sb(f"Vpg{i}", [128, 512], F32) for i in range(NPG)]
        Vpgb = bufs(NPG)
        KTs = [sb(f"KTs{i}", [128, 4, 2048], BF16) for i in range(2)]
        KTsb = bufs(2)
        Vs = [sb(f"Vs{i}", [128, 16, 4, 130], BF16) for i in range(2)]
        Vsb = bufs(2)
        for i in range(2):
            POOL(lambda E: E.memset(Vs[i][:, :, :, 128:130], 1.0), [], [Vsb[i]])
        ptk = [ps("ptk0", [128, 4, 128], BF16)] * 2
        ptkb = [Buf()] * 2
        Kbf = [sb(f"Kbf{i}", [128, 512], BF16) for i in range(2)]
        Kbfb = bufs(2)
        Ps = [sb(f"Ps{i}", [128, 512], BF16) for i in range(2)]
        Psb = bufs(2)
        pgc = 0
        for s_ in range(16 if STOP > 3 else 0):
            sl = s_ % 2
            for j in range(NPAGE):
                g_ = pgc % NPG
                pgc += 1
                icol = idxp[:, s_ * 16 + j:s_ * 16 + j + 1]
                k.op("pool", lambda E: E.indirect_dma_start(out=Kpg[g_][:], out_offset=None, in_=dr["cache_k"][:, :],
                                                            in_offset=bass.IndirectOffsetOnAxis(ap=icol, axis=0)),
                     r=[gb2], w=[Kpgb[g_]], dsem=f"kg{g_}")
                k.op("pool", lambda E: E.indirect_dma_start(out=Vpg[g_][:], out_offset=None, in_=dr["cache_v"][:, :],
                                                            in_offset=bass.IndirectOffsetOnAxis(ap=icol, axis=0)),
                     r=[gb2], w=[Vpgb[g_]], dsem=f"vg{g_}")
                tk = j % 2
                DVE(lambda E: E.tensor_copy(out=Kbf[tk][:], in_=Kpg[g_][:]), [Kpgb[g_]], [Kbfb[tk]])
                for h in range(4):
                    PE(lambda E: E.transpose(out=ptk[tk][:, h, :], in_=Kbf[tk][:, h * 128:(h + 1) * 128], identity=c["ident_b"][:]),
                       [Kbfb[tk], cb], [ptkb[tk]])
                ACT(lambda E: E.activation(out=KTs[sl][:, :, j * 128:(j + 1) * 128], in_=ptk[tk][:], func=AF.Copy), [ptkb[tk]], [KTsb[sl]])
                ACT(lambda E: E.activation(out=Vs[sl][:, j, :, 0:128], in_=Vpg[g_][:].rearrange("p (h e) -> p h e", e=128), func=AF.Copy),
                    [Vpgb[g_]], [Vsb[sl]])
            for j in range(NPAGE):
                for h in range(4):
                    for cm in range(2):
                        col = (j * 4 + h) * 4
                        PE(lambda E: E.matmul(out=pScm[cm][:, col:col + 4], lhsT=KTs[sl][cm * 64:(cm + 1) * 64, h, j * 128:(j + 1) * 128],
                                              rhs=qkT[cm * 64:(cm + 1) * 64, h, s_ * 4:(s_ + 1) * 4], start=True, stop=True),
                           [KTsb[sl], gb], [pScb])
            for cm in range(2):
                ACT(lambda E: E.activation(out=Ps[sl][:, cm * 256:(cm + 1) * 256], in_=pScm[cm][:, 0:256], func=AF.Exp), [pScb], [Psb[sl]])
            for h in range(4):
                for cm in range(2):
                    hc = h * 2 + cm
                    bk, off = divmod(hc, 3)
                    for j in range(NPAGE):
                        col = cm * 256 + (j * 4 + h) * 4
                        PE(lambda E: E.matmul(out=pOs[0:4, bk, off * 130:off * 130 + 129], lhsT=Ps[sl][:, col:col + 4],
                                              rhs=Vs[sl][:, j, h, 0:129], start=(j == 0), stop=False), [Psb[sl], Vsb[sl]], [pOsb])
                    PE(lambda E: E.matmul(out=pOs[0:4, bk, off * 130:off * 130 + 129], lhsT=PnT[:, hc, s_ * 4:(s_ + 1) * 4],
                                          rhs=Vn[:, h, 0:129], start=False, stop=True), [gb2, gb], [pOsb])
            for bk in range(3):
                ncol = 3 if bk < 2 else 2
                DVE(lambda E: E.tensor_copy(out=OA[:, s_, bk * 3:bk * 3 + ncol, 0:129],
                                            in_=pOs[0:4, bk, 0:ncol * 130].rearrange("p (a b) -> p a b", b=130)[:, :, 0:129]), [pOsb], [OAb])
        k.barrier()
        lp.close()
        if STOP <= 4:
            return smp
        sb = _sb_outer
        rden = sb("rden", [4, 16, 8], F32)
        A1 = sb("A1", [4, 64, 128], F32)
        A2 = sb("A2", [4, 64, 128], F32)
        ss = sb("ss", [4, 64], F32)
        oan = sb("oan", [4, 64, 128], BF16)
        OA5 = OA[:].rearrange("p s (h c) e -> p (s h) c e", c=2)
        rd3 = rden[:].rearrange("p s (h c) -> p (s h) c", c=2)
        DVE(lambda E: E.reciprocal(out=rden[:], in_=OA[:, :, :, 128]), [OAb], [gb2])
        DVE(lambda E: E.tensor_scalar(out=rd3[:, :, 1:2], in0=rd3[:, :, 1:2], scalar1=c["neglam"][0:4, :], scalar2=None, op0=ALU.mult),
            [gb2, cb], [gb2])
        DVE(lambda E: E.tensor_tensor(out=A1[:], in0=OA5[:, :, 0, 0:128], in1=rd3[:, :, 0:1].to_broadcast([4, 64, 128]), op=ALU.mult),
            [OAb, gb2], [gb2])
        DVE(lambda E: E.tensor_tensor(out=A2[:], in0=OA5[:, :, 1, 0:128], in1=rd3[:, :, 1:2].to_broadcast([4, 64, 128]), op=ALU.mult),
            [OAb, gb2], [gb2])
        DVE(lambda E: E.tensor_tensor(out=A1[:], in0=A1[:], in1=A2[:], op=ALU.add), [gb2], [gb2])
        DVE(lambda E: E.tensor_tensor(out=A2[:], in0=A1[:], in1=A1[:], op=ALU.mult), [gb2], [gb2])
        DVE(lambda E: E.tensor_reduce(out=ss[:], in_=A2[:], axis=AX.X, op=ALU.add), [gb2], [gb2])
        rsqrt_col(k, ss[:], gb2, ss[:], gb2, 128.0, SUBLN_EPS)
        DVE(lambda E: E.tensor_tensor(out=oan[:], in0=A1[:], in1=ss[:].unsqueeze(2).to_broadcast([4, 64, 128]), op=ALU.mult), [gb2], [gb2])
        pT = ptk[0][:, :, 0:64]
        for s_ in range(16):
            for h in range(4):
                PE(lambda E: E.transpose(out=pT[:, h, s_ * 4:(s_ + 1) * 4], in_=oan[0:4, s_ * 4 + h, :], identity=c["ident_b"][0:4, 0:4]),
                   [gb2, cb], [gb2])
        ACT(lambda E: E.activation(out=smp["mixT"][:, 0:4, :], in_=pT, func=AF.Copy), [gb2], [mb])
        k.barrier()
    return smp


def build(L=8192, sample=True, debug=0, pool_rows=2560 * 128):
    nc = bass.Bass("TRN2", target_bir_lowering=False)
    NT = L // 128
    T2 = L // 4
    NT2 = T2 // 128
    dr = {}

    def din(name, shape, dt=F32):
        dr[name] = nc.dram_tensor(name, shape, dt, kind="ExternalInput").ap()

    def dout(name, shape, dt=F32):
        dr[name] = nc.dram_tensor(name, shape, dt, kind="ExternalOutput").ap()

    din("xp", [L if debug != 7 else 128, D])
    din("xp2", [T2, D])
    din("xs", [64, D])
    din("w_in_h", [D, 896])
    din("w_in", [D, NIN])
    din("w_out", [D, D])
    din("w_gate", [NFF, 128, 8, 128])
    din("w_up", [NFF, 128, 8, 128])
    din("w_down", [DFF, D])
    din("cache_k", [pool_rows, 512])
    din("cache_v", [pool_rows, 512])
    din("state", [16, 4, 128, 128])
    din("pt", [1, 256], I32)
    din("idx_mix", [128, 8], I32)
    din("idx_ssq", [128, 4], I32)
    for nm, shp in (("c_ident", [128, 128]), ("c_mask", [128, 128]), ("c_mask64", [128, 64]), ("c_bmask", [64, 64]),
                    ("c_sel", [64, 16]), ("c_cosp", [128, NT, 8]), ("c_sinp", [128, NT, 8]), ("c_coss", [64, 8]),
                    ("c_sins", [64, 8]), ("c_iota", [128, 1]), ("an", [128, 8]), ("fnc", [128, 8]), ("lamv", [1, 256]),
                    ("lbl_h", [128, 2]), ("lbl_a", [128, 8]), ("hn_h", [128, 1]), ("hn_a", [128, 4]), ("subln", [128, 1]),
                    ("fin", [1, 1024])):
        din(nm, shp)
    dout("y_p", [T2, D])
    dout("y_s", [64, D])
    dout("k_out", [L, 128])
    dout("v_out", [L, 128])
    dout("s_out", [128, 128])
    dout("ks_out", [64, 512])
    dout("vs_out", [64, 512])
    dout("ss_out", [16, 4, 128, 128])
    dr["xch"] = nc.dram_tensor("xch", [4 * 256, T2], BF16).ap()
    dr["xssq"] = nc.dram_tensor("xssq", [4 * 128, NT2], F32).ap()
    dr["xg"] = nc.dram_tensor("xg", [4 * 4 * 256, T2], BF16).ap()
    dr["xgs"] = nc.dram_tensor("xgs", [4 * 4 * 128, NT2], F32).ap()
    dr["xgb"] = Buf()
    dr["xgsb"] = Buf()
    with ExitStack() as es:
        block = es.enter_context(nc.Block())

        def body(_):
            k = KB(nc, es)
            c = load_consts(k, es, dr, L, True)
            if debug == 7:
                smp = phaseS(k, c, dr, es)
                k.op("pool", lambda E: E.dma_start(out=dr["y_s"][:, 0:512].bitcast(BF16).rearrange("t (a b) -> t a b", a=8)[:, :, 0:128],
                                                   in_=smp["mixT"][:].rearrange("p a b -> p a b")), r=[smp["b"]], dsem="dbg")
                k.barrier()
                return
            phase1(k, c, dr, L)
            k.barrier()
            groups = [[0, 1, 2, 3], [4, 5, 6, 7]]
            if debug != 8:
                for rr_ in range(4):
                    k.op("pool", lambda E: E.collective_compute(
                        "AllGather", ALU.bypass, replica_groups=groups,
                        ins=[dr["xch"][rr_ * 256:(rr_ + 1) * 256, :]], outs=[dr["xg"][rr_ * 1024:(rr_ + 1) * 1024, :]]),
                        w=[dr["xgb"]], dsem="cc1", inc=CC_INC)
                k.op("pool", lambda E: E.collective_compute("AllGather", ALU.bypass, replica_groups=groups,
                                                            ins=[dr["xssq"][:, :]], outs=[dr["xgs"][:, :]]),
                     w=[dr["xgsb"]], dsem="cc2", inc=CC_INC)
            smp = phaseS(k, c, dr, es) if sample else None
            phase2(k, c, dr, L, smp)
            k.barrier()

        block.sync(body)
    return nc


CC_INC = 1

def host_consts(L):
    NT = L // 128
    cst = {}
    cst["c_ident"] = np.eye(128, dtype=np.float32)
    p = np.arange(128)
    cst["c_mask"] = (p[:, None] <= p[None, :]).astype(np.float32)
    cst["c_mask64"] = ((p[:, None] % 64) <= np.arange(64)[None, :]).astype(np.float32)
    t = np.arange(64)
    cst["c_bmask"] = ((t[:, None] // 4 == t[None, :] // 4) & (t[:, None] <= t[None, :])).astype(np.float32)
    cst["c_sel"] = (t[:, None] // 4 == np.arange(16)[None, :]).astype(np.float32)
    half = 8
    inv_freq = (1.0 / (np.float32(ROPE_THETA) ** (np.arange(half, dtype=np.float32) * np.float32(2.0) / np.float32(16)))).astype(np.float32)
    pos = np.arange(L, dtype=np.float32)
    ang = (pos[:, None] * inv_freq[None, :]).astype(np.float32)
    cst["c_cosp"] = np.ascontiguousarray(np.cos(ang).astype(np.float32).reshape(NT, 128, 8).transpose(1, 0, 2))
    cst["c_sinp"] = np.ascontiguousarray(np.sin(ang).astype(np.float32).reshape(NT, 128, 8).transpose(1, 0, 2))
    poss = (PAST + (np.arange(64) % 4)).astype(np.float32)
    angs = (poss[:, None] * inv_freq[None, :]).astype(np.float32)
    cst["c_coss"] = np.cos(angs).astype(np.float32)
    cst["c_sins"] = np.sin(angs).astype(np.float32)
    cst["c_iota"] = np.arange(128, dtype=np.float32).reshape(128, 1)
    return cst


def make_in_maps(inp, L):
    cst = host_consts(L)
    f = lambda a: np.ascontiguousarray(np.asarray(a, dtype=np.float32))
    w_in = np.asarray(inp["w_in"])[0]
    segs = {"qa": 0, "ka": 1, "va": 2, "qh": 3, "fh": 4, "ih": 5, "gh": 6}
    order = ["qa", "ka", "va", "ih", "qh", "fh", "gh"]
    lb = np.asarray(inp["hgrn_lb_logits"])
    maps = []
    for cidx in range(8):
        b, hd = divmod(cidx, 4)
        m = dict(cst)
        m["xp"] = f(np.asarray(inp["x_prompt"])[b, :L])
        m["w_in_h"] = f(np.concatenate([w_in[:, segs[s] * 512 + hd * 128: segs[s] * 512 + (hd + 1) * 128] for s in order], axis=1))
        m["an"] = f(np.asarray(inp["attn_norm"])[0].reshape(8, 128).T)
        m["fnc"] = f(np.asarray(inp["ffn_norm"])[0].reshape(8, 128).T)
        m["lamv"] = f(np.concatenate([np.asarray(inp[n])[0] for n in ("lambda_q1", "lambda_k1", "lambda_q2", "lambda_k2")]).reshape(1, 256))
        m["lbl_h"] = f(lb[:, hd * 128:(hd + 1) * 128].T)
        m["lbl_a"] = f(lb.reshape(2, 4, 128).transpose(2, 0, 1).reshape(128, 8))
        hn = np.asarray(inp["hgrn_norm"])[0]
        m["hn_h"] = f(hn[hd * 128:(hd + 1) * 128].reshape(128, 1))
        m["hn_a"] = f(hn.reshape(4, 128).T)
        m["subln"] = f(np.asarray(inp["subln_w"])[0].reshape(128, 1))
        m["fin"] = f(np.asarray(inp["final_norm"]).reshape(1, 1024))
        T2 = L // 4
        m["xp2"] = f(np.asarray(inp["x_prompt"])[b, hd * T2:(hd + 1) * T2])
        m["xs"] = f(np.asarray(inp["x_sample"])[cidx * 16:(cidx + 1) * 16].reshape(64, D))
        m["w_in"] = f(w_in)
        m["w_out"] = f(np.asarray(inp["w_out"])[0])
        m["w_gate"] = f(np.asarray(inp["w_gate"])[0].reshape(8, 128, NFF, 128).transpose(2, 1, 0, 3))
        m["w_up"] = f(np.asarray(inp["w_up"])[0].reshape(8, 128, NFF, 128).transpose(2, 1, 0, 3))
        m["w_down"] = f(np.asarray(inp["w_down"])[0])
        m["cache_k"] = np.asarray(inp["cache_k"], dtype=np.float32).reshape(2560 * 128, 512)
        m["cache_v"] = np.asarray(inp["cache_v"], dtype=np.float32).reshape(2560 * 128, 512)
        m["state"] = f(np.asarray(inp["state_hgrn"])[0, cidx * 16:(cidx + 1) * 16])
        m["pt"] = np.ascontiguousarray(np.asarray(inp["page_table"], dtype=np.int32)[cidx * 16:(cidx + 1) * 16].reshape(1, 256))
        pp = np.arange(128, dtype=np.int32)
        r_ = hd
        m["idx_mix"] = np.ascontiguousarray(np.stack([r_ * 1024 + (kc % 4) * 256 + (kc // 4) * 128 + pp for kc in range(8)], axis=1).astype(np.int32))
        m["idx_ssq"] = np.ascontiguousarray(np.stack([h * 512 + r_ * 128 + pp for h in range(4)], axis=1).astype(np.int32))
        maps.append(m)
    return maps


_NC_CACHE = {}


def kernel(**inp):
    L = 8192
    if L not in _NC_CACHE:
        _NC_CACHE[L] = build(L=L)
    nc = _NC_CACHE[L]
    maps = make_in_maps(inp, L)
    res = run_bass_kernel_spmd(nc, maps, core_ids=list(range(8)))
    return assemble(res.results, L)


def assemble(R, L):
    T2 = L // 4
    y_p = np.zeros((2, L, D), np.float32)
    y_s = np.zeros((128, 4, D), np.float32)
    k_p = np.zeros((1, 2, L, 4, 2, 64), np.float32)
    v_p = np.zeros((1, 2, L, 4, 128), np.float32)
    s_p = np.zeros((1, 2, 4, 128, 128), np.float32)
    k_s = np.zeros((1, 128, 4, 4, 2, 64), np.float32)
    v_s = np.zeros((1, 128, 4, 4, 128), np.float32)
    s_s = np.zeros((1, 128, 4, 128, 128), np.float32)
    for cidx in range(8):
        b, hd = divmod(cidx, 4)
        r = R[cidx]
        y_p[b, hd * T2:(hd + 1) * T2] = np.asarray(r["y_p"])
        y_s[cidx * 16:(cidx + 1) * 16] = np.asarray(r["y_s"]).reshape(16, 4, D)
        k_p[0, b, :, hd] = np.asarray(r["k_out"]).reshape(L, 2, 64)
        v_p[0, b, :, hd] = np.asarray(r["v_out"])
        s_p[0, b, hd] = np.asarray(r["s_out"])
        k_s[0, cidx * 16:(cidx + 1) * 16] = np.asarray(r["ks_out"]).reshape(16, 4, 4, 2, 64)
        v_s[0, cidx * 16:(cidx + 1) * 16] = np.asarray(r["vs_out"]).reshape(16, 4, 4, 128)
        s_s[0, cidx * 16:(cidx + 1) * 16] = np.asarray(r["ss_out"])
    return (y_p, y_s, k_p, v_p, s_p, k_s, v_s, s_s)
```

```python
import numpy as np
from contextlib import ExitStack
import concourse.bass as bass
import concourse.mybir as mybir
from concourse.bass_utils import run_bass_kernel_spmd

F32 = mybir.dt.float32
BF16 = mybir.dt.bfloat16
I32 = mybir.dt.int32
AF = mybir.ActivationFunctionType
ALU = mybir.AluOpType
AX = mybir.AxisListType

D = 1024
NIN = 3584
DFF = 2816
NFF = DFF // 128
ROPE_THETA = 500000.0
PAST = 2048
NPAGE = 16
EPS = 1e-6
SUBLN_EPS = 1e-5
LAM_INIT = 0.2
STOP = 99


class Buf:
    __slots__ = ("w", "r")

    def __init__(self):
        self.w = None
        self.r = {}


def bufs(n):
    return [Buf() for _ in range(n)]


class KB:
    def __init__(self, nc, es):
        self.nc = nc
        self.es = es
        self.eng = dict(pe=nc.tensor, act=nc.scalar, dve=nc.vector, pool=nc.gpsimd, sp=nc.sync)
        self.sem = {}
        self.cnt = {}
        self.seen = {}
        for k in self.eng:
            self._mksem(k)

    def _mksem(self, k):
        self.sem[k] = self.es.enter_context(self.nc.semaphore("s_" + k))
        self.cnt[k] = 0

    def op(self, e, fn, r=(), w=(), dsem=None, inc=16):
        need = {}

        def add(t):
            if t is None:
                return
            kk, v = t
            if need.get(kk, 0) < v:
                need[kk] = v

        for b in r:
            add(b.w)
        for b in w:
            add(b.w)
            for kk, v in b.r.items():
                add((kk, v))
        E = self.eng[e]
        for kk, v in need.items():
            if kk == e and e == "pe":
                continue
            if self.seen.get((e, kk), 0) >= v:
                continue
            E.wait_ge(self.sem[kk], v)
            self.seen[(e, kk)] = v
        inst = fn(E)
        if dsem is not None:
            if dsem not in self.sem:
                self._mksem(dsem)
            self.cnt[dsem] += inc
            inst.then_inc(self.sem[dsem], inc)
            tag = (dsem, self.cnt[dsem])
        else:
            self.cnt[e] += 1
            inst.then_inc(self.sem[e], 1)
            tag = (e, self.cnt[e])
        for b in r:
            if b.r.get(tag[0], 0) < tag[1]:
                b.r[tag[0]] = tag[1]
        for b in w:
            b.w = tag
            b.r = {}
        return inst

    def barrier(self, engines=("pe", "act", "dve", "pool", "sp")):
        for e in engines:
            E = self.eng[e]
            for kk, v in self.cnt.items():
                if v == 0 or kk == e:
                    continue
                if self.seen.get((e, kk), 0) >= v:
                    continue
                E.wait_ge(self.sem[kk], v)
                self.seen[(e, kk)] = v


def rsqrt_col(k, src, srcb, dst, dstb, n, eps):
    k.op("act", lambda E: E.activation(out=dst, in_=src, func=AF.Ln, scale=1.0 / n, bias=k.epsc[eps][0:src.shape[0], :]),
         r=[srcb], w=[dstb])
    k.op("act", lambda E: E.activation(out=dst, in_=dst, func=AF.Exp, scale=-0.5), r=[dstb], w=[dstb])


def load_consts(k, es, dr, L, do_sample):
    nc = k.nc
    NT = L // 128
    c = {}

    def sb(name, shape, dt):
        return es.enter_context(nc.sbuf_tensor("cs_" + name, shape, dt))

    cb = Buf()
    c["b"] = cb
    stage = sb("c_stage", [128, 256], F32)
    c["ident_f"] = sb("ident_f", [128, 128], F32)
    c["ident_b"] = sb("ident_b", [128, 128], BF16)
    c["mask_b"] = sb("mask_b", [128, 128], BF16)
    c["mask64_f"] = sb("mask64_f", [128, 64], F32)
    c["bmask_f"] = sb("bmask_f", [64, 64], F32)
    c["bmask_b"] = sb("bmask_b", [64, 64], BF16)
    c["sel"] = sb("sel", [64, 16], F32)
    c["ones_f"] = sb("ones_f", [128, 512], F32)
    c["ones_b"] = sb("ones_b", [128, 1], BF16)
    c["cosp"] = sb("cosp", [128, NT, 8], F32)
    c["sinp"] = sb("sinp", [128, NT, 8], F32)
    c["coss"] = sb("coss", [64, 8], F32)
    c["sins"] = sb("sins", [64, 8], F32)
    c["an"] = sb("an", [128, 8], F32)
    c["fnc"] = sb("fnc", [128, 8], F32)
    c["lamv"] = sb("lamv", [128, 256], F32)
    c["lbl_h"] = sb("lbl_h", [128, 2], F32)
    c["lbl_a"] = sb("lbl_a", [128, 8], F32)
    c["hn_h"] = sb("hn_h", [128, 1], F32)
    c["hn_a"] = sb("hn_a", [128, 4], F32)
    c["subln"] = sb("subln", [128, 1], F32)
    c["fin"] = sb("fin", [128, 1024], F32)
    c["iota_p"] = sb("iota_p", [128, 1], F32)
    c["small"] = sb("c_small", [128, 32], F32)
    epst = sb("epst", [128, 2], F32)

    def ld(dst, src):
        k.op("sp", lambda E: E.dma_start(out=dst, in_=src), w=[cb], dsem="cst")

    ld(c["ident_f"][:], dr["c_ident"])
    ld(stage[:, 0:128], dr["c_mask"])
    ld(c["mask64_f"][:], dr["c_mask64"])
    ld(c["bmask_f"][:], dr["c_bmask"])
    ld(c["sel"][:], dr["c_sel"])
    ld(c["cosp"][:], dr["c_cosp"])
    ld(c["sinp"][:], dr["c_sinp"])
    ld(c["coss"][:], dr["c_coss"])
    ld(c["sins"][:], dr["c_sins"])
    ld(c["an"][:], dr["an"])
    ld(c["fnc"][:], dr["fnc"])
    ld(c["lamv"][:], dr["lamv"].partition_broadcast(128))
    ld(c["lbl_h"][:], dr["lbl_h"])
    ld(c["lbl_a"][:], dr["lbl_a"])
    ld(c["hn_h"][:], dr["hn_h"])
    ld(c["hn_a"][:], dr["hn_a"])
    ld(c["subln"][:], dr["subln"])
    ld(c["fin"][:], dr["fin"].partition_broadcast(128))
    ld(c["iota_p"][:], dr["c_iota"])
    V = lambda fn: k.op("dve", fn, r=[cb], w=[cb])
    V(lambda E: E.memset(c["ones_f"][:], 1.0))
    V(lambda E: E.memset(c["ones_b"][:], 1.0))
    V(lambda E: E.memset(epst[:, 0:1], EPS))
    V(lambda E: E.memset(epst[:, 1:2], SUBLN_EPS))
    k.epsc = {EPS: epst[:, 0:1], SUBLN_EPS: epst[:, 1:2]}
    V(lambda E: E.tensor_copy(out=c["ident_b"][:], in_=c["ident_f"][:]))
    V(lambda E: E.tensor_copy(out=c["mask_b"][:], in_=stage[:, 0:128]))
    V(lambda E: E.tensor_copy(out=c["bmask_b"][:], in_=c["bmask_f"][:]))
    sm = c["small"]
    lv = c["lamv"]
    V(lambda E: E.scalar_tensor_tensor(out=stage[:, 0:64], in0=lv[:, 0:64], scalar=1.0, in1=lv[:, 64:128],
                                       op0=ALU.mult, op1=ALU.mult, accum_out=sm[:, 0:1]))
    V(lambda E: E.scalar_tensor_tensor(out=stage[:, 0:64], in0=lv[:, 128:192], scalar=1.0, in1=lv[:, 192:256],
                                       op0=ALU.mult, op1=ALU.mult, accum_out=sm[:, 1:2]))
    k.op("act", lambda E: E.activation(out=sm[:, 2:4], in_=sm[:, 0:2], func=AF.Exp), r=[cb], w=[cb])
    V(lambda E: E.tensor_tensor(out=sm[:, 4:5], in0=sm[:, 2:3], in1=sm[:, 3:4], op=ALU.subtract))
    V(lambda E: E.tensor_scalar(out=sm[:, 5:6], in0=sm[:, 4:5], scalar1=LAM_INIT, scalar2=-1.0,
                                op0=ALU.add, op1=ALU.mult))
    c["neglam"] = sm[:, 5:6]
    V(lambda E: E.tensor_tensor(out=sm[:, 6:7], in0=c["lbl_h"][:, 0:1], in1=c["lbl_h"][:, 1:2], op=ALU.subtract))
    V(lambda E: E.tensor_tensor(out=sm[:, 8:12], in0=c["lbl_a"][:, 0:4], in1=c["lbl_a"][:, 4:8], op=ALU.subtract))
    k.op("act", lambda E: E.activation(out=sm[:, 6:7], in_=sm[:, 6:7], func=AF.Sigmoid), r=[cb], w=[cb])
    k.op("act", lambda E: E.activation(out=sm[:, 8:12], in_=sm[:, 8:12], func=AF.Sigmoid), r=[cb], w=[cb])
    V(lambda E: E.tensor_scalar(out=sm[:, 7:8], in0=sm[:, 6:7], scalar1=-1.0, scalar2=1.0, op0=ALU.mult, op1=ALU.add))
    V(lambda E: E.tensor_scalar(out=sm[:, 12:16], in0=sm[:, 8:12], scalar1=-1.0, scalar2=1.0, op0=ALU.mult, op1=ALU.add))
    c["lb_h"] = sm[:, 6:7]
    c["oml_h"] = sm[:, 7:8]
    c["lb_a"] = sm[:, 8:12]
    c["oml_a"] = sm[:, 12:16]
    return c


def rope_inplace(k, v4, vb, cos, sin, tmp, tmpb, npart, ng):
    x1 = v4[:, :, 0:8]
    x2 = v4[:, :, 8:16]
    cb_ = cos.unsqueeze(1).to_broadcast([npart, ng, 8])
    sb_ = sin.unsqueeze(1).to_broadcast([npart, ng, 8])
    t = [tmp[0:npart, j * ng * 8:(j + 1) * ng * 8].rearrange("p (g d) -> p g d", d=8) for j in range(4)]
    V = lambda fn, r, w: k.op("dve", fn, r=r, w=w)
    V(lambda E: E.tensor_tensor(out=t[0], in0=x1, in1=cb_, op=ALU.mult), [vb], [tmpb])
    V(lambda E: E.tensor_tensor(out=t[1], in0=x2, in1=sb_, op=ALU.mult), [vb], [tmpb])
    V(lambda E: E.tensor_tensor(out=t[2], in0=x2, in1=cb_, op=ALU.mult), [vb], [tmpb])
    V(lambda E: E.tensor_tensor(out=t[3], in0=x1, in1=sb_, op=ALU.mult), [vb], [tmpb])
    V(lambda E: E.tensor_tensor(out=x1, in0=t[0], in1=t[1], op=ALU.subtract), [tmpb], [vb])
    V(lambda E: E.tensor_tensor(out=x2, in0=t[2], in1=t[3], op=ALU.add), [tmpb], [vb])


def phase1(k, c, dr, L):
    nc = k.nc
    NT = L // 128
    NG = L // 512
    cb = c["b"]
    with ExitStack() as ph, ExitStack() as ip:
        cur = [ph]

        def sb(name, shape, dt):
            return cur[0].enter_context(nc.sbuf_tensor(name, shape, dt))

        def ps(name, shape, dt):
            return cur[0].enter_context(nc.psum_tensor(name, shape, dt))

        QKT = sb("QKT", [128, 2, L], BF16)
        QKTb = bufs(NT)
        Vaug = sb("Vaug", [128, NT, 130], BF16)
        Vaugb = bufs(NT)
        cur[0] = ip
        Vh = sb("Vh", [128, NT, 128], BF16)
        Vhb = bufs(NT)
        w1 = sb("w1", [128, 8, 896], BF16)
        w1b = Buf()
        SSQ = sb("SSQ", [128, NT], F32)
        SSQb = Buf()
        with ExitStack() as wph:
            wst = [wph.enter_context(nc.sbuf_tensor(f"wst{i}", [128, 896], F32)) for i in range(2)]
            wstb = bufs(2)
            for kc in range(8):
                s = kc % 2
                k.op("sp", lambda E: E.dma_start(out=wst[s][:], in_=dr["w_in_h"][kc * 128:(kc + 1) * 128, :]),
                     w=[wstb[s]], dsem=f"w{s}")
                k.op("dve", lambda E: E.tensor_scalar(out=w1[:, kc, :], in0=wst[s][:], scalar1=c["an"][:, kc:kc + 1],
                                                      scalar2=None, op0=ALU.mult), r=[wstb[s], cb], w=[w1b])
            k.barrier()
        k.op("pool", lambda E: E.memset(Vaug[:, :, 128:130], 1.0), w=Vaugb)
        if STOP <= 1:
            return

        NX = 3
        xs = [sb(f"xs{i}", [128, 1024], F32) for i in range(NX)]
        xsb = bufs(NX)
        junk = sb("junk", [128, 1024], BF16)
        junkb = Buf()
        cols = sb("cols", [128, 16], F32)
        colb = bufs(16)
        xn = [sb(f"xn{i}", [128, 1024], BF16) for i in range(2)]
        xnb = bufs(2)
        hT = [sb(f"hT{i}", [128, 8, 512], BF16) for i in range(2)]
        hTb = [bufs(4) for _ in range(2)]
        qkvf = [sb(f"qkvf{i}", [128, 384], F32) for i in range(2)]
        qkvfb = bufs(2)
        rtmp = sb("rtmp", [128, 128], F32)
        rtmpb = Buf()
        qkb = [sb(f"qkb{i}", [128, 256], BF16) for i in range(2)]
        qkbb = bufs(2)
        tp = [ps(f"tp{i}", [128, 8, 128], BF16) for i in range(2)]
        tpb = bufs(2)
        ztm = [ps(f"ztm{i}", [128, 512], F32) for i in range(2)]
        ztmb = bufs(2)
        tq = ps("tq", [128, 8, 128], BF16)
        tqb = Buf()
        zfm = [ps(f"zfm{i}", [128, 512], F32) for i in range(3)]
        zfmb = bufs(3)
        pOT, pOTb = zfm[0], zfmb[0]
        pmisc, pAb = zfm[1], zfmb[1]
        pSb = pAb
        pU, pUb = zfm[2], zfmb[2]
        hg = {n: sb("hg_" + n, [128, 512], F32) for n in ("F", "G", "KT", "GC", "D", "QS", "SG", "E", "E1")}
        hgb = {n: Buf() for n in hg}
        GS = sb("GS", [128, 8], F32)
        GSb = Buf()
        hb = {n: sb("hb_" + n, [128, 512], BF16) for n in ("Qh", "Kh", "Kb", "Qb", "Y")}
        hbb = {n: Buf() for n in hb}
        KbT = sb("KbT", [128, 4, 128], BF16)
        KbTb = Buf()
        Am = [sb(f"Am{i}", [128, 64], BF16) for i in range(2)]
        Amb = bufs(2)
        S = [sb(f"S{i}", [128, 128], F32) for i in range(2)]
        Sb = bufs(2)
        Sbf = [sb(f"Sbf{i}", [128, 128], BF16) for i in range(2)]
        Sbfb = bufs(2)
        k.op("dve", lambda E: E.memset(S[0][:], 0.0), w=[Sb[0]])
        k.op("dve", lambda E: E.memset(Sbf[0][:], 0.0), w=[Sbfb[0]])
        scur = 0

        def hgrn(g, gs):
            nonlocal scur
            for m in range(3):
                for kc in range(8):
                    k.op("pe", lambda E: E.matmul(out=zfm[m][:, :], lhsT=w1[:, kc, 512 + m * 128:512 + (m + 1) * 128],
                                                  rhs=hT[gs][:, kc, :], start=(kc == 0), stop=(kc == 7)),
                         r=hTb[gs] + [w1b], w=[zfmb[m]])
            yield
            zq, zf, zg = zfm
            ACT = lambda fn, r, w: k.op("act", fn, r=r, w=w)
            DVE = lambda fn, r, w: k.op("dve", fn, r=r, w=w)
            POOL = lambda fn, r, w: k.op("pool", fn, r=r, w=w)
            ACT(lambda E: E.activation(out=hg["F"][:], in_=zf[:], func=AF.Sigmoid), [zfmb[1]], [hgb["F"]])
            ACT(lambda E: E.activation(out=hg["QS"][:], in_=zq[:], func=AF.Silu), [zfmb[0]], [hgb["QS"]])
            ACT(lambda E: E.activation(out=hg["SG"][:], in_=zg[:], func=AF.Silu), [zfmb[2]], [hgb["SG"]])
            DVE(lambda E: E.tensor_scalar(out=hg["F"][:], in0=hg["F"][:], scalar1=c["oml_h"], scalar2=c["lb_h"],
                                          op0=ALU.mult, op1=ALU.add), [hgb["F"], cb], [hgb["F"]])
            ACT(lambda E: E.activation(out=hg["G"][:], in_=hg["F"][:], func=AF.Ln), [hgb["F"]], [hgb["G"]])
            POOL(lambda E: E.tensor_scalar(out=hg["KT"][:], in0=hg["F"][:], scalar1=-1.0, scalar2=1.0,
                                           op0=ALU.mult, op1=ALU.add), [hgb["F"]], [hgb["KT"]])
            DVE(lambda E: E.tensor_tensor_scan(out=hg["GC"][:], data0=c["ones_f"][:], data1=hg["G"][:], initial=0.0,
                                               op0=ALU.mult, op1=ALU.add), [hgb["G"], cb], [hgb["GC"]])
            GC3 = hg["GC"][:].rearrange("p (c t) -> p c t", t=64)
            D3 = hg["D"][:].rearrange("p (c t) -> p c t", t=64)
            DVE(lambda E: E.memset(GS[:, 0:1], 0.0), [], [GSb])
            DVE(lambda E: E.tensor_copy(out=GS[:, 1:8], in_=GC3[:, 0:7, 63]), [hgb["GC"]], [GSb])
            bc = lambda ap: ap.to_broadcast([128, 8, 64])
            DVE(lambda E: E.tensor_tensor(out=D3, in0=GC3, in1=bc(GC3[:, :, 31:32]), op=ALU.subtract),
                [hgb["GC"]], [hgb["D"]])
            ACT(lambda E: E.activation(out=hg["E"][:], in_=hg["D"][:], func=AF.Exp), [hgb["D"]], [hgb["E"]])
            DVE(lambda E: E.tensor_tensor(out=hb["Qh"][:], in0=hg["QS"][:], in1=hg["E"][:], op=ALU.mult),
                [hgb["QS"], hgb["E"]], [hbb["Qh"]])
            ACT(lambda E: E.activation(out=hg["E"][:], in_=hg["D"][:], func=AF.Exp, scale=-1.0), [hgb["D"]], [hgb["E"]])
            DVE(lambda E: E.tensor_tensor(out=hb["Kh"][:], in0=hg["KT"][:], in1=hg["E"][:], op=ALU.mult),
                [hgb["KT"], hgb["E"]], [hbb["Kh"]])
            DVE(lambda E: E.tensor_tensor(out=D3, in0=GC3, in1=bc(GC3[:, :, 63:64]), op=ALU.subtract),
                [hgb["GC"]], [hgb["D"]])
            ACT(lambda E: E.activation(out=hg["E"][:], in_=hg["D"][:], func=AF.Exp, scale=-1.0), [hgb["D"]], [hgb["E"]])
            DVE(lambda E: E.tensor_tensor(out=hb["Kb"][:], in0=hg["KT"][:], in1=hg["E"][:], op=ALU.mult),
                [hgb["KT"], hgb["E"]], [hbb["Kb"]])
            DVE(lambda E: E.tensor_tensor(out=D3, in0=GC3, in1=bc(GS[:].unsqueeze(2)), op=ALU.subtract),
                [hgb["GC"], GSb], [hgb["D"]])
            ACT(lambda E: E.activation(out=hg["E1"][:], in_=hg["D"][:], func=AF.Exp), [hgb["D"]], [hgb["E1"]])
            DVE(lambda E: E.tensor_tensor(out=hb["Qb"][:], in0=hg["QS"][:], in1=hg["E1"][:], op=ALU.mult),
                [hgb["QS"], hgb["E1"]], [hbb["Qb"]])
            for bl in range(4):
                k.op("pe", lambda E: E.transpose(out=tp[0][:, bl, :], in_=hb["Kb"][:, bl * 128:(bl + 1) * 128],
                                                 identity=c["ident_b"][:]), r=[hbb["Kb"], cb], w=[tpb[0]])
            ACT(lambda E: E.activation(out=KbT[:], in_=tp[0][:, 0:4, :], func=AF.Copy), [tpb[0]], [KbTb])
            yield
            for cc in range(8):
                bl, half = divmod(cc, 2)
                tile_i = g * 4 + bl
                p0 = half * 64
                csl = slice(cc * 64, (cc + 1) * 64)
                am_t, am_b = Am[cc % 2], Amb[cc % 2]
                k.op("pe", lambda E: E.matmul(out=pmisc[p0:p0 + 64, 0:64], lhsT=hb["Kh"][:, csl], rhs=hb["Qh"][:, csl],
                                              start=True, stop=True), r=[hbb["Kh"], hbb["Qh"]], w=[pAb])
                DVE(lambda E: E.tensor_tensor(out=am_t[p0:p0 + 64, :], in0=pmisc[p0:p0 + 64, 0:64],
                                              in1=c["mask64_f"][p0:p0 + 64, :], op=ALU.mult), [pAb, cb], [am_b])
                k.op("pe", lambda E: E.matmul(out=pOT[:, csl], lhsT=Sbf[scur][:], rhs=hb["Qb"][:, csl],
                                              start=True, stop=False), r=[Sbfb[scur], hbb["Qb"]], w=[pOTb])
                k.op("pe", lambda E: E.matmul(out=pOT[:, csl], lhsT=Vh[p0:p0 + 64, tile_i, :], rhs=am_t[p0:p0 + 64, :],
                                              start=False, stop=True), r=[Vhb[tile_i], am_b], w=[pOTb])
                k.op("pe", lambda E: E.matmul(out=pU[:, 0:128], lhsT=KbT[p0:p0 + 64, bl, :],
                                              rhs=Vh[p0:p0 + 64, tile_i, :], start=True, stop=True),
                     r=[KbTb, Vhb[tile_i]], w=[pUb])
                nxt = 1 - scur
                dcol = hg["E1"][:, cc * 64 + 63:cc * 64 + 64]
                DVE(lambda E: E.scalar_tensor_tensor(out=S[nxt][:], in0=S[scur][:], scalar=dcol, in1=pU[:, 0:128],
                                                     op0=ALU.mult, op1=ALU.add), [Sb[scur], hgb["E1"], pUb], [Sb[nxt]])
                ACT(lambda E: E.activation(out=Sbf[nxt][:], in_=S[nxt][:], func=AF.Copy), [Sb[nxt]], [Sbfb[nxt]])
                scur = nxt
                yield
            ACT(lambda E: E.activation(out=hg["E"][:], in_=pOT[:], func=AF.Square), [pOTb], [hgb["E"]])
            for bl in range(4):
                k.op("pe", lambda E: E.matmul(out=pmisc[:, 256 + bl:257 + bl], lhsT=hg["E"][:, bl * 128:(bl + 1) * 128],
                                              rhs=c["ones_f"][:, 0:1], start=True, stop=True), r=[hgb["E"], cb], w=[pSb])
            DVE(lambda E: E.tensor_copy(out=SSQ[:, g * 4:(g + 1) * 4], in_=pmisc[:, 256:260]), [pSb], [SSQb])
            DVE(lambda E: E.scalar_tensor_tensor(out=hb["Y"][:], in0=pOT[:], scalar=c["hn_h"][:, 0:1], in1=hg["SG"][:],
                                                 op0=ALU.mult, op1=ALU.mult), [pOTb, hgb["SG"], hgb["E"], cb], [hbb["Y"]])
            T2_ = L // 4
            rr_, co_ = divmod(g * 512, T2_)
            k.op("pool", lambda E: E.dma_start(out=dr["xch"][rr_ * 256 + 128:rr_ * 256 + 256, co_:co_ + 512], in_=hb["Y"][:]),
                 r=[hbb["Y"]], dsem="yo")

        pending = []

        def advance(n):
            for _ in range(n):
                if not pending:
                    return
                try:
                    next(pending[0])
                except StopIteration:
                    pending.pop(0)

        for i in range(NT):
            g, j = divmod(i, 4)
            gs = g % 2
            advance(3)
            xslot = i % NX
            x_t, x_b = xs[xslot], xsb[xslot]
            k.op("sp", lambda E: E.dma_start(out=x_t[:], in_=dr["xp"][i * 128:(i + 1) * 128, :]),
                 w=[x_b], dsem=f"x{xslot}")
            ci = (i % 8) * 2
            ssc, ssb_ = cols[:, ci:ci + 1], colb[ci]
            rsc, rsb_ = cols[:, ci + 1:ci + 2], colb[ci + 1]
            k.op("act", lambda E: E.activation(out=junk[:], in_=x_t[:], func=AF.Square, accum_out=ssc),
                 r=[x_b], w=[junkb, ssb_])
            rsqrt_col(k, ssc, ssb_, rsc, rsb_, 1024.0, EPS)
            if STOP <= 2.1:
                continue
            xn_t, xn_b = xn[i % 2], xnb[i % 2]
            k.op("dve", lambda E: E.tensor_scalar(out=xn_t[:], in0=x_t[:], scalar1=rsc, scalar2=None, op0=ALU.mult),
                 r=[x_b, rsb_], w=[xn_b])
            for kc in range(8):
                k.op("pe", lambda E: E.transpose(out=tp[i % 2][:, kc, :], in_=xn_t[:, kc * 128:(kc + 1) * 128],
                                                 identity=c["ident_b"][:]), r=[xn_b, cb], w=[tpb[i % 2]])
            k.op("act", lambda E: E.activation(out=hT[gs][:, :, j * 128:(j + 1) * 128], in_=tp[i % 2][:], func=AF.Copy),
                 r=[tpb[i % 2]], w=[hTb[gs][j]])
            if STOP <= 2.2:
                continue
            for kc in range(8):
                k.op("pe", lambda E: E.matmul(out=ztm[i % 2][:, :], lhsT=hT[gs][:, kc, j * 128:(j + 1) * 128],
                                              rhs=w1[:, kc, 0:512], start=(kc == 0), stop=(kc == 7)),
                     r=[hTb[gs][j], w1b], w=[ztmb[i % 2]])
            qs = i % 2
            qf, qfb = qkvf[qs], qkvfb[qs]
            if STOP <= 2.25:
                continue
            k.op("act", lambda E: E.activation(out=qf[:], in_=ztm[i % 2][:, 0:384], func=AF.Copy), r=[ztmb[i % 2]], w=[qfb])
            if STOP <= 2.27:
                continue
            k.op("act", lambda E: E.activation(out=Vh[:, i, :], in_=ztm[i % 2][:, 384:512], func=AF.Copy), r=[ztmb[i % 2]], w=[Vhb[i]])
            if STOP <= 2.3:
                continue
            v4 = qf[:, 0:256].rearrange("p (g d) -> p g d", d=64)
            rope_inplace(k, v4, qfb, c["cosp"][:, i, :], c["sinp"][:, i, :], rtmp, rtmpb, 128, 4)
            if STOP <= 2.4:
                continue
            k.op("pool", lambda E: E.dma_start(out=dr["k_out"][i * 128:(i + 1) * 128, :], in_=qf[:, 128:256]),
                 r=[qfb], dsem=f"kvo{qs}")
            k.op("pool", lambda E: E.dma_start(out=dr["v_out"][i * 128:(i + 1) * 128, :], in_=qf[:, 256:384]),
                 r=[qfb], dsem=f"kvo{qs}")
            if STOP <= 2.5:
                continue
            qb_t, qb_b = qkb[qs], qkbb[qs]
            k.op("act", lambda E: E.activation(out=qb_t[:, 0:128], in_=qf[:, 0:128], func=AF.Copy, scale=0.125),
                 r=[qfb], w=[qb_b])
            k.op("pool", lambda E: E.tensor_copy(out=qb_t[:, 128:256], in_=qf[:, 128:256]), r=[qfb], w=[qb_b])
            k.op("pool", lambda E: E.tensor_copy(out=Vaug[:, i, 0:128], in_=qf[:, 256:384]), r=[qfb], w=[Vaugb[i]])
            if STOP <= 2.6:
                continue
            for h_ in range(2):
                k.op("pe", lambda E: E.transpose(out=tq[:, h_, :], in_=qb_t[:, h_ * 128:(h_ + 1) * 128],
                                                 identity=c["ident_b"][:]), r=[qb_b, cb], w=[tqb])
            k.op("dve", lambda E: E.tensor_copy(out=QKT[:, :, i * 128:(i + 1) * 128], in_=tq[:, 0:2, :]), r=[tqb], w=[QKTb[i]])
            if j != 3 or STOP <= 2:
                continue
            pending.append(hgrn(g, gs))
            advance(1)
        while pending:
            advance(1)
        k.op("pool", lambda E: E.dma_start(out=dr["s_out"][:, :], in_=S[scur][:]), r=[Sb[scur]], dsem="so")
        NT2_ = NT // 4
        for rr_ in range(4):
            k.op("pool", lambda E: E.dma_start(out=dr["xssq"][rr_ * 128:(rr_ + 1) * 128, :],
                                               in_=SSQ[:, rr_ * NT2_:(rr_ + 1) * NT2_]), r=[SSQb], dsem="so")
        k.barrier()
        ip.close()
        if STOP <= 3:
            return
        attention_prompt(k, c, dr, L, ph, QKT, QKTb, Vaug, Vaugb)
        k.barrier()


def attention_prompt(k, c, dr, L, ph, QKT, QKTb, Vaug, Vaugb):
    nc = k.nc
    NT = L // 128
    cb = c["b"]
    with ExitStack() as ap:
        def sb(name, shape, dt):
            return ap.enter_context(nc.sbuf_tensor(name, shape, dt))

        def ps(name, shape, dt):
            return ap.enter_context(nc.psum_tensor(name, shape, dt))
        NS = 2
        ST = [[ps(f"ST{cc}_{s}", [128, 512], F32) for s in range(NS)] for cc in range(2)]
        STb = [bufs(NS) for _ in range(2)]
        PT = [[sb(f"PT{cc}_{s}", [128, 512], BF16) for s in range(NS)] for cc in range(2)]
        PTb = [bufs(NS) for _ in range(2)]
        O = ps("Oacc", [128, 2, 512], F32)
        Ob = Buf()
        tpo = ps("tpo", [128, 128], BF16)
        tpob = Buf()
        ec = sb("ecols", [128, 8], F32)
        ecb = Buf()
        Osb = sb("Osb", [128, 2, 129], F32)
        Osbb = Buf()
        a_t = sb("a_t", [128, 128], F32)
        oa_t = sb("oa_t", [128, 128], F32)
        ej = sb("ejunk", [128, 128], F32)
        oan = sb("oan", [128, 128], BF16)
        eb = Buf()
        ostage = [sb(f"ostage{i}", [128, 512], BF16) for i in range(2)]
        ostb = bufs(2)

        items = []
        for qb in range(NT):
            ng = qb // 4 + 1
            for g in range(ng):
                kbs = list(range(4 * g, min(4 * g + 4, qb + 1)))
                items.append((qb, g, kbs))
        slots = [n % NS for n in range(len(items))]

        def emit_qk(n):
            qb, g, kbs = items[n]
            s = slots[n]
            for jj, kb in enumerate(kbs):
                for cc in range(2):
                    k.op("pe", lambda E: E.matmul(out=ST[cc][s][:, jj * 128:(jj + 1) * 128],
                                                  lhsT=QKT[cc * 64:(cc + 1) * 64, 1, kb * 128:(kb + 1) * 128],
                                                  rhs=QKT[cc * 64:(cc + 1) * 64, 0, qb * 128:(qb + 1) * 128],
                                                  start=True, stop=True), r=[QKTb[kb], QKTb[qb]], w=[STb[cc][s]])

        emit_qk(0)
        for n, (qb, g, kbs) in enumerate(items):
            if n + 1 < len(items):
                emit_qk(n + 1)
            s = slots[n]
            nk = len(kbs)
            for cc in range(2):
                k.op("act", lambda E: E.activation(out=PT[cc][s][:, 0:nk * 128], in_=ST[cc][s][:, 0:nk * 128], func=AF.Exp),
                     r=[STb[cc][s]], w=[PTb[cc][s]])
            if kbs[-1] == qb:
                jd = nk - 1
                for cc in range(2):
                    k.op("pool", lambda E: E.tensor_tensor(out=PT[cc][s][:, jd * 128:(jd + 1) * 128],
                                                           in0=PT[cc][s][:, jd * 128:(jd + 1) * 128], in1=c["mask_b"][:],
                                                           op=ALU.mult), r=[PTb[cc][s], cb], w=[PTb[cc][s]])
            for cc in range(2):
                for jj, kb in enumerate(kbs):
                    k.op("pe", lambda E: E.matmul(out=O[:, cc, 0:129], lhsT=PT[cc][s][:, jj * 128:(jj + 1) * 128],
                                                  rhs=Vaug[:, kb, 0:129], start=(kb == 0), stop=(kb == qb)),
                         r=[PTb[cc][s], Vaugb[kb]], w=[Ob])
            if kbs[-1] != qb:
                continue
            DVE = lambda fn, r, w: k.op("dve", fn, r=r, w=w)
            DVE(lambda E: E.tensor_copy(out=Osb[:], in_=O[:, :, 0:129]), [Ob], [Osbb])
            DVE(lambda E: E.reciprocal(out=ec[:, 0:2], in_=Osb[:, :, 128]), [Osbb], [ecb])
            DVE(lambda E: E.tensor_tensor(out=ec[:, 2:3], in0=ec[:, 1:2], in1=c["neglam"], op=ALU.mult), [ecb, cb], [ecb])
            DVE(lambda E: E.tensor_scalar(out=a_t[:], in0=Osb[:, 0, 0:128], scalar1=ec[:, 0:1], scalar2=None, op0=ALU.mult),
                [Osbb, ecb], [eb])
            DVE(lambda E: E.scalar_tensor_tensor(out=oa_t[:], in0=Osb[:, 1, 0:128], scalar=ec[:, 2:3], in1=a_t[:],
                                                 op0=ALU.mult, op1=ALU.add), [Osbb, ecb, eb], [eb])
            DVE(lambda E: E.scalar_tensor_tensor(out=ej[:], in0=oa_t[:], scalar=1.0, in1=oa_t[:], op0=ALU.mult,
                                                 op1=ALU.mult, accum_out=ec[:, 3:4]), [eb], [eb, ecb])
            rsqrt_col(k, ec[:, 3:4], ecb, ec[:, 4:5], ecb, 128.0, SUBLN_EPS)
            DVE(lambda E: E.tensor_scalar(out=oan[:], in0=oa_t[:], scalar1=ec[:, 4:5], scalar2=None, op0=ALU.mult),
                [eb, ecb], [eb])
            k.op("pe", lambda E: E.transpose(out=tpo[:], in_=oan[:], identity=c["ident_b"][:]), r=[eb, cb], w=[tpob])
            og, oj = divmod(qb, 4)
            os_ = og % 2
            DVE(lambda E: E.tensor_copy(out=ostage[os_][:, oj * 128:(oj + 1) * 128], in_=tpo[:]), [tpob], [ostb[os_]])
            if oj == 3 or qb == NT - 1:
                rr_, co_ = divmod(og * 512, L // 4)
                k.op("pool", lambda E: E.dma_start(out=dr["xch"][rr_ * 256:rr_ * 256 + 128, co_:co_ + (oj + 1) * 128],
                                                   in_=ostage[os_][:, 0:(oj + 1) * 128]), r=[ostb[os_]], dsem=f"oo{os_}")


def phase2(k, c, dr, L, smp):
    nc = k.nc
    T2 = L // 4
    NT2 = T2 // 128
    cb = c["b"]
    tiles = [(128, t) for t in range(NT2)]
    if smp is not None:
        tiles.append((64, NT2))
    with ExitStack() as p2:
        def mk(stack):
            def sb(name, shape, dt):
                return stack.enter_context(nc.sbuf_tensor(name, shape, dt))

            def ps(name, shape, dt):
                return stack.enter_context(nc.psum_tensor(name, shape, dt))
            return sb, ps
        sb, ps = mk(p2)
        X1 = sb("X1", [128, NT2 + 1, 1024], F32)
        X1b = bufs(NT2 + 1)
        cols = sb("p2cols", [128, 16], F32)
        colb = bufs(16)
        junk = sb("p2junk", [128, 1024], BF16)
        junkb = Buf()
        with ExitStack() as pa:
            sba, psa = mk(pa)
            mixT = sba("mixT", [128, 8, T2], BF16)
            mixb = bufs(8)
            idxm = sba("idxm", [128, 8], I32)
            idxs = sba("idxs", [128, 4], I32)
            idxb = Buf()
            ssqg = sba("ssqg", [128, 4, NT2], F32)
            ssqb = Buf()
            rsth = sba("rsth", [128, NT2], F32)
            wout = sba("wout", [128, 8, 1024], BF16)
            woutb = Buf()
            wst = [sba(f"wost{i}", [128, 1024], F32) for i in range(2)]
            wstb = bufs(2)
            xt = [sba(f"p2x{i}", [128, 1024], F32) for i in range(2)]
            xtb = bufs(2)
            pA = psa("pA", [128, 2, 512], F32)
            pH = psa("pH", [128, 2, 512], F32)
            pAb, pHb = Buf(), Buf()
            k.op("sp", lambda E: E.dma_start(out=idxm[:], in_=dr["idx_mix"]), w=[idxb], dsem="p2i")
            k.op("sp", lambda E: E.dma_start(out=idxs[:], in_=dr["idx_ssq"]), w=[idxb], dsem="p2i")
            for kc in range(8):
                k.op("pool", lambda E: E.indirect_dma_start(
                    out=mixT[:, kc, :], out_offset=None, in_=dr["xg"][:, :],
                    in_offset=bass.IndirectOffsetOnAxis(ap=idxm[:, kc:kc + 1], axis=0)),
                    r=[idxb, dr["xgb"]], w=[mixb[kc]], dsem="p2g")
            for h in range(4):
                k.op("pool", lambda E: E.indirect_dma_start(
                    out=ssqg[:, h, :], out_offset=None, in_=dr["xgs"][:, :],
                    in_offset=bass.IndirectOffsetOnAxis(ap=idxs[:, h:h + 1], axis=0)),
                    r=[idxb, dr["xgsb"]], w=[ssqb], dsem="p2g")
            for b_ in mixb + [ssqb]:
                b_.w = ("p2g", k.cnt["p2g"])
            for kc in range(8):
                s_ = kc % 2
                k.op("sp", lambda E: E.dma_start(out=wst[s_][:], in_=dr["w_out"][kc * 128:(kc + 1) * 128, :]),
                     w=[wstb[s_]], dsem=f"wo{s_}")
                if kc < 4:
                    k.op("dve", lambda E: E.tensor_scalar(out=wout[:, kc, :], in0=wst[s_][:], scalar1=c["subln"][:, 0:1],
                                                          scalar2=1.0 - LAM_INIT, op0=ALU.mult, op1=ALU.mult),
                         r=[wstb[s_], cb], w=[woutb])
                else:
                    k.op("dve", lambda E: E.tensor_copy(out=wout[:, kc, :], in_=wst[s_][:]), r=[wstb[s_]], w=[woutb])
            k.op("dve", lambda E: E.tensor_tensor(out=rsth[:], in0=ssqg[:, 0, :], in1=ssqg[:, 1, :], op=ALU.add), r=[ssqb], w=[ssqb])
            k.op("dve", lambda E: E.tensor_tensor(out=rsth[:], in0=rsth[:], in1=ssqg[:, 2, :], op=ALU.add), r=[ssqb], w=[ssqb])
            k.op("dve", lambda E: E.tensor_tensor(out=rsth[:], in0=rsth[:], in1=ssqg[:, 3, :], op=ALU.add), r=[ssqb], w=[ssqb])
            rsqrt_col(k, rsth[:], ssqb, rsth[:], ssqb, 512.0, EPS)
            for ti, (nt, t) in enumerate(tiles):
                xs_ = ti % 2
                src = dr["xp2"][t * 128:(t + 1) * 128, :] if nt == 128 else dr["xs"][:, :]
                k.op("sp", lambda E: E.dma_start(out=xt[xs_][0:nt, :], in_=src), w=[xtb[xs_]], dsem=f"p2x{xs_}")
                for half in range(2):
                    for kc in range(8):
                        if nt == 128:
                            lhs, lb_ = mixT[:, kc, t * 128:(t + 1) * 128], mixb[kc]
                        else:
                            lhs, lb_ = smp["mixT"][:, kc, :], smp["b"]
                        dst, db = (pA, pAb) if kc < 4 else (pH, pHb)
                        k.op("pe", lambda E: E.matmul(out=dst[0:nt, half, :], lhsT=lhs, rhs=wout[:, kc, half * 512:(half + 1) * 512],
                                                      start=(kc % 4 == 0), stop=(kc % 4 == 3)), r=[lb_, woutb], w=[db])
                rcol = rsth[:, t:t + 1] if nt == 128 else smp["rstdh"][0:64, 0:1]
                rb = ssqb if nt == 128 else smp["b"]
                for half in range(2):
                    hs = slice(half * 512, (half + 1) * 512)
                    k.op("dve", lambda E: E.tensor_tensor(out=xt[xs_][0:nt, hs], in0=xt[xs_][0:nt, hs], in1=pA[0:nt, half, :],
                                                          op=ALU.add), r=[xtb[xs_], pAb], w=[xtb[xs_]])
                    k.op("dve", lambda E: E.scalar_tensor_tensor(out=X1[0:nt, t, hs], in0=pH[0:nt, half, :], scalar=rcol,
                                                                 in1=xt[xs_][0:nt, hs], op0=ALU.mult, op1=ALU.add),
                         r=[pHb, rb, xtb[xs_]], w=[X1b[t]])
            k.barrier()
        with ExitStack() as pb:
            sbb, psb = mk(pb)
            wd = sbb("wd", [128, NFF, 1024], BF16)
            wdb = Buf()
            with ExitStack() as pw:
                wdst = [pw.enter_context(nc.sbuf_tensor(f"wdst{i}", [128, 1024], F32)) for i in range(2)]
                wdstb = bufs(2)
                for f in range(NFF):
                    s_ = f % 2
                    k.op("sp", lambda E: E.dma_start(out=wdst[s_][:], in_=dr["w_down"][f * 128:(f + 1) * 128, :]),
                         w=[wdstb[s_]], dsem=f"wd{s_}")
                    k.op("pool", lambda E: E.tensor_copy(out=wd[:, f, :], in_=wdst[s_][:]), r=[wdstb[s_]], w=[wdb])
                k.barrier()
            GW = 576
            h2 = sbb("h2", [128, 1024], BF16)
            h2b = Buf()
            h2T = sbb("h2T", [128, 8, GW], BF16)
            h2Tb = Buf()
            actT = sbb("actT", [128, NFF, GW], BF16)
            actTb = Buf()
            wgs = [sbb(f"wgs{i}", [128, 8, 128], F32) for i in range(3)]
            wgsb = bufs(3)
            wgb_ = [sbb(f"wgb{i}", [128, 8, 128], BF16) for i in range(4)]
            wgbb = bufs(4)
            sg = [sbb(f"sg{i}", [128, 512], F32) for i in range(2)]
            sgb = bufs(2)
            x2 = [sbb(f"x2_{i}", [128, 1024], F32) for i in range(2)]
            x2b = bufs(2)
            yt = [sbb(f"yt_{i}", [128, 1024], F32) for i in range(2)]
            ytb = bufs(2)
            tp2 = psb("tp2", [128, 8, 128], BF16)
            tp2b = Buf()
            pG = [psb(f"pG{i}", [128, 512], F32) for i in range(2)]
            pGb = bufs(2)
            pUp = [psb(f"pUp{i}", [128, 512], F32) for i in range(2)]
            pUpb = bufs(2)
            pD = [psb(f"pD{i}", [128, 512], F32) for i in range(2)]
            pDb = bufs(2)
            groups = [tiles[i:i + 4] for i in range(0, NT2, 4)]
            if smp is not None:
                groups[-1] = groups[-1] + [tiles[-1]]
            fnb = c["fnc"][:].unsqueeze(2).to_broadcast([128, 8, 128])
            blkctr = 0
            dctr = 0
            tctr = 0
            for gt in groups:
                ncols = sum(nt for nt, _ in gt)
                col = 0
                for (nt, t) in gt:
                    ci = (tctr % 8) * 2
                    tctr += 1
                    k.op("act", lambda E: E.activation(out=junk[0:nt, :], in_=X1[0:nt, t, :], func=AF.Square,
                                                       accum_out=cols[0:nt, ci:ci + 1]), r=[X1b[t]], w=[junkb, colb[ci]])
                    rsqrt_col(k, cols[0:nt, ci:ci + 1], colb[ci], cols[0:nt, ci + 1:ci + 2], colb[ci + 1], 1024.0, EPS)
                    k.op("dve", lambda E: E.tensor_scalar(out=h2[0:nt, :], in0=X1[0:nt, t, :], scalar1=cols[0:nt, ci + 1:ci + 2],
                                                          scalar2=None, op0=ALU.mult), r=[X1b[t], colb[ci + 1]], w=[h2b])
                    for kc in range(8):
                        k.op("pe", lambda E: E.transpose(out=tp2[:, kc, 0:nt], in_=h2[0:nt, kc * 128:(kc + 1) * 128],
                                                         identity=c["ident_b"][0:nt, 0:nt]), r=[h2b, cb], w=[tp2b])
                    k.op("act", lambda E: E.activation(out=h2T[:, :, col:col + nt], in_=tp2[:, :, 0:nt], func=AF.Copy),
                         r=[tp2b], w=[h2Tb])
                    col += nt
                blocks = [(0, min(512, ncols))]
                if ncols > 512:
                    blocks.append((512, ncols - 512))
                for f in range(NFF):
                    ws = []
                    for wi, wn in enumerate(("w_gate", "w_up")):
                        s_ = (f * 2 + wi) % 4
                        s3 = (f * 2 + wi) % 3
                        src = dr[wn][f]
                        k.op("sp", lambda E: E.dma_start(out=wgs[s3][:], in_=src), w=[wgsb[s3]], dsem=f"wg{s3}")
                        k.op("pool", lambda E: E.tensor_tensor(out=wgb_[s_][:], in0=wgs[s3][:], in1=fnb, op=ALU.mult),
                             r=[wgsb[s3], cb], w=[wgbb[s_]])
                        ws.append(s_)
                    for (c0, cn) in blocks:
                        bs = blkctr % 2
                        blkctr += 1
                        for kc in range(8):
                            k.op("pe", lambda E: E.matmul(out=pG[bs][:, 0:cn], lhsT=wgb_[ws[0]][:, kc, :], rhs=h2T[:, kc, c0:c0 + cn],
                                                          start=(kc == 0), stop=(kc == 7)), r=[wgbb[ws[0]], h2Tb], w=[pGb[bs]])
                        for kc in range(8):
                            k.op("pe", lambda E: E.matmul(out=pUp[bs][:, 0:cn], lhsT=wgb_[ws[1]][:, kc, :], rhs=h2T[:, kc, c0:c0 + cn],
                                                          start=(kc == 0), stop=(kc == 7)), r=[wgbb[ws[1]], h2Tb], w=[pUpb[bs]])
                        k.op("act", lambda E: E.activation(out=sg[bs][:, 0:cn], in_=pG[bs][:, 0:cn], func=AF.Silu),
                             r=[pGb[bs]], w=[sgb[bs]])
                        k.op("dve", lambda E: E.tensor_tensor(out=actT[:, f, c0:c0 + cn], in0=sg[bs][:, 0:cn], in1=pUp[bs][:, 0:cn],
                                                              op=ALU.mult), r=[sgb[bs], pUpb[bs]], w=[actTb])
                col = 0
                for (nt, t) in gt:
                    xs_ = t % 2
                    for half in range(2):
                        ds = dctr % 2
                        dctr += 1
                        hs = slice(half * 512, (half + 1) * 512)
                        for f in range(NFF):
                            k.op("pe", lambda E: E.matmul(out=pD[ds][0:nt, :], lhsT=actT[:, f, col:col + nt], rhs=wd[:, f, hs],
                                                          start=(f == 0), stop=(f == NFF - 1)), r=[actTb, wdb], w=[pDb[ds]])
                        k.op("dve", lambda E: E.tensor_tensor(out=x2[xs_][0:nt, hs], in0=X1[0:nt, t, hs], in1=pD[ds][0:nt, :],
                                                              op=ALU.add), r=[X1b[t], pDb[ds]], w=[x2b[xs_]])
                    ci = (tctr % 8) * 2
                    tctr += 1
                    k.op("act", lambda E: E.activation(out=junk[0:nt, :], in_=x2[xs_][0:nt, :], func=AF.Square,
                                                       accum_out=cols[0:nt, ci:ci + 1]), r=[x2b[xs_]], w=[junkb, colb[ci]])
                    rsqrt_col(k, cols[0:nt, ci:ci + 1], colb[ci], cols[0:nt, ci + 1:ci + 2], colb[ci + 1], 1024.0, EPS)
                    k.op("dve", lambda E: E.scalar_tensor_tensor(out=yt[xs_][0:nt, :], in0=x2[xs_][0:nt, :],
                                                                 scalar=cols[0:nt, ci + 1:ci + 2], in1=c["fin"][0:nt, :],
                                                                 op0=ALU.mult, op1=ALU.mult), r=[x2b[xs_], colb[ci + 1], cb], w=[ytb[xs_]])
                    dst = dr["y_p"][t * 128:(t + 1) * 128, :] if nt == 128 else dr["y_s"][:, :]
                    k.op("pool", lambda E: E.dma_start(out=dst, in_=yt[xs_][0:nt, :]), r=[ytb[xs_]], dsem=f"yo{xs_}")
                    col += nt
            k.barrier()


def phaseS(k, c, dr, es):
    nc = k.nc
    cb = c["b"]
    smp = {"b": Buf()}
    smp["mixT"] = es.enter_context(nc.sbuf_tensor("smixT", [128, 8, 64], BF16))
    smp["rstdh"] = es.enter_context(nc.sbuf_tensor("srstdh", [128, 1], F32))
    mb = smp["b"]
    qkT = es.enter_context(nc.sbuf_tensor("s_qkT", [128, 8, 64], BF16))
    Vn = es.enter_context(nc.sbuf_tensor("s_Vn", [64, 4, 130], BF16))
    with ExitStack() as st:
        def sb(name, shape, dt):
            return st.enter_context(nc.sbuf_tensor("ps_" + name, shape, dt))

        def ps(name, shape, dt):
            return st.enter_context(nc.psum_tensor("pp_" + name, shape, dt))
        ACT = lambda fn, r, w: k.op("act", fn, r=r, w=w)
        DVE = lambda fn, r, w: k.op("dve", fn, r=r, w=w)
        POOL = lambda fn, r, w: k.op("pool", fn, r=r, w=w)
        PE = lambda fn, r, w: k.op("pe", fn, r=r, w=w)
        Z = sb("Z", [64, NIN], F32)
        Zb = Buf()
        ZT = [sb(f"ZT{i}", [128, 4, 64], F32) for i in range(3)]
        ZTb = Buf()
        gb = Buf()
        with ExitStack() as s0:
            xs_t = s0.enter_context(nc.sbuf_tensor("sx", [64, 1024], F32))
            xn = s0.enter_context(nc.sbuf_tensor("sxn", [64, 1024], BF16))
            junk = s0.enter_context(nc.sbuf_tensor("sjunk", [64, 1024], BF16))
            cl = s0.enter_context(nc.sbuf_tensor("scl", [64, 2], F32))
            hTs = s0.enter_context(nc.sbuf_tensor("shT", [128, 8, 64], BF16))
            wst = [s0.enter_context(nc.sbuf_tensor(f"swst{i}", [128, 512], F32)) for i in range(3)]
            wstb = bufs(3)
            wb = [s0.enter_context(nc.sbuf_tensor(f"swb{i}", [128, 512], BF16)) for i in range(3)]
            wbb = bufs(3)
            tps = s0.enter_context(nc.psum_tensor("stp", [128, 8, 128], BF16))
            pz = [s0.enter_context(nc.psum_tensor(f"spz{i}", [128, 512], F32)) for i in range(2)]
            pzb = bufs(2)
            pfh = [s0.enter_context(nc.psum_tensor(f"spfh{i}", [128, 512], F32)) for i in range(4)]
            pfhb = bufs(4)
            k.op("sp", lambda E: E.dma_start(out=xs_t[:], in_=dr["xs"][:, :]), w=[gb], dsem="sx")
            ACT(lambda E: E.activation(out=junk[:], in_=xs_t[:], func=AF.Square, accum_out=cl[:, 0:1]), [gb], [gb])
            rsqrt_col(k, cl[:, 0:1], gb, cl[:, 1:2], gb, 1024.0, EPS)
            DVE(lambda E: E.tensor_scalar(out=xn[:], in0=xs_t[:], scalar1=cl[:, 1:2], scalar2=None, op0=ALU.mult), [gb], [gb])
            for kc in range(8):
                PE(lambda E: E.transpose(out=tps[:, kc, 0:64], in_=xn[:, kc * 128:(kc + 1) * 128], identity=c["ident_b"][0:64, 0:64]),
                   [gb, cb], [gb])
            ACT(lambda E: E.activation(out=hTs[:], in_=tps[:, :, 0:64], func=AF.Copy), [gb], [gb])
            n = 0
            for cg in range(7):
                for kc in range(8):
                    s_ = n % 3
                    n += 1
                    k.op("sp", lambda E: E.dma_start(out=wst[s_][:], in_=dr["w_in"][kc * 128:(kc + 1) * 128, cg * 512:(cg + 1) * 512]),
                         w=[wstb[s_]], dsem=f"sw{s_}")
                    DVE(lambda E: E.tensor_scalar(out=wb[s_][:], in0=wst[s_][:], scalar1=c["an"][:, kc:kc + 1], scalar2=None,
                                                  op0=ALU.mult), [wstb[s_], cb], [wbb[s_]])
                    PE(lambda E: E.matmul(out=pz[cg % 2][0:64, :], lhsT=hTs[:, kc, :], rhs=wb[s_][:], start=(kc == 0), stop=(kc == 7)),
                       [gb, wbb[s_]], [pzb[cg % 2]])
                    if cg in (3, 4, 6):
                        for h in range(4):
                            PE(lambda E: E.matmul(out=pfh[h][:, 0:64], lhsT=wb[s_][:, h * 128:(h + 1) * 128], rhs=hTs[:, kc, :],
                                                  start=(kc == 0), stop=(kc == 7)), [gb, wbb[s_]], [pfhb[h]])
                ACT(lambda E: E.activation(out=Z[:, cg * 512:(cg + 1) * 512], in_=pz[cg % 2][0:64, :], func=AF.Copy),
                    [pzb[cg % 2]], [Zb])
                if cg in (3, 4, 6):
                    qi_ = {3: 0, 4: 1, 6: 2}[cg]
                    for h in range(4):
                        ACT(lambda E: E.activation(out=ZT[qi_][:, h, :], in_=pfh[h][:, 0:64], func=AF.Copy), [pfhb[h]], [ZTb])
            k.barrier()
        rtmp = sb("rtmp", [64, 512], F32)
        rope_inplace(k, Z[:, 0:1024].rearrange("p (g d) -> p g d", d=64), Zb, c["coss"][:, :], c["sins"][:, :], rtmp, gb, 64, 16)
        k.op("pool", lambda E: E.dma_start(out=dr["ks_out"][:, :], in_=Z[:, 512:1024]), r=[Zb], dsem="so2")
        k.op("pool", lambda E: E.dma_start(out=dr["vs_out"][:, :], in_=Z[:, 1024:1536]), r=[Zb], dsem="so2")
        if STOP <= 1:
            k.barrier()
            return smp
        qkb = sb("qkb", [64, 1024], BF16)
        Vhs = sb("Vhs", [64, 4, 128], BF16)
        tpb_ = ps("tpb", [128, 8, 128], BF16)
        ACT(lambda E: E.activation(out=qkb[:, 0:512], in_=Z[:, 0:512], func=AF.Copy, scale=0.125), [Zb], [gb])
        DVE(lambda E: E.tensor_copy(out=qkb[:, 512:1024], in_=Z[:, 512:1024]), [Zb], [gb])
        DVE(lambda E: E.memset(Vn[:, :, 128:130], 1.0), [], [gb])
        DVE(lambda E: E.tensor_copy(out=Vn[:, :, 0:128], in_=Z[:, 1024:1536].rearrange("p (h e) -> p h e", e=128)), [Zb], [gb])
        DVE(lambda E: E.tensor_copy(out=Vhs[:], in_=Z[:, 2560:3072].rearrange("p (h e) -> p h e", e=128)), [Zb], [gb])
        for i8 in range(8):
            PE(lambda E: E.transpose(out=tpb_[:, i8, 0:64], in_=qkb[:, i8 * 128:(i8 + 1) * 128], identity=c["ident_b"][0:64, 0:64]),
               [gb, cb], [gb])
        ACT(lambda E: E.activation(out=qkT[:], in_=tpb_[:, :, 0:64], func=AF.Copy), [gb], [gb])
        if STOP <= 1.3:
            k.barrier()
            return smp
        pf = ZT
        hs = {n_: sb("h_" + n_, [128, 4, 64], F32) for n_ in ("F", "G", "KT", "QS", "SG", "GC", "E1", "E", "D")}
        hbf = {n_: sb("hb_" + n_, [128, 4, 64], BF16) for n_ in ("Qb", "Kh", "Kb")}
        ACT(lambda E: E.activation(out=hs["F"][:], in_=pf[1][:], func=AF.Sigmoid), [gb, ZTb], [gb])
        ACT(lambda E: E.activation(out=hs["QS"][:], in_=pf[0][:], func=AF.Silu), [gb, ZTb], [gb])
        ACT(lambda E: E.activation(out=hs["SG"][:], in_=pf[2][:], func=AF.Silu), [gb, ZTb], [gb])
        for h in range(4):
            DVE(lambda E: E.tensor_scalar(out=hs["F"][:, h, :], in0=hs["F"][:, h, :], scalar1=c["oml_a"][:, h:h + 1],
                                          scalar2=c["lb_a"][:, h:h + 1], op0=ALU.mult, op1=ALU.add), [gb, cb], [gb])
        ACT(lambda E: E.activation(out=hs["GC"][:], in_=hs["F"][:], func=AF.Ln), [gb], [gb])
        DVE(lambda E: E.tensor_scalar(out=hs["KT"][:], in0=hs["F"][:], scalar1=-1.0, scalar2=1.0, op0=ALU.mult, op1=ALU.add), [gb], [gb])
        GC4 = hs["GC"][:].rearrange("p h (s t) -> p (h s) t", t=4)
        for t_ in range(1, 4):
            DVE(lambda E: E.tensor_tensor(out=GC4[:, :, t_:t_ + 1], in0=GC4[:, :, t_:t_ + 1], in1=GC4[:, :, t_ - 1:t_], op=ALU.add), [gb], [gb])
        ACT(lambda E: E.activation(out=hs["E1"][:], in_=hs["GC"][:], func=AF.Exp), [gb], [gb])
        DVE(lambda E: E.tensor_tensor(out=hbf["Qb"][:], in0=hs["QS"][:], in1=hs["E1"][:], op=ALU.mult), [gb], [gb])
        ACT(lambda E: E.activation(out=hs["E"][:], in_=hs["GC"][:], func=AF.Exp, scale=-1.0), [gb], [gb])
        DVE(lambda E: E.tensor_tensor(out=hbf["Kh"][:], in0=hs["KT"][:], in1=hs["E"][:], op=ALU.mult), [gb], [gb])
        D4 = hs["D"][:].rearrange("p h (s t) -> p (h s) t", t=4)
        DVE(lambda E: E.tensor_tensor(out=D4, in0=GC4[:, :, 3:4].to_broadcast([128, 64, 4]), in1=GC4, op=ALU.subtract), [gb], [gb])
        ACT(lambda E: E.activation(out=hs["E"][:], in_=hs["D"][:], func=AF.Exp), [gb], [gb])
        DVE(lambda E: E.tensor_tensor(out=hbf["Kb"][:], in0=hs["KT"][:], in1=hs["E"][:], op=ALU.mult), [gb], [gb])
        KbTok = sb("KbTok", [64, 4, 128], BF16)
        KbSel = sb("KbSel", [64, 4, 16, 128], BF16)
        for h in range(4):
            PE(lambda E: E.transpose(out=tpb_[0:64, h, :], in_=hbf["Kb"][:, h, :], identity=c["ident_b"][:]), [gb, cb], [gb])
        ACT(lambda E: E.activation(out=KbTok[:], in_=tpb_[0:64, 0:4, :], func=AF.Copy), [gb], [gb])
        for h in range(4):
            DVE(lambda E: E.tensor_tensor(out=KbSel[:, h, :, :], in0=KbTok[:, h, :].unsqueeze(1).to_broadcast([64, 16, 128]),
                                           in1=c["sel"][:].unsqueeze(2).to_broadcast([64, 16, 128]), op=ALU.mult), [gb, cb], [gb])
        pAs = ps("pAs", [128, 4, 64], F32)
        Am = sb("Am", [64, 4, 64], BF16)
        for h in range(4):
            PE(lambda E: E.matmul(out=pAs[0:64, h, :], lhsT=hbf["Kh"][:, h, :], rhs=hbf["Qb"][:, h, :], start=True, stop=True), [gb], [gb])
        DVE(lambda E: E.tensor_tensor(out=Am[:], in0=pAs[0:64, :, :], in1=c["bmask_f"][:].unsqueeze(1).to_broadcast([64, 4, 64]),
                                      op=ALU.mult), [gb, cb], [gb])
        if STOP <= 1.5:
            k.barrier()
            return smp
        pO = ps("pO", [128, 4, 64], F32)
        pOb = Buf()
        pUs = [ps(f"pUs{i}", [128, 4, 128], F32) for i in range(2)]
        pUsb = bufs(2)
        S0 = [sb(f"S0_{i}", [128, 4, 128], F32) for i in range(2)]
        S0b = bufs(2)
        S0bf = [sb(f"S0bf_{i}", [128, 4, 128], BF16) for i in range(2)]
        S0bfb = bufs(2)
        Sn = [sb(f"Sn_{i}", [128, 4, 128], F32) for i in range(2)]
        Snb = bufs(2)
        pOi = ps("pOi", [128, 4, 64], F32)
        Oin = sb("Oin", [128, 4, 64], F32)
        for h in range(4):
            PE(lambda E: E.matmul(out=pOi[:, h, :], lhsT=Vhs[:, h, :], rhs=Am[:, h, :], start=True, stop=True), [gb], [gb])
        ACT(lambda E: E.activation(out=Oin[:], in_=pOi[:], func=AF.Copy), [gb], [gb])
        for s_ in range(16):
            sl = s_ % 2
            k.op("sp", lambda E: E.dma_start(out=S0[sl][:], in_=dr["state"][s_].rearrange("h k v -> k h v")), w=[S0b[sl]], dsem=f"st{sl}")
            ACT(lambda E: E.activation(out=S0bf[sl][:], in_=S0[sl][:], func=AF.Copy), [S0b[sl]], [S0bfb[sl]])
            for h in range(4):
                PE(lambda E: E.matmul(out=pO[:, h, s_ * 4:(s_ + 1) * 4], lhsT=S0bf[sl][:, h, :], rhs=hbf["Qb"][:, h, s_ * 4:(s_ + 1) * 4],
                                      start=True, stop=True), [S0bfb[sl], gb], [pOb])
            for h in range(4):
                PE(lambda E: E.matmul(out=pUs[sl][:, h, :], lhsT=KbSel[:, h, s_, :], rhs=Vhs[:, h, :], start=True, stop=True),
                   [gb], [pUsb[sl]])
            for h in range(4):
                DVE(lambda E: E.scalar_tensor_tensor(out=Sn[sl][:, h, :], in0=S0[sl][:, h, :], scalar=hs["E1"][:, h, s_ * 4 + 3:s_ * 4 + 4],
                                                     in1=pUs[sl][:, h, :], op0=ALU.mult, op1=ALU.add), [S0b[sl], pUsb[sl], gb], [Snb[sl]])
            k.op("pool", lambda E: E.dma_start(out=dr["ss_out"][s_].rearrange("h k v -> k h v"), in_=Sn[sl][:]), r=[Snb[sl]], dsem=f"sso{sl}")
        if STOP <= 1.7:
            k.barrier()
            return smp
        SQ = sb("SQ", [128, 4, 64], BF16)
        pSs = pAs[:, 0, 0:4]
        DVE(lambda E: E.tensor_tensor(out=Oin[:], in0=Oin[:], in1=pO[:], op=ALU.add), [pOb, gb], [gb])
        ACT(lambda E: E.activation(out=SQ[:], in_=Oin[:], func=AF.Square), [gb], [gb])
        for h in range(4):
            PE(lambda E: E.matmul(out=pSs[0:64, h:h + 1], lhsT=SQ[:, h, :], rhs=c["ones_b"][:, 0:1], start=True, stop=True), [gb, cb], [gb])
        sc = sb("sc", [64, 4], F32)
        DVE(lambda E: E.tensor_reduce(out=sc[:, 0:1], in_=pSs[0:64, 0:4], axis=AX.X, op=ALU.add), [gb], [gb])
        rsqrt_col(k, sc[:, 0:1], gb, smp["rstdh"][0:64, 0:1], mb, 512.0, EPS)
        for h in range(4):
            DVE(lambda E: E.scalar_tensor_tensor(out=smp["mixT"][:, 4 + h, :], in0=Oin[:, h, :], scalar=c["hn_a"][:, h:h + 1],
                                                 in1=hs["SG"][:, h, :], op0=ALU.mult, op1=ALU.mult), [pOb, gb, cb], [mb])
        k.barrier()
    with ExitStack() as sa:
        def sb(name, shape, dt):
            return sa.enter_context(nc.sbuf_tensor("pa_" + name, shape, dt))

        def ps(name, shape, dt):
            return sa.enter_context(nc.psum_tensor("ppa_" + name, shape, dt))
        ACT = lambda fn, r, w: k.op("act", fn, r=r, w=w)
        DVE = lambda fn, r, w: k.op("dve", fn, r=r, w=w)
        POOL = lambda fn, r, w: k.op("pool", fn, r=r, w=w)
        PE = lambda fn, r, w: k.op("pe", fn, r=r, w=w)
        gb2 = Buf()
        pScm = [ps(f"pSc{i}", [128, 512], F32) for i in range(2)]
        pScb = Buf()
        pNm = [pScm[i][0:64, 0:256].rearrange("p (h t) -> p h t", t=64) for i in range(2)]
        PnT = sb("PnT", [64, 8, 64], BF16)
        Pn32 = sb("Pn32", [64, 8, 64], F32)
        for h in range(4):
            for cm in range(2):
                PE(lambda E: E.matmul(out=pNm[cm][:, h, :], lhsT=qkT[cm * 64:(cm + 1) * 64, 4 + h, :],
                                      rhs=qkT[cm * 64:(cm + 1) * 64, h, :], start=True, stop=True), [gb], [pScb])
        Pn4 = Pn32[:].rearrange("p (h c) t -> p h c t", c=2)
        for cm in range(2):
            ACT(lambda E: E.activation(out=Pn4[:, :, cm, :], in_=pNm[cm], func=AF.Exp), [pScb], [gb2])
        DVE(lambda E: E.tensor_tensor(out=PnT[:], in0=Pn32[:], in1=c["bmask_f"][:].unsqueeze(1).to_broadcast([64, 8, 64]), op=ALU.mult),
            [gb2, cb], [gb2])
        ptb = sb("ptb", [128, 256], I32)
        idxp = sb("idxp", [128, 256], I32)
        if STOP <= 2:
            k.barrier()
            return smp
        k.op("sp", lambda E: E.dma_start(out=ptb[:], in_=dr["pt"].partition_broadcast(128)), w=[gb2], dsem="spt")
        DVE(lambda E: E.tensor_scalar(out=idxp[:], in0=ptb[:], scalar1=128.0, scalar2=c["iota_p"][:, 0:1], op0=ALU.mult, op1=ALU.add),
            [gb2, cb], [gb2])
        NPG = 4
        OA = sb("OA", [4, 16, 8, 130], F32)
        OAb = Buf()
        pOs = ps("pOs", [128, 3, 512], F32)
        pOsb = Buf()
        lp = ExitStack()
        _sb_outer = sb
        sb = lambda name, shape, dt: lp.enter_context(nc.sbuf_tensor("pl_" + name, shape, dt))
        Kpg = [sb(f"Kpg{i}", [128, 512], F32) for i in range(NPG)]
        Kpgb = bufs(NPG)
        Vpg = [sb(f"Vpg{i}", [128, 512], F32) for i in range(NPG)]
        Vpgb = bufs(NPG)
        KTs = [sb(f"KTs{i}", [128, 4, 2048], BF16) for i in range(2)]
        KTsb = bufs(2)
        Vs = [sb(f"Vs{i}", [128, 16, 4, 130], BF16) for i in range(2)]
        Vsb = bufs(2)
        for i in range(2):
            POOL(lambda E: E.memset(Vs[i][:, :, :, 128:130], 1.0), [], [Vsb[i]])
        ptk = [ps("ptk0", [128, 4, 128], BF16)] * 2
        ptkb = [Buf()] * 2
        Kbf = [sb(f"Kbf{i}", [128, 512], BF16) for i in range(2)]
        Kbfb = bufs(2)
        Ps = [sb(f"Ps{i}", [128, 512], BF16) for i in range(2)]
        Psb = bufs(2)
        pgc = 0
        for s_ in range(16 if STOP > 3 else 0):
            sl = s_ % 2
            for j in range(NPAGE):
                g_ = pgc % NPG
                pgc += 1
                icol = idxp[:, s_ * 16 + j:s_ * 16 + j + 1]
                k.op("pool", lambda E: E.indirect_dma_start(out=Kpg[g_][:], out_offset=None, in_=dr["cache_k"][:, :],
                                                            in_offset=bass.IndirectOffsetOnAxis(ap=icol, axis=0)),
                     r=[gb2], w=[Kpgb[g_]], dsem=f"kg{g_}")
                k.op("pool", lambda E: E.indirect_dma_start(out=Vpg[g_][:], out_offset=None, in_=dr["cache_v"][:, :],
                                                            in_offset=bass.IndirectOffsetOnAxis(ap=icol, axis=0)),
                     r=[gb2], w=[Vpgb[g_]], dsem=f"vg{g_}")
                tk = j % 2
                DVE(lambda E: E.tensor_copy(out=Kbf[tk][:], in_=Kpg[g_][:]), [Kpgb[g_]], [Kbfb[tk]])
                for h in range(4):
                    PE(lambda E: E.transpose(out=ptk[tk][:, h, :], in_=Kbf[tk][:, h * 128:(h + 1) * 128], identity=c["ident_b"][:]),
                       [Kbfb[tk], cb], [ptkb[tk]])
                ACT(lambda E: E.activation(out=KTs[sl][:, :, j * 128:(j + 1) * 128], in_=ptk[tk][:], func=AF.Copy), [ptkb[tk]], [KTsb[sl]])
                ACT(lambda E: E.activation(out=Vs[sl][:, j, :, 0:128], in_=Vpg[g_][:].rearrange("p (h e) -> p h e", e=128), func=AF.Copy),
                    [Vpgb[g_]], [Vsb[sl]])
            for j in range(NPAGE):
                for h in range(4):
                    for cm in range(2):
                        col = (j * 4 + h) * 4
                        PE(lambda E: E.matmul(out=pScm[cm][:, col:col + 4], lhsT=KTs[sl][cm * 64:(cm + 1) * 64, h, j * 128:(j + 1) * 128],
                                              rhs=qkT[cm * 64:(cm + 1) * 64, h, s_ * 4:(s_ + 1) * 4], start=True, stop=True),
                           [KTsb[sl], gb], [pScb])
            for cm in range(2):
                ACT(lambda E: E.activation(out=Ps[sl][:, cm * 256:(cm + 1) * 256], in_=pScm[cm][:, 0:256], func=AF.Exp), [pScb], [Psb[sl]])
            for h in range(4):
                for cm in range(2):
                    hc = h * 2 + cm
                    bk, off = divmod(hc, 3)
                    for j in range(NPAGE):
                        col = cm * 256 + (j * 4 + h) * 4
                        PE(lambda E: E.matmul(out=pOs[0:4, bk, off * 130:off * 130 + 129], lhsT=Ps[sl][:, col:col + 4],
                                              rhs=Vs[sl][:, j, h, 0:129], start=(j == 0), stop=False), [Psb[sl], Vsb[sl]], [pOsb])
                    PE(lambda E: E.matmul(out=pOs[0:4, bk, off * 130:off * 130 + 129], lhsT=PnT[:, hc, s_ * 4:(s_ + 1) * 4],
                                          rhs=Vn[:, h, 0:129], start=False, stop=True), [gb2, gb], [pOsb])
            for bk in range(3):
                ncol = 3 if bk < 2 else 2
                DVE(lambda E: E.tensor_copy(out=OA[:, s_, bk * 3:bk * 3 + ncol, 0:129],
                                            in_=pOs[0:4, bk, 0:ncol * 130].rearrange("p (a b) -> p a b", b=130)[:, :, 0:129]), [pOsb], [OAb])
        k.barrier()
        lp.close()
        if STOP <= 4:
            return smp
        sb = _sb_outer
        rden = sb("rden", [4, 16, 8], F32)
        A1 = sb("A1", [4, 64, 128], F32)
        A2 = sb("A2", [4, 64, 128], F32)
        ss = sb("ss", [4, 64], F32)
        oan = sb("oan", [4, 64, 128], BF16)
        OA5 = OA[:].rearrange("p s (h c) e -> p (s h) c e", c=2)
        rd3 = rden[:].rearrange("p s (h c) -> p (s h) c", c=2)
        DVE(lambda E: E.reciprocal(out=rden[:], in_=OA[:, :, :, 128]), [OAb], [gb2])
        DVE(lambda E: E.tensor_scalar(out=rd3[:, :, 1:2], in0=rd3[:, :, 1:2], scalar1=c["neglam"][0:4, :], scalar2=None, op0=ALU.mult),
            [gb2, cb], [gb2])
        DVE(lambda E: E.tensor_tensor(out=A1[:], in0=OA5[:, :, 0, 0:128], in1=rd3[:, :, 0:1].to_broadcast([4, 64, 128]), op=ALU.mult),
            [OAb, gb2], [gb2])
        DVE(lambda E: E.tensor_tensor(out=A2[:], in0=OA5[:, :, 1, 0:128], in1=rd3[:, :, 1:2].to_broadcast([4, 64, 128]), op=ALU.mult),
            [OAb, gb2], [gb2])
        DVE(lambda E: E.tensor_tensor(out=A1[:], in0=A1[:], in1=A2[:], op=ALU.add), [gb2], [gb2])
        DVE(lambda E: E.tensor_tensor(out=A2[:], in0=A1[:], in1=A1[:], op=ALU.mult), [gb2], [gb2])
        DVE(lambda E: E.tensor_reduce(out=ss[:], in_=A2[:], axis=AX.X, op=ALU.add), [gb2], [gb2])
        rsqrt_col(k, ss[:], gb2, ss[:], gb2, 128.0, SUBLN_EPS)
        DVE(lambda E: E.tensor_tensor(out=oan[:], in0=A1[:], in1=ss[:].unsqueeze(2).to_broadcast([4, 64, 128]), op=ALU.mult), [gb2], [gb2])
        pT = ptk[0][:, :, 0:64]
        for s_ in range(16):
            for h in range(4):
                PE(lambda E: E.transpose(out=pT[:, h, s_ * 4:(s_ + 1) * 4], in_=oan[0:4, s_ * 4 + h, :], identity=c["ident_b"][0:4, 0:4]),
                   [gb2, cb], [gb2])
        ACT(lambda E: E.activation(out=smp["mixT"][:, 0:4, :], in_=pT, func=AF.Copy), [gb2], [mb])
        k.barrier()
    return smp


def build(L=8192, sample=True, debug=0, pool_rows=2560 * 128):
    nc = bass.Bass("TRN2", target_bir_lowering=False)
    NT = L // 128
    T2 = L // 4
    NT2 = T2 // 128
    dr = {}

    def din(name, shape, dt=F32):
        dr[name] = nc.dram_tensor(name, shape, dt, kind="ExternalInput").ap()

    def dout(name, shape, dt=F32):
        dr[name] = nc.dram_tensor(name, shape, dt, kind="ExternalOutput").ap()

    din("xp", [L if debug != 7 else 128, D])
    din("xp2", [T2, D])
    din("xs", [64, D])
    din("w_in_h", [D, 896])
    din("w_in", [D, NIN])
    din("w_out", [D, D])
    din("w_gate", [NFF, 128, 8, 128])
    din("w_up", [NFF, 128, 8, 128])
    din("w_down", [DFF, D])
    din("cache_k", [pool_rows, 512])
    din("cache_v", [pool_rows, 512])
    din("state", [16, 4, 128, 128])
    din("pt", [1, 256], I32)
    din("idx_mix", [128, 8], I32)
    din("idx_ssq", [128, 4], I32)
    for nm, shp in (("c_ident", [128, 128]), ("c_mask", [128, 128]), ("c_mask64", [128, 64]), ("c_bmask", [64, 64]),
                    ("c_sel", [64, 16]), ("c_cosp", [128, NT, 8]), ("c_sinp", [128, NT, 8]), ("c_coss", [64, 8]),
                    ("c_sins", [64, 8]), ("c_iota", [128, 1]), ("an", [128, 8]), ("fnc", [128, 8]), ("lamv", [1, 256]),
                    ("lbl_h", [128, 2]), ("lbl_a", [128, 8]), ("hn_h", [128, 1]), ("hn_a", [128, 4]), ("subln", [128, 1]),
                    ("fin", [1, 1024])):
        din(nm, shp)
    dout("y_p", [T2, D])
    dout("y_s", [64, D])
    dout("k_out", [L, 128])
    dout("v_out", [L, 128])
    dout("s_out", [128, 128])
    dout("ks_out", [64, 512])
    dout("vs_out", [64, 512])
    dout("ss_out", [16, 4, 128, 128])
    dr["xch"] = nc.dram_tensor("xch", [4 * 256, T2], BF16).ap()
    dr["xssq"] = nc.dram_tensor("xssq", [4 * 128, NT2], F32).ap()
    dr["xg"] = nc.dram_tensor("xg", [4 * 4 * 256, T2], BF16).ap()
    dr["xgs"] = nc.dram_tensor("xgs", [4 * 4 * 128, NT2], F32).ap()
    dr["xgb"] = Buf()
    dr["xgsb"] = Buf()
    with ExitStack() as es:
        block = es.enter_context(nc.Block())

        def body(_):
            k = KB(nc, es)
            c = load_consts(k, es, dr, L, True)
            if debug == 7:
                smp = phaseS(k, c, dr, es)
                k.op("pool", lambda E: E.dma_start(out=dr["y_s"][:, 0:512].bitcast(BF16).rearrange("t (a b) -> t a b", a=8)[:, :, 0:128],
                                                   in_=smp["mixT"][:].rearrange("p a b -> p a b")), r=[smp["b"]], dsem="dbg")
                k.barrier()
                return
            phase1(k, c, dr, L)
            k.barrier()
            groups = [[0, 1, 2, 3], [4, 5, 6, 7]]
            if debug != 8:
                for rr_ in range(4):
                    k.op("pool", lambda E: E.collective_compute(
                        "AllGather", ALU.bypass, replica_groups=groups,
                        ins=[dr["xch"][rr_ * 256:(rr_ + 1) * 256, :]], outs=[dr["xg"][rr_ * 1024:(rr_ + 1) * 1024, :]]),
                        w=[dr["xgb"]], dsem="cc1", inc=CC_INC)
                k.op("pool", lambda E: E.collective_compute("AllGather", ALU.bypass, replica_groups=groups,
                                                            ins=[dr["xssq"][:, :]], outs=[dr["xgs"][:, :]]),
                     w=[dr["xgsb"]], dsem="cc2", inc=CC_INC)
            smp = phaseS(k, c, dr, es) if sample else None
            phase2(k, c, dr, L, smp)
            k.barrier()

        block.sync(body)
    return nc


CC_INC = 1

def host_consts(L):
    NT = L // 128
    cst = {}
    cst["c_ident"] = np.eye(128, dtype=np.float32)
    p = np.arange(128)
    cst["c_mask"] = (p[:, None] <= p[None, :]).astype(np.float32)
    cst["c_mask64"] = ((p[:, None] % 64) <= np.arange(64)[None, :]).astype(np.float32)
    t = np.arange(64)
    cst["c_bmask"] = ((t[:, None] // 4 == t[None, :] // 4) & (t[:, None] <= t[None, :])).astype(np.float32)
    cst["c_sel"] = (t[:, None] // 4 == np.arange(16)[None, :]).astype(np.float32)
    half = 8
    inv_freq = (1.0 / (np.float32(ROPE_THETA) ** (np.arange(half, dtype=np.float32) * np.float32(2.0) / np.float32(16)))).astype(np.float32)
    pos = np.arange(L, dtype=np.float32)
    ang = (pos[:, None] * inv_freq[None, :]).astype(np.float32)
    cst["c_cosp"] = np.ascontiguousarray(np.cos(ang).astype(np.float32).reshape(NT, 128, 8).transpose(1, 0, 2))
    cst["c_sinp"] = np.ascontiguousarray(np.sin(ang).astype(np.float32).reshape(NT, 128, 8).transpose(1, 0, 2))
    poss = (PAST + (np.arange(64) % 4)).astype(np.float32)
    angs = (poss[:, None] * inv_freq[None, :]).astype(np.float32)
    cst["c_coss"] = np.cos(angs).astype(np.float32)
    cst["c_sins"] = np.sin(angs).astype(np.float32)
    cst["c_iota"] = np.arange(128, dtype=np.float32).reshape(128, 1)
    return cst


def make_in_maps(inp, L):
    cst = host_consts(L)
    f = lambda a: np.ascontiguousarray(np.asarray(a, dtype=np.float32))
    w_in = np.asarray(inp["w_in"])[0]
    segs = {"qa": 0, "ka": 1, "va": 2, "qh": 3, "fh": 4, "ih": 5, "gh": 6}
    order = ["qa", "ka", "va", "ih", "qh", "fh", "gh"]
    lb = np.asarray(inp["hgrn_lb_logits"])
    maps = []
    for cidx in range(8):
        b, hd = divmod(cidx, 4)
        m = dict(cst)
        m["xp"] = f(np.asarray(inp["x_prompt"])[b, :L])
        m["w_in_h"] = f(np.concatenate([w_in[:, segs[s] * 512 + hd * 128: segs[s] * 512 + (hd + 1) * 128] for s in order], axis=1))
        m["an"] = f(np.asarray(inp["attn_norm"])[0].reshape(8, 128).T)
        m["fnc"] = f(np.asarray(inp["ffn_norm"])[0].reshape(8, 128).T)
        m["lamv"] = f(np.concatenate([np.asarray(inp[n])[0] for n in ("lambda_q1", "lambda_k1", "lambda_q2", "lambda_k2")]).reshape(1, 256))
        m["lbl_h"] = f(lb[:, hd * 128:(hd + 1) * 128].T)
        m["lbl_a"] = f(lb.reshape(2, 4, 128).transpose(2, 0, 1).reshape(128, 8))
        hn = np.asarray(inp["hgrn_norm"])[0]
        m["hn_h"] = f(hn[hd * 128:(hd + 1) * 128].reshape(128, 1))
        m["hn_a"] = f(hn.reshape(4, 128).T)
        m["subln"] = f(np.asarray(inp["subln_w"])[0].reshape(128, 1))
        m["fin"] = f(np.asarray(inp["final_norm"]).reshape(1, 1024))
        T2 = L // 4
        m["xp2"] = f(np.asarray(inp["x_prompt"])[b, hd * T2:(hd + 1) * T2])
        m["xs"] = f(np.asarray(inp["x_sample"])[cidx * 16:(cidx + 1) * 16].reshape(64, D))
        m["w_in"] = f(w_in)
        m["w_out"] = f(np.asarray(inp["w_out"])[0])
        m["w_gate"] = f(np.asarray(inp["w_gate"])[0].reshape(8, 128, NFF, 128).transpose(2, 1, 0, 3))
        m["w_up"] = f(np.asarray(inp["w_up"])[0].reshape(8, 128, NFF, 128).transpose(2, 1, 0, 3))
        m["w_down"] = f(np.asarray(inp["w_down"])[0])
        m["cache_k"] = np.asarray(inp["cache_k"], dtype=np.float32).reshape(2560 * 128, 512)
        m["cache_v"] = np.asarray(inp["cache_v"], dtype=np.float32).reshape(2560 * 128, 512)
        m["state"] = f(np.asarray(inp["state_hgrn"])[0, cidx * 16:(cidx + 1) * 16])
        m["pt"] = np.ascontiguousarray(np.asarray(inp["page_table"], dtype=np.int32)[cidx * 16:(cidx + 1) * 16].reshape(1, 256))
        pp = np.arange(128, dtype=np.int32)
        r_ = hd
        m["idx_mix"] = np.ascontiguousarray(np.stack([r_ * 1024 + (kc % 4) * 256 + (kc // 4) * 128 + pp for kc in range(8)], axis=1).astype(np.int32))
        m["idx_ssq"] = np.ascontiguousarray(np.stack([h * 512 + r_ * 128 + pp for h in range(4)], axis=1).astype(np.int32))
        maps.append(m)
    return maps


_NC_CACHE = {}


def kernel(**inp):
    L = 8192
    if L not in _NC_CACHE:
        _NC_CACHE[L] = build(L=L)
    nc = _NC_CACHE[L]
    maps = make_in_maps(inp, L)
    res = run_bass_kernel_spmd(nc, maps, core_ids=list(range(8)))
    return assemble(res.results, L)


def assemble(R, L):
    T2 = L // 4
    y_p = np.zeros((2, L, D), np.float32)
    y_s = np.zeros((128, 4, D), np.float32)
    k_p = np.zeros((1, 2, L, 4, 2, 64), np.float32)
    v_p = np.zeros((1, 2, L, 4, 128), np.float32)
    s_p = np.zeros((1, 2, 4, 128, 128), np.float32)
    k_s = np.zeros((1, 128, 4, 4, 2, 64), np.float32)
    v_s = np.zeros((1, 128, 4, 4, 128), np.float32)
    s_s = np.zeros((1, 128, 4, 128, 128), np.float32)
    for cidx in range(8):
        b, hd = divmod(cidx, 4)
        r = R[cidx]
        y_p[b, hd * T2:(hd + 1) * T2] = np.asarray(r["y_p"])
        y_s[cidx * 16:(cidx + 1) * 16] = np.asarray(r["y_s"]).reshape(16, 4, D)
        k_p[0, b, :, hd] = np.asarray(r["k_out"]).reshape(L, 2, 64)
        v_p[0, b, :, hd] = np.asarray(r["v_out"])
        s_p[0, b, hd] = np.asarray(r["s_out"])
        k_s[0, cidx * 16:(cidx + 1) * 16] = np.asarray(r["ks_out"]).reshape(16, 4, 4, 2, 64)
        v_s[0, cidx * 16:(cidx + 1) * 16] = np.asarray(r["vs_out"]).reshape(16, 4, 4, 128)
        s_s[0, cidx * 16:(cidx + 1) * 16] = np.asarray(r["ss_out"])
    return (y_p, y_s, k_p, v_p, s_p, k_s, v_s, s_s)
```

```python
import numpy as np
from contextlib import ExitStack
import concourse.bass as bass
import concourse.mybir as mybir
from concourse.bass_utils import run_bass_kernel_spmd

F32 = mybir.dt.float32
BF16 = mybir.dt.bfloat16
I32 = mybir.dt.int32
AF = mybir.ActivationFunctionType
ALU = mybir.AluOpType
AX = mybir.AxisListType

D = 1024
NIN = 3584
DFF = 2816
NFF = DFF // 128
ROPE_THETA = 500000.0
PAST = 2048
NPAGE = 16
EPS = 1e-6
SUBLN_EPS = 1e-5
LAM_INIT = 0.2
STOP = 99


class Buf:
    __slots__ = ("w", "r")

    def __init__(self):
        self.w = None
        self.r = {}


def bufs(n):
    return [Buf() for _ in range(n)]


class KB:
    def __init__(self, nc, es):
        self.nc = nc
        self.es = es
        self.eng = dict(pe=nc.tensor, act=nc.scalar, dve=nc.vector, pool=nc.gpsimd, sp=nc.sync)
        self.sem = {}
        self.cnt = {}
        self.seen = {}
        for k in self.eng:
            self._mksem(k)

    def _mksem(self, k):
        self.sem[k] = self.es.enter_context(self.nc.semaphore("s_" + k))
        self.cnt[k] = 0

    def op(self, e, fn, r=(), w=(), dsem=None, inc=16):
        need = {}

        def add(t):
            if t is None:
                return
            kk, v = t
            if need.get(kk, 0) < v:
                need[kk] = v

        for b in r:
            add(b.w)
        for b in w:
            add(b.w)
            for kk, v in b.r.items():
                add((kk, v))
        E = self.eng[e]
        for kk, v in need.items():
            if kk == e and e == "pe":
                continue
            if self.seen.get((e, kk), 0) >= v:
                continue
            E.wait_ge(self.sem[kk], v)
            self.seen[(e, kk)] = v
        inst = fn(E)
        if dsem is not None:
            if dsem not in self.sem:
                self._mksem(dsem)
            self.cnt[dsem] += inc
            inst.then_inc(self.sem[dsem], inc)
            tag = (dsem, self.cnt[dsem])
        else:
            self.cnt[e] += 1
            inst.then_inc(self.sem[e], 1)
            tag = (e, self.cnt[e])
        for b in r:
            if b.r.get(tag[0], 0) < tag[1]:
                b.r[tag[0]] = tag[1]
        for b in w:
            b.w = tag
            b.r = {}
        return inst

    def barrier(self, engines=("pe", "act", "dve", "pool", "sp")):
        for e in engines:
            E = self.eng[e]
            for kk, v in self.cnt.items():
                if v == 0 or kk == e:
                    continue
                if self.seen.get((e, kk), 0) >= v:
                    continue
                E.wait_ge(self.sem[kk], v)
                self.seen[(e, kk)] = v


def rsqrt_col(k, src, srcb, dst, dstb, n, eps):
    k.op("act", lambda E: E.activation(out=dst, in_=src, func=AF.Ln, scale=1.0 / n, bias=k.epsc[eps][0:src.shape[0], :]),
         r=[srcb], w=[dstb])
    k.op("act", lambda E: E.activation(out=dst, in_=dst, func=AF.Exp, scale=-0.5), r=[dstb], w=[dstb])


def load_consts(k, es, dr, L, do_sample):
    nc = k.nc
    NT = L // 128
    c = {}

    def sb(name, shape, dt):
        return es.enter_context(nc.sbuf_tensor("cs_" + name, shape, dt))

    cb = Buf()
    c["b"] = cb
    stage = sb("c_stage", [128, 256], F32)
    c["ident_f"] = sb("ident_f", [128, 128], F32)
    c["ident_b"] = sb("ident_b", [128, 128], BF16)
    c["mask_b"] = sb("mask_b", [128, 128], BF16)
    c["mask64_f"] = sb("mask64_f", [128, 64], F32)
    c["bmask_f"] = sb("bmask_f", [64, 64], F32)
    c["bmask_b"] = sb("bmask_b", [64, 64], BF16)
    c["sel"] = sb("sel", [64, 16], F32)
    c["ones_f"] = sb("ones_f", [128, 512], F32)
    c["ones_b"] = sb("ones_b", [128, 1], BF16)
    c["cosp"] = sb("cosp", [128, NT, 8], F32)
    c["sinp"] = sb("sinp", [128, NT, 8], F32)
    c["coss"] = sb("coss", [64, 8], F32)
    c["sins"] = sb("sins", [64, 8], F32)
    c["an"] = sb("an", [128, 8], F32)
    c["fnc"] = sb("fnc", [128, 8], F32)
    c["lamv"] = sb("lamv", [128, 256], F32)
    c["lbl_h"] = sb("lbl_h", [128, 2], F32)
    c["lbl_a"] = sb("lbl_a", [128, 8], F32)
    c["hn_h"] = sb("hn_h", [128, 1], F32)
    c["hn_a"] = sb("hn_a", [128, 4], F32)
    c["subln"] = sb("subln", [128, 1], F32)
    c["fin"] = sb("fin", [128, 1024], F32)
    c["iota_p"] = sb("iota_p", [128, 1], F32)
    c["small"] = sb("c_small", [128, 32], F32)
    epst = sb("epst", [128, 2], F32)

    def ld(dst, src):
        k.op("sp", lambda E: E.dma_start(out=dst, in_=src), w=[cb], dsem="cst")

    ld(c["ident_f"][:], dr["c_ident"])
    ld(stage[:, 0:128], dr["c_mask"])
    ld(c["mask64_f"][:], dr["c_mask64"])
    ld(c["bmask_f"][:], dr["c_bmask"])
    ld(c["sel"][:], dr["c_sel"])
    ld(c["cosp"][:], dr["c_cosp"])
    ld(c["sinp"][:], dr["c_sinp"])
    ld(c["coss"][:], dr["c_coss"])
    ld(c["sins"][:], dr["c_sins"])
    ld(c["an"][:], dr["an"])
    ld(c["fnc"][:], dr["fnc"])
    ld(c["lamv"][:], dr["lamv"].partition_broadcast(128))
    ld(c["lbl_h"][:], dr["lbl_h"])
    ld(c["lbl_a"][:], dr["lbl_a"])
    ld(c["hn_h"][:], dr["hn_h"])
    ld(c["hn_a"][:], dr["hn_a"])
    ld(c["subln"][:], dr["subln"])
    ld(c["fin"][:], dr["fin"].partition_broadcast(128))
    ld(c["iota_p"][:], dr["c_iota"])
    V = lambda fn: k.op("dve", fn, r=[cb], w=[cb])
    V(lambda E: E.memset(c["ones_f"][:], 1.0))
    V(lambda E: E.memset(c["ones_b"][:], 1.0))
    V(lambda E: E.memset(epst[:, 0:1], EPS))
    V(lambda E: E.memset(epst[:, 1:2], SUBLN_EPS))
    k.epsc = {EPS: epst[:, 0:1], SUBLN_EPS: epst[:, 1:2]}
    V(lambda E: E.tensor_copy(out=c["ident_b"][:], in_=c["ident_f"][:]))
    V(lambda E: E.tensor_copy(out=c["mask_b"][:], in_=stage[:, 0:128]))
    V(lambda E: E.tensor_copy(out=c["bmask_b"][:], in_=c["bmask_f"][:]))
    sm = c["small"]
    lv = c["lamv"]
    V(lambda E: E.scalar_tensor_tensor(out=stage[:, 0:64], in0=lv[:, 0:64], scalar=1.0, in1=lv[:, 64:128],
                                       op0=ALU.mult, op1=ALU.mult, accum_out=sm[:, 0:1]))
    V(lambda E: E.scalar_tensor_tensor(out=stage[:, 0:64], in0=lv[:, 128:192], scalar=1.0, in1=lv[:, 192:256],
                                       op0=ALU.mult, op1=ALU.mult, accum_out=sm[:, 1:2]))
    k.op("act", lambda E: E.activation(out=sm[:, 2:4], in_=sm[:, 0:2], func=AF.Exp), r=[cb], w=[cb])
    V(lambda E: E.tensor_tensor(out=sm[:, 4:5], in0=sm[:, 2:3], in1=sm[:, 3:4], op=ALU.subtract))
    V(lambda E: E.tensor_scalar(out=sm[:, 5:6], in0=sm[:, 4:5], scalar1=LAM_INIT, scalar2=-1.0,
                                op0=ALU.add, op1=ALU.mult))
    c["neglam"] = sm[:, 5:6]
    V(lambda E: E.tensor_tensor(out=sm[:, 6:7], in0=c["lbl_h"][:, 0:1], in1=c["lbl_h"][:, 1:2], op=ALU.subtract))
    V(lambda E: E.tensor_tensor(out=sm[:, 8:12], in0=c["lbl_a"][:, 0:4], in1=c["lbl_a"][:, 4:8], op=ALU.subtract))
    k.op("act", lambda E: E.activation(out=sm[:, 6:7], in_=sm[:, 6:7], func=AF.Sigmoid), r=[cb], w=[cb])
    k.op("act", lambda E: E.activation(out=sm[:, 8:12], in_=sm[:, 8:12], func=AF.Sigmoid), r=[cb], w=[cb])
    V(lambda E: E.tensor_scalar(out=sm[:, 7:8], in0=sm[:, 6:7], scalar1=-1.0, scalar2=1.0, op0=ALU.mult, op1=ALU.add))
    V(lambda E: E.tensor_scalar(out=sm[:, 12:16], in0=sm[:, 8:12], scalar1=-1.0, scalar2=1.0, op0=ALU.mult, op1=ALU.add))
    c["lb_h"] = sm[:, 6:7]
    c["oml_h"] = sm[:, 7:8]
    c["lb_a"] = sm[:, 8:12]
    c["oml_a"] = sm[:, 12:16]
    return c


def rope_inplace(k, v4, vb, cos, sin, tmp, tmpb, npart, ng):
    x1 = v4[:, :, 0:8]
    x2 = v4[:, :, 8:16]
    cb_ = cos.unsqueeze(1).to_broadcast([npart, ng, 8])
    sb_ = sin.unsqueeze(1).to_broadcast([npart, ng, 8])
    t = [tmp[0:npart, j * ng * 8:(j + 1) * ng * 8].rearrange("p (g d) -> p g d", d=8) for j in range(4)]
    V = lambda fn, r, w: k.op("dve", fn, r=r, w=w)
    V(lambda E: E.tensor_tensor(out=t[0], in0=x1, in1=cb_, op=ALU.mult), [vb], [tmpb])
    V(lambda E: E.tensor_tensor(out=t[1], in0=x2, in1=sb_, op=ALU.mult), [vb], [tmpb])
    V(lambda E: E.tensor_tensor(out=t[2], in0=x2, in1=cb_, op=ALU.mult), [vb], [tmpb])
    V(lambda E: E.tensor_tensor(out=t[3], in0=x1, in1=sb_, op=ALU.mult), [vb], [tmpb])
    V(lambda E: E.tensor_tensor(out=x1, in0=t[0], in1=t[1], op=ALU.subtract), [tmpb], [vb])
    V(lambda E: E.tensor_tensor(out=x2, in0=t[2], in1=t[3], op=ALU.add), [tmpb], [vb])


def phase1(k, c, dr, L):
    nc = k.nc
    NT = L // 128
    NG = L // 512
    cb = c["b"]
    with ExitStack() as ph, ExitStack() as ip:
        cur = [ph]

        def sb(name, shape, dt):
            return cur[0].enter_context(nc.sbuf_tensor(name, shape, dt))

        def ps(name, shape, dt):
            return cur[0].enter_context(nc.psum_tensor(name, shape, dt))

        QKT = sb("QKT", [128, 2, L], BF16)
        QKTb = bufs(NT)
        Vaug = sb("Vaug", [128, NT, 130], BF16)
        Vaugb = bufs(NT)
        cur[0] = ip
        Vh = sb("Vh", [128, NT, 128], BF16)
        Vhb = bufs(NT)
        w1 = sb("w1", [128, 8, 896], BF16)
        w1b = Buf()
        SSQ = sb("SSQ", [128, NT], F32)
        SSQb = Buf()
        with ExitStack() as wph:
            wst = [wph.enter_context(nc.sbuf_tensor(f"wst{i}", [128, 896], F32)) for i in range(2)]
            wstb = bufs(2)
            for kc in range(8):
                s = kc % 2
                k.op("sp", lambda E: E.dma_start(out=wst[s][:], in_=dr["w_in_h"][kc * 128:(kc + 1) * 128, :]),
                     w=[wstb[s]], dsem=f"w{s}")
                k.op("dve", lambda E: E.tensor_scalar(out=w1[:, kc, :], in0=wst[s][:], scalar1=c["an"][:, kc:kc + 1],
                                                      scalar2=None, op0=ALU.mult), r=[wstb[s], cb], w=[w1b])
            k.barrier()
        k.op("pool", lambda E: E.memset(Vaug[:, :, 128:130], 1.0), w=Vaugb)
        if STOP <= 1:
            return

        NX = 3
        xs = [sb(f"xs{i}", [128, 1024], F32) for i in range(NX)]
        xsb = bufs(NX)
        junk = sb("junk", [128, 1024], BF16)
        junkb = Buf()
        cols = sb("cols", [128, 16], F32)
        colb = bufs(16)
        xn = [sb(f"xn{i}", [128, 1024], BF16) for i in range(2)]
        xnb = bufs(2)
        hT = [sb(f"hT{i}", [128, 8, 512], BF16) for i in range(2)]
        hTb = [bufs(4) for _ in range(2)]
        qkvf = [sb(f"qkvf{i}", [128, 384], F32) for i in range(2)]
        qkvfb = bufs(2)
        rtmp = sb("rtmp", [128, 128], F32)
        rtmpb = Buf()
        qkb = [sb(f"qkb{i}", [128, 256], BF16) for i in range(2)]
        qkbb = bufs(2)
        tp = [ps(f"tp{i}", [128, 8, 128], BF16) for i in range(2)]
        tpb = bufs(2)
        ztm = [ps(f"ztm{i}", [128, 512], F32) for i in range(2)]
        ztmb = bufs(2)
        tq = ps("tq", [128, 8, 128], BF16)
        tqb = Buf()
        zfm = [ps(f"zfm{i}", [128, 512], F32) for i in range(3)]
        zfmb = bufs(3)
        pOT, pOTb = zfm[0], zfmb[0]
        pmisc, pAb = zfm[1], zfmb[1]
        pSb = pAb
        pU, pUb = zfm[2], zfmb[2]
        hg = {n: sb("hg_" + n, [128, 512], F32) for n in ("F", "G", "KT", "GC", "D", "QS", "SG", "E", "E1")}
        hgb = {n: Buf() for n in hg}
        GS = sb("GS", [128, 8], F32)
        GSb = Buf()
        hb = {n: sb("hb_" + n, [128, 512], BF16) for n in ("Qh", "Kh", "Kb", "Qb", "Y", "SQ")}
        hbb = {n: Buf() for n in hb}
        KbT = sb("KbT", [128, 4, 128], BF16)
        KbTb = Buf()
        Am = [sb(f"Am{i}", [128, 64], BF16) for i in range(2)]
        Amb = bufs(2)
        S = [sb(f"S{i}", [128, 128], F32) for i in range(2)]
        Sb = bufs(2)
        Sbf = [sb(f"Sbf{i}", [128, 128], BF16) for i in range(2)]
        Sbfb = bufs(2)
        k.op("dve", lambda E: E.memset(S[0][:], 0.0), w=[Sb[0]])
        k.op("dve", lambda E: E.memset(Sbf[0][:], 0.0), w=[Sbfb[0]])
        scur = 0

        def hgrn(g, gs):
            nonlocal scur
            for m in range(3):
                for kc in range(8):
                    k.op("pe", lambda E: E.matmul(out=zfm[m][:, :], lhsT=w1[:, kc, 512 + m * 128:512 + (m + 1) * 128],
                                                  rhs=hT[gs][:, kc, :], start=(kc == 0), stop=(kc == 7)),
                         r=hTb[gs] + [w1b], w=[zfmb[m]])
            yield
            zq, zf, zg = zfm
            ACT = lambda fn, r, w: k.op("act", fn, r=r, w=w)
            DVE = lambda fn, r, w: k.op("dve", fn, r=r, w=w)
            POOL = lambda fn, r, w: k.op("pool", fn, r=r, w=w)
            ACT(lambda E: E.activation(out=hg["F"][:], in_=zf[:], func=AF.Sigmoid), [zfmb[1]], [hgb["F"]])
            ACT(lambda E: E.activation(out=hg["QS"][:], in_=zq[:], func=AF.Silu), [zfmb[0]], [hgb["QS"]])
            ACT(lambda E: E.activation(out=hg["SG"][:], in_=zg[:], func=AF.Silu), [zfmb[2]], [hgb["SG"]])
            DVE(lambda E: E.tensor_scalar(out=hg["F"][:], in0=hg["F"][:], scalar1=c["oml_h"], scalar2=c["lb_h"],
                                          op0=ALU.mult, op1=ALU.add), [hgb["F"], cb], [hgb["F"]])
            ACT(lambda E: E.activation(out=hg["G"][:], in_=hg["F"][:], func=AF.Ln), [hgb["F"]], [hgb["G"]])
            POOL(lambda E: E.tensor_scalar(out=hg["KT"][:], in0=hg["F"][:], scalar1=-1.0, scalar2=1.0,
                                           op0=ALU.mult, op1=ALU.add), [hgb["F"]], [hgb["KT"]])
            DVE(lambda E: E.tensor_tensor_scan(out=hg["GC"][:], data0=c["ones_f"][:], data1=hg["G"][:], initial=0.0,
                                               op0=ALU.mult, op1=ALU.add), [hgb["G"], cb], [hgb["GC"]])
            GC3 = hg["GC"][:].rearrange("p (c t) -> p c t", t=64)
            D3 = hg["D"][:].rearrange("p (c t) -> p c t", t=64)
            DVE(lambda E: E.memset(GS[:, 0:1], 0.0), [], [GSb])
            DVE(lambda E: E.tensor_copy(out=GS[:, 1:8], in_=GC3[:, 0:7, 63]), [hgb["GC"]], [GSb])
            bc = lambda ap: ap.to_broadcast([128, 8, 64])
            DVE(lambda E: E.tensor_tensor(out=D3, in0=GC3, in1=bc(GC3[:, :, 31:32]), op=ALU.subtract),
                [hgb["GC"]], [hgb["D"]])
            ACT(lambda E: E.activation(out=hg["E"][:], in_=hg["D"][:], func=AF.Exp), [hgb["D"]], [hgb["E"]])
            DVE(lambda E: E.tensor_tensor(out=hb["Qh"][:], in0=hg["QS"][:], in1=hg["E"][:], op=ALU.mult),
                [hgb["QS"], hgb["E"]], [hbb["Qh"]])
            ACT(lambda E: E.activation(out=hg["E"][:], in_=hg["D"][:], func=AF.Exp, scale=-1.0), [hgb["D"]], [hgb["E"]])
            DVE(lambda E: E.tensor_tensor(out=hb["Kh"][:], in0=hg["KT"][:], in1=hg["E"][:], op=ALU.mult),
                [hgb["KT"], hgb["E"]], [hbb["Kh"]])
            DVE(lambda E: E.tensor_tensor(out=D3, in0=GC3, in1=bc(GC3[:, :, 63:64]), op=ALU.subtract),
                [hgb["GC"]], [hgb["D"]])
            ACT(lambda E: E.activation(out=hg["E"][:], in_=hg["D"][:], func=AF.Exp, scale=-1.0), [hgb["D"]], [hgb["E"]])
            DVE(lambda E: E.tensor_tensor(out=hb["Kb"][:], in0=hg["KT"][:], in1=hg["E"][:], op=ALU.mult),
                [hgb["KT"], hgb["E"]], [hbb["Kb"]])
            DVE(lambda E: E.tensor_tensor(out=D3, in0=GC3, in1=bc(GS[:].unsqueeze(2)), op=ALU.subtract),
                [hgb["GC"], GSb], [hgb["D"]])
            ACT(lambda E: E.activation(out=hg["E1"][:], in_=hg["D"][:], func=AF.Exp), [hgb["D"]], [hgb["E1"]])
            DVE(lambda E: E.tensor_tensor(out=hb["Qb"][:], in0=hg["QS"][:], in1=hg["E1"][:], op=ALU.mult),
                [hgb["QS"], hgb["E1"]], [hbb["Qb"]])
            for bl in range(4):
                k.op("pe", lambda E: E.transpose(out=tp[0][:, bl, :], in_=hb["Kb"][:, bl * 128:(bl + 1) * 128],
                                                 identity=c["ident_b"][:]), r=[hbb["Kb"], cb], w=[tpb[0]])
            ACT(lambda E: E.activation(out=KbT[:], in_=tp[0][:, 0:4, :], func=AF.Copy), [tpb[0]], [KbTb])
            yield
            for cc in range(8):
                bl, half = divmod(cc, 2)
                tile_i = g * 4 + bl
                p0 = half * 64
                csl = slice(cc * 64, (cc + 1) * 64)
                am_t, am_b = Am[cc % 2], Amb[cc % 2]
                k.op("pe", lambda E: E.matmul(out=pmisc[p0:p0 + 64, 0:64], lhsT=hb["Kh"][:, csl], rhs=hb["Qh"][:, csl],
                                              start=True, stop=True), r=[hbb["Kh"], hbb["Qh"]], w=[pAb])
                DVE(lambda E: E.tensor_tensor(out=am_t[p0:p0 + 64, :], in0=pmisc[p0:p0 + 64, 0:64],
                                              in1=c["mask64_f"][p0:p0 + 64, :], op=ALU.mult), [pAb, cb], [am_b])
                k.op("pe", lambda E: E.matmul(out=pOT[:, csl], lhsT=Sbf[scur][:], rhs=hb["Qb"][:, csl],
                                              start=True, stop=False), r=[Sbfb[scur], hbb["Qb"]], w=[pOTb])
                k.op("pe", lambda E: E.matmul(out=pOT[:, csl], lhsT=Vh[p0:p0 + 64, tile_i, :], rhs=am_t[p0:p0 + 64, :],
                                              start=False, stop=True), r=[Vhb[tile_i], am_b], w=[pOTb])
                k.op("pe", lambda E: E.matmul(out=pU[:, 0:128], lhsT=KbT[p0:p0 + 64, bl, :],
                                              rhs=Vh[p0:p0 + 64, tile_i, :], start=True, stop=True),
                     r=[KbTb, Vhb[tile_i]], w=[pUb])
                nxt = 1 - scur
                dcol = hg["E1"][:, cc * 64 + 63:cc * 64 + 64]
                DVE(lambda E: E.scalar_tensor_tensor(out=S[nxt][:], in0=S[scur][:], scalar=dcol, in1=pU[:, 0:128],
                                                     op0=ALU.mult, op1=ALU.add), [Sb[scur], hgb["E1"], pUb], [Sb[nxt]])
                ACT(lambda E: E.activation(out=Sbf[nxt][:], in_=S[nxt][:], func=AF.Copy), [Sb[nxt]], [Sbfb[nxt]])
                scur = nxt
                yield
            ACT(lambda E: E.activation(out=hb["SQ"][:], in_=pOT[:], func=AF.Square), [pOTb], [hbb["SQ"]])
            for bl in range(4):
                k.op("pe", lambda E: E.matmul(out=pmisc[:, 256 + bl:257 + bl], lhsT=hb["SQ"][:, bl * 128:(bl + 1) * 128],
                                              rhs=c["ones_b"][:, 0:1], start=True, stop=True), r=[hbb["SQ"], cb], w=[pSb])
            DVE(lambda E: E.tensor_copy(out=SSQ[:, g * 4:(g + 1) * 4], in_=pmisc[:, 256:260]), [pSb], [SSQb])
            DVE(lambda E: E.scalar_tensor_tensor(out=hb["Y"][:], in0=pOT[:], scalar=c["hn_h"][:, 0:1], in1=hg["SG"][:],
                                                 op0=ALU.mult, op1=ALU.mult), [pOTb, hgb["SG"], hbb["SQ"], cb], [hbb["Y"]])
            T2_ = L // 4
            rr_, co_ = divmod(g * 512, T2_)
            k.op("pool", lambda E: E.dma_start(out=dr["xch"][rr_ * 256 + 128:rr_ * 256 + 256, co_:co_ + 512], in_=hb["Y"][:]),
                 r=[hbb["Y"]], dsem="yo")

        pending = []

        def advance(n):
            for _ in range(n):
                if not pending:
                    return
                try:
                    next(pending[0])
                except StopIteration:
                    pending.pop(0)

        for i in range(NT):
            g, j = divmod(i, 4)
            gs = g % 2
            advance(3)
            xslot = i % NX
            x_t, x_b = xs[xslot], xsb[xslot]
            k.op("sp", lambda E: E.dma_start(out=x_t[:], in_=dr["xp"][i * 128:(i + 1) * 128, :]),
                 w=[x_b], dsem=f"x{xslot}")
            ci = (i % 8) * 2
            ssc, ssb_ = cols[:, ci:ci + 1], colb[ci]
            rsc, rsb_ = cols[:, ci + 1:ci + 2], colb[ci + 1]
            k.op("act", lambda E: E.activation(out=junk[:], in_=x_t[:], func=AF.Square, accum_out=ssc),
                 r=[x_b], w=[junkb, ssb_])
            rsqrt_col(k, ssc, ssb_, rsc, rsb_, 1024.0, EPS)
            if STOP <= 2.1:
                continue
            xn_t, xn_b = xn[i % 2], xnb[i % 2]
            k.op("dve", lambda E: E.tensor_scalar(out=xn_t[:], in0=x_t[:], scalar1=rsc, scalar2=None, op0=ALU.mult),
                 r=[x_b, rsb_], w=[xn_b])
            for kc in range(8):
                k.op("pe", lambda E: E.transpose(out=tp[i % 2][:, kc, :], in_=xn_t[:, kc * 128:(kc + 1) * 128],
                                                 identity=c["ident_b"][:]), r=[xn_b, cb], w=[tpb[i % 2]])
            k.op("act", lambda E: E.activation(out=hT[gs][:, :, j * 128:(j + 1) * 128], in_=tp[i % 2][:], func=AF.Copy),
                 r=[tpb[i % 2]], w=[hTb[gs][j]])
            if STOP <= 2.2:
                continue
            for kc in range(8):
                k.op("pe", lambda E: E.matmul(out=ztm[i % 2][:, :], lhsT=hT[gs][:, kc, j * 128:(j + 1) * 128],
                                              rhs=w1[:, kc, 0:512], start=(kc == 0), stop=(kc == 7)),
                     r=[hTb[gs][j], w1b], w=[ztmb[i % 2]])
            qs = i % 2
            qf, qfb = qkvf[qs], qkvfb[qs]
            if STOP <= 2.25:
                continue
            k.op("act", lambda E: E.activation(out=qf[:], in_=ztm[i % 2][:, 0:384], func=AF.Copy), r=[ztmb[i % 2]], w=[qfb])
            if STOP <= 2.27:
                continue
            k.op("act", lambda E: E.activation(out=Vh[:, i, :], in_=ztm[i % 2][:, 384:512], func=AF.Copy), r=[ztmb[i % 2]], w=[Vhb[i]])
            if STOP <= 2.3:
                continue
            v4 = qf[:, 0:256].rearrange("p (g d) -> p g d", d=64)
            rope_inplace(k, v4, qfb, c["cosp"][:, i, :], c["sinp"][:, i, :], rtmp, rtmpb, 128, 4)
            if STOP <= 2.4:
                continue
            k.op("pool", lambda E: E.dma_start(out=dr["k_out"][i * 128:(i + 1) * 128, :], in_=qf[:, 128:256]),
                 r=[qfb], dsem=f"kvo{qs}")
            k.op("pool", lambda E: E.dma_start(out=dr["v_out"][i * 128:(i + 1) * 128, :], in_=qf[:, 256:384]),
                 r=[qfb], dsem=f"kvo{qs}")
            if STOP <= 2.5:
                continue
            qb_t, qb_b = qkb[qs], qkbb[qs]
            k.op("act", lambda E: E.activation(out=qb_t[:, 0:128], in_=qf[:, 0:128], func=AF.Copy, scale=0.125),
                 r=[qfb], w=[qb_b])
            k.op("pool", lambda E: E.tensor_copy(out=qb_t[:, 128:256], in_=qf[:, 128:256]), r=[qfb], w=[qb_b])
            k.op("pool", lambda E: E.tensor_copy(out=Vaug[:, i, 0:128], in_=qf[:, 256:384]), r=[qfb], w=[Vaugb[i]])
            if STOP <= 2.6:
                continue
            for h_ in range(2):
                k.op("pe", lambda E: E.transpose(out=tq[:, h_, :], in_=qb_t[:, h_ * 128:(h_ + 1) * 128],
                                                 identity=c["ident_b"][:]), r=[qb_b, cb], w=[tqb])
            k.op("dve", lambda E: E.tensor_copy(out=QKT[:, :, i * 128:(i + 1) * 128], in_=tq[:, 0:2, :]), r=[tqb], w=[QKTb[i]])
            if j != 3 or STOP <= 2:
                continue
            pending.append(hgrn(g, gs))
            advance(1)
        while pending:
            advance(1)
        k.op("pool", lambda E: E.dma_start(out=dr["s_out"][:, :], in_=S[scur][:]), r=[Sb[scur]], dsem="so")
        NT2_ = NT // 4
        for rr_ in range(4):
            k.op("pool", lambda E: E.dma_start(out=dr["xssq"][rr_ * 128:(rr_ + 1) * 128, :],
                                               in_=SSQ[:, rr_ * NT2_:(rr_ + 1) * NT2_]), r=[SSQb], dsem="so")
        k.barrier()
        ip.close()
        if STOP <= 3:
            return
        attention_prompt(k, c, dr, L, ph, QKT, QKTb, Vaug, Vaugb)
        k.barrier()


def attention_prompt(k, c, dr, L, ph, QKT, QKTb, Vaug, Vaugb):
    nc = k.nc
    NT = L // 128
    cb = c["b"]
    with ExitStack() as ap:
        def sb(name, shape, dt):
            return ap.enter_context(nc.sbuf_tensor(name, shape, dt))

        def ps(name, shape, dt):
            return ap.enter_context(nc.psum_tensor(name, shape, dt))
        NS = 2
        ST = [[ps(f"ST{cc}_{s}", [128, 512], F32) for s in range(NS)] for cc in range(2)]
        STb = [bufs(NS) for _ in range(2)]
        PT = [[sb(f"PT{cc}_{s}", [128, 512], BF16) for s in range(NS)] for cc in range(2)]
        PTb = [bufs(NS) for _ in range(2)]
        O = ps("Oacc", [128, 2, 512], F32)
        Ob = Buf()
        tpo = ps("tpo", [128, 128], BF16)
        tpob = Buf()
        ec = sb("ecols", [128, 8], F32)
        ecb = Buf()
        Osb = sb("Osb", [128, 2, 129], F32)
        Osbb = Buf()
        a_t = sb("a_t", [128, 128], F32)
        oa_t = sb("oa_t", [128, 128], F32)
        ej = sb("ejunk", [128, 128], F32)
        oan = sb("oan", [128, 128], BF16)
        eb = Buf()
        ostage = [sb(f"ostage{i}", [128, 512], BF16) for i in range(2)]
        ostb = bufs(2)

        items = []
        for qb in range(NT):
            ng = qb // 4 + 1
            for g in range(ng):
                kbs = list(range(4 * g, min(4 * g + 4, qb + 1)))
                items.append((qb, g, kbs))
        slots = [n % NS for n in range(len(items))]

        def emit_qk(n):
            qb, g, kbs = items[n]
            s = slots[n]
            for jj, kb in enumerate(kbs):
                for cc in range(2):
                    k.op("pe", lambda E: E.matmul(out=ST[cc][s][:, jj * 128:(jj + 1) * 128],
                                                  lhsT=QKT[cc * 64:(cc + 1) * 64, 1, kb * 128:(kb + 1) * 128],
                                                  rhs=QKT[cc * 64:(cc + 1) * 64, 0, qb * 128:(qb + 1) * 128],
                                                  start=True, stop=True), r=[QKTb[kb], QKTb[qb]], w=[STb[cc][s]])

        emit_qk(0)
        for n, (qb, g, kbs) in enumerate(items):
            if n + 1 < len(items):
                emit_qk(n + 1)
            s = slots[n]
            nk = len(kbs)
            for cc in range(2):
                k.op("act", lambda E: E.activation(out=PT[cc][s][:, 0:nk * 128], in_=ST[cc][s][:, 0:nk * 128], func=AF.Exp),
                     r=[STb[cc][s]], w=[PTb[cc][s]])
            if kbs[-1] == qb:
                jd = nk - 1
                for cc in range(2):
                    k.op("dve", lambda E: E.tensor_tensor(out=PT[cc][s][:, jd * 128:(jd + 1) * 128],
                                                           in0=PT[cc][s][:, jd * 128:(jd + 1) * 128], in1=c["mask_b"][:],
                                                           op=ALU.mult), r=[PTb[cc][s], cb], w=[PTb[cc][s]])
            for cc in range(2):
                for jj, kb in enumerate(kbs):
                    k.op("pe", lambda E: E.matmul(out=O[:, cc, 0:129], lhsT=PT[cc][s][:, jj * 128:(jj + 1) * 128],
                                                  rhs=Vaug[:, kb, 0:129], start=(kb == 0), stop=(kb == qb)),
                         r=[PTb[cc][s], Vaugb[kb]], w=[Ob])
            if kbs[-1] != qb:
                continue
            DVE = lambda fn, r, w: k.op("dve", fn, r=r, w=w)
            DVE(lambda E: E.tensor_copy(out=Osb[:], in_=O[:, :, 0:129]), [Ob], [Osbb])
            DVE(lambda E: E.reciprocal(out=ec[:, 0:2], in_=Osb[:, :, 128]), [Osbb], [ecb])
            DVE(lambda E: E.tensor_tensor(out=ec[:, 2:3], in0=ec[:, 1:2], in1=c["neglam"], op=ALU.mult), [ecb, cb], [ecb])
            DVE(lambda E: E.tensor_scalar(out=a_t[:], in0=Osb[:, 0, 0:128], scalar1=ec[:, 0:1], scalar2=None, op0=ALU.mult),
                [Osbb, ecb], [eb])
            DVE(lambda E: E.scalar_tensor_tensor(out=oa_t[:], in0=Osb[:, 1, 0:128], scalar=ec[:, 2:3], in1=a_t[:],
                                                 op0=ALU.mult, op1=ALU.add), [Osbb, ecb, eb], [eb])
            DVE(lambda E: E.scalar_tensor_tensor(out=ej[:], in0=oa_t[:], scalar=1.0, in1=oa_t[:], op0=ALU.mult,
                                                 op1=ALU.mult, accum_out=ec[:, 3:4]), [eb], [eb, ecb])
            rsqrt_col(k, ec[:, 3:4], ecb, ec[:, 4:5], ecb, 128.0, SUBLN_EPS)
            DVE(lambda E: E.tensor_scalar(out=oan[:], in0=oa_t[:], scalar1=ec[:, 4:5], scalar2=None, op0=ALU.mult),
                [eb, ecb], [eb])
            k.op("pe", lambda E: E.transpose(out=tpo[:], in_=oan[:], identity=c["ident_b"][:]), r=[eb, cb], w=[tpob])
            og, oj = divmod(qb, 4)
            os_ = og % 2
            DVE(lambda E: E.tensor_copy(out=ostage[os_][:, oj * 128:(oj + 1) * 128], in_=tpo[:]), [tpob], [ostb[os_]])
            if oj == 3 or qb == NT - 1:
                rr_, co_ = divmod(og * 512, L // 4)
                k.op("pool", lambda E: E.dma_start(out=dr["xch"][rr_ * 256:rr_ * 256 + 128, co_:co_ + (oj + 1) * 128],
                                                   in_=ostage[os_][:, 0:(oj + 1) * 128]), r=[ostb[os_]], dsem=f"oo{os_}")


def phase2(k, c, dr, L, smp):
    nc = k.nc
    T2 = L // 4
    NT2 = T2 // 128
    cb = c["b"]
    tiles = [(128, t) for t in range(NT2)]
    if smp is not None:
        tiles.append((64, NT2))
    with ExitStack() as p2:
        def mk(stack):
            def sb(name, shape, dt):
                return stack.enter_context(nc.sbuf_tensor(name, shape, dt))

            def ps(name, shape, dt):
                return stack.enter_context(nc.psum_tensor(name, shape, dt))
            return sb, ps
        sb, ps = mk(p2)
        X1 = sb("X1", [128, NT2 + 1, 1024], F32)
        X1b = bufs(NT2 + 1)
        cols = sb("p2cols", [128, 16], F32)
        colb = bufs(16)
        junk = sb("p2junk", [128, 1024], BF16)
        junkb = Buf()
        with ExitStack() as pa:
            sba, psa = mk(pa)
            mixT = sba("mixT", [128, 8, T2], BF16)
            mixb = bufs(8)
            idxm = sba("idxm", [128, 8], I32)
            idxs = sba("idxs", [128, 4], I32)
            idxb = Buf()
            ssqg = sba("ssqg", [128, 4, NT2], F32)
            ssqb = Buf()
            rsth = sba("rsth", [128, NT2], F32)
            wout = sba("wout", [128, 8, 1024], BF16)
            woutb = Buf()
            wst = [sba(f"wost{i}", [128, 1024], F32) for i in range(2)]
            wstb = bufs(2)
            xt = [sba(f"p2x{i}", [128, 1024], F32) for i in range(2)]
            xtb = bufs(2)
            pA = psa("pA", [128, 2, 512], F32)
            pH = psa("pH", [128, 2, 512], F32)
            pAb, pHb = Buf(), Buf()
            k.op("sp", lambda E: E.dma_start(out=idxm[:], in_=dr["idx_mix"]), w=[idxb], dsem="p2i")
            k.op("sp", lambda E: E.dma_start(out=idxs[:], in_=dr["idx_ssq"]), w=[idxb], dsem="p2i")
            for kc in range(8):
                k.op("pool", lambda E: E.indirect_dma_start(
                    out=mixT[:, kc, :], out_offset=None, in_=dr["xg"][:, :],
                    in_offset=bass.IndirectOffsetOnAxis(ap=idxm[:, kc:kc + 1], axis=0)),
                    r=[idxb, dr["xgb"]], w=[mixb[kc]], dsem="p2g")
            for h in range(4):
                k.op("pool", lambda E: E.indirect_dma_start(
                    out=ssqg[:, h, :], out_offset=None, in_=dr["xgs"][:, :],
                    in_offset=bass.IndirectOffsetOnAxis(ap=idxs[:, h:h + 1], axis=0)),
                    r=[idxb, dr["xgsb"]], w=[ssqb], dsem="p2g")
            for b_ in mixb + [ssqb]:
                b_.w = ("p2g", k.cnt["p2g"])
            for kc in range(8):
                s_ = kc % 2
                k.op("sp", lambda E: E.dma_start(out=wst[s_][:], in_=dr["w_out"][kc * 128:(kc + 1) * 128, :]),
                     w=[wstb[s_]], dsem=f"wo{s_}")
                if kc < 4:
                    k.op("dve", lambda E: E.tensor_scalar(out=wout[:, kc, :], in0=wst[s_][:], scalar1=c["subln"][:, 0:1],
                                                          scalar2=1.0 - LAM_INIT, op0=ALU.mult, op1=ALU.mult),
                         r=[wstb[s_], cb], w=[woutb])
                else:
                    k.op("dve", lambda E: E.tensor_copy(out=wout[:, kc, :], in_=wst[s_][:]), r=[wstb[s_]], w=[woutb])
            k.op("dve", lambda E: E.tensor_tensor(out=rsth[:], in0=ssqg[:, 0, :], in1=ssqg[:, 1, :], op=ALU.add), r=[ssqb], w=[ssqb])
            k.op("dve", lambda E: E.tensor_tensor(out=rsth[:], in0=rsth[:], in1=ssqg[:, 2, :], op=ALU.add), r=[ssqb], w=[ssqb])
            k.op("dve", lambda E: E.tensor_tensor(out=rsth[:], in0=rsth[:], in1=ssqg[:, 3, :], op=ALU.add), r=[ssqb], w=[ssqb])
            rsqrt_col(k, rsth[:], ssqb, rsth[:], ssqb, 512.0, EPS)
            for ti, (nt, t) in enumerate(tiles):
                xs_ = ti % 2
                src = dr["xp2"][t * 128:(t + 1) * 128, :] if nt == 128 else dr["xs"][:, :]
                k.op("sp", lambda E: E.dma_start(out=xt[xs_][0:nt, :], in_=src), w=[xtb[xs_]], dsem=f"p2x{xs_}")
                for half in range(2):
                    for kc in range(8):
                        if nt == 128:
                            lhs, lb_ = mixT[:, kc, t * 128:(t + 1) * 128], mixb[kc]
                        else:
                            lhs, lb_ = smp["mixT"][:, kc, :], smp["b"]
                        dst, db = (pA, pAb) if kc < 4 else (pH, pHb)
                        k.op("pe", lambda E: E.matmul(out=dst[0:nt, half, :], lhsT=lhs, rhs=wout[:, kc, half * 512:(half + 1) * 512],
                                                      start=(kc % 4 == 0), stop=(kc % 4 == 3)), r=[lb_, woutb], w=[db])
                rcol = rsth[:, t:t + 1] if nt == 128 else smp["rstdh"][0:64, 0:1]
                rb = ssqb if nt == 128 else smp["b"]
                for half in range(2):
                    hs = slice(half * 512, (half + 1) * 512)
                    k.op("dve", lambda E: E.tensor_tensor(out=xt[xs_][0:nt, hs], in0=xt[xs_][0:nt, hs], in1=pA[0:nt, half, :],
                                                          op=ALU.add), r=[xtb[xs_], pAb], w=[xtb[xs_]])
                    k.op("dve", lambda E: E.scalar_tensor_tensor(out=X1[0:nt, t, hs], in0=pH[0:nt, half, :], scalar=rcol,
                                                                 in1=xt[xs_][0:nt, hs], op0=ALU.mult, op1=ALU.add),
                         r=[pHb, rb, xtb[xs_]], w=[X1b[t]])
            k.barrier()
        with ExitStack() as pb:
            sbb, psb = mk(pb)
            wd = sbb("wd", [128, NFF, 1024], BF16)
            wdb = Buf()
            with ExitStack() as pw:
                wdst = [pw.enter_context(nc.sbuf_tensor(f"wdst{i}", [128, 1024], F32)) for i in range(2)]
                wdstb = bufs(2)
                for f in range(NFF):
                    s_ = f % 2
                    k.op("sp", lambda E: E.dma_start(out=wdst[s_][:], in_=dr["w_down"][f * 128:(f + 1) * 128, :]),
                         w=[wdstb[s_]], dsem=f"wd{s_}")
                    k.op("pool", lambda E: E.tensor_copy(out=wd[:, f, :], in_=wdst[s_][:]), r=[wdstb[s_]], w=[wdb])
                k.barrier()
            GW = 576
            h2 = sbb("h2", [128, 1024], BF16)
            h2b = Buf()
            h2T = sbb("h2T", [128, 8, GW], BF16)
            h2Tb = Buf()
            actT = sbb("actT", [128, NFF, GW], BF16)
            actTb = Buf()
            wgs = [sbb(f"wgs{i}", [128, 8, 128], F32) for i in range(3)]
            wgsb = bufs(3)
            wgb_ = [sbb(f"wgb{i}", [128, 8, 128], BF16) for i in range(4)]
            wgbb = bufs(4)
            sg = [sbb(f"sg{i}", [128, 512], F32) for i in range(2)]
            sgb = bufs(2)
            x2 = [sbb(f"x2_{i}", [128, 1024], F32) for i in range(2)]
            x2b = bufs(2)
            yt = [sbb(f"yt_{i}", [128, 1024], F32) for i in range(2)]
            ytb = bufs(2)
            tp2 = psb("tp2", [128, 8, 128], BF16)
            tp2b = Buf()
            pG = [psb(f"pG{i}", [128, 512], F32) for i in range(2)]
            pGb = bufs(2)
            pUp = [psb(f"pUp{i}", [128, 512], F32) for i in range(2)]
            pUpb = bufs(2)
            pD = [psb(f"pD{i}", [128, 512], F32) for i in range(2)]
            pDb = bufs(2)
            groups = [tiles[i:i + 4] for i in range(0, NT2, 4)]
            if smp is not None:
                groups[-1] = groups[-1] + [tiles[-1]]
            fnb = c["fnc"][:].unsqueeze(2).to_broadcast([128, 8, 128])
            blkctr = 0
            dctr = 0
            tctr = 0
            for gt in groups:
                ncols = sum(nt for nt, _ in gt)
                col = 0
                for (nt, t) in gt:
                    ci = (tctr % 8) * 2
                    tctr += 1
                    k.op("act", lambda E: E.activation(out=junk[0:nt, :], in_=X1[0:nt, t, :], func=AF.Square,
                                                       accum_out=cols[0:nt, ci:ci + 1]), r=[X1b[t]], w=[junkb, colb[ci]])
                    rsqrt_col(k, cols[0:nt, ci:ci + 1], colb[ci], cols[0:nt, ci + 1:ci + 2], colb[ci + 1], 1024.0, EPS)
                    k.op("dve", lambda E: E.tensor_scalar(out=h2[0:nt, :], in0=X1[0:nt, t, :], scalar1=cols[0:nt, ci + 1:ci + 2],
                                                          scalar2=None, op0=ALU.mult), r=[X1b[t], colb[ci + 1]], w=[h2b])
                    for kc in range(8):
                        k.op("pe", lambda E: E.transpose(out=tp2[:, kc, 0:nt], in_=h2[0:nt, kc * 128:(kc + 1) * 128],
                                                         identity=c["ident_b"][0:nt, 0:nt]), r=[h2b, cb], w=[tp2b])
                    k.op("act", lambda E: E.activation(out=h2T[:, :, col:col + nt], in_=tp2[:, :, 0:nt], func=AF.Copy),
                         r=[tp2b], w=[h2Tb])
                    col += nt
                blocks = [(0, min(512, ncols))]
                if ncols > 512:
                    blocks.append((512, ncols - 512))
                for f in range(NFF):
                    ws = []
                    for wi, wn in enumerate(("w_gate", "w_up")):
                        s_ = (f * 2 + wi) % 4
                        s3 = (f * 2 + wi) % 3
                        src = dr[wn][f]
                        k.op("sp", lambda E: E.dma_start(out=wgs[s3][:], in_=src), w=[wgsb[s3]], dsem=f"wg{s3}")
                        k.op("pool", lambda E: E.tensor_tensor(out=wgb_[s_][:], in0=wgs[s3][:], in1=fnb, op=ALU.mult),
                             r=[wgsb[s3], cb], w=[wgbb[s_]])
                        ws.append(s_)
                    for (c0, cn) in blocks:
                        bs = blkctr % 2
                        blkctr += 1
                        for kc in range(8):
                            k.op("pe", lambda E: E.matmul(out=pG[bs][:, 0:cn], lhsT=wgb_[ws[0]][:, kc, :], rhs=h2T[:, kc, c0:c0 + cn],
                                                          start=(kc == 0), stop=(kc == 7)), r=[wgbb[ws[0]], h2Tb], w=[pGb[bs]])
                        for kc in range(8):
                            k.op("pe", lambda E: E.matmul(out=pUp[bs][:, 0:cn], lhsT=wgb_[ws[1]][:, kc, :], rhs=h2T[:, kc, c0:c0 + cn],
                                                          start=(kc == 0), stop=(kc == 7)), r=[wgbb[ws[1]], h2Tb], w=[pUpb[bs]])
                        k.op("act", lambda E: E.activation(out=sg[bs][:, 0:cn], in_=pG[bs][:, 0:cn], func=AF.Silu),
                             r=[pGb[bs]], w=[sgb[bs]])
                        k.op("dve", lambda E: E.tensor_tensor(out=actT[:, f, c0:c0 + cn], in0=sg[bs][:, 0:cn], in1=pUp[bs][:, 0:cn],
                                                              op=ALU.mult), r=[sgb[bs], pUpb[bs]], w=[actTb])
                col = 0
                for (nt, t) in gt:
                    xs_ = t % 2
                    for half in range(2):
                        ds = dctr % 2
                        dctr += 1
                        hs = slice(half * 512, (half + 1) * 512)
                        for f in range(NFF):
                            k.op("pe", lambda E: E.matmul(out=pD[ds][0:nt, :], lhsT=actT[:, f, col:col + nt], rhs=wd[:, f, hs],
                                                          start=(f == 0), stop=(f == NFF - 1)), r=[actTb, wdb], w=[pDb[ds]])
                        k.op("dve", lambda E: E.tensor_tensor(out=x2[xs_][0:nt, hs], in0=X1[0:nt, t, hs], in1=pD[ds][0:nt, :],
                                                              op=ALU.add), r=[X1b[t], pDb[ds]], w=[x2b[xs_]])
                    ci = (tctr % 8) * 2
                    tctr += 1
                    k.op("act", lambda E: E.activation(out=junk[0:nt, :], in_=x2[xs_][0:nt, :], func=AF.Square,
                                                       accum_out=cols[0:nt, ci:ci + 1]), r=[x2b[xs_]], w=[junkb, colb[ci]])
                    rsqrt_col(k, cols[0:nt, ci:ci + 1], colb[ci], cols[0:nt, ci + 1:ci + 2], colb[ci + 1], 1024.0, EPS)
                    k.op("dve", lambda E: E.scalar_tensor_tensor(out=yt[xs_][0:nt, :], in0=x2[xs_][0:nt, :],
                                                                 scalar=cols[0:nt, ci + 1:ci + 2], in1=c["fin"][0:nt, :],
                                                                 op0=ALU.mult, op1=ALU.mult), r=[x2b[xs_], colb[ci + 1], cb], w=[ytb[xs_]])
                    dst = dr["y_p"][t * 128:(t + 1) * 128, :] if nt == 128 else dr["y_s"][:, :]
                    k.op("pool", lambda E: E.dma_start(out=dst, in_=yt[xs_][0:nt, :]), r=[ytb[xs_]], dsem=f"yo{xs_}")
                    col += nt
            k.barrier()


def phaseS(k, c, dr, es):
    nc = k.nc
    cb = c["b"]
    smp = {"b": Buf()}
    smp["mixT"] = es.enter_context(nc.sbuf_tensor("smixT", [128, 8, 64], BF16))
    smp["rstdh"] = es.enter_context(nc.sbuf_tensor("srstdh", [128, 1], F32))
    mb = smp["b"]
    qkT = es.enter_context(nc.sbuf_tensor("s_qkT", [128, 8, 64], BF16))
    Vn = es.enter_context(nc.sbuf_tensor("s_Vn", [64, 4, 130], BF16))
    with ExitStack() as st:
        def sb(name, shape, dt):
            return st.enter_context(nc.sbuf_tensor("ps_" + name, shape, dt))

        def ps(name, shape, dt):
            return st.enter_context(nc.psum_tensor("pp_" + name, shape, dt))
        ACT = lambda fn, r, w: k.op("act", fn, r=r, w=w)
        DVE = lambda fn, r, w: k.op("dve", fn, r=r, w=w)
        POOL = lambda fn, r, w: k.op("pool", fn, r=r, w=w)
        PE = lambda fn, r, w: k.op("pe", fn, r=r, w=w)
        Z = sb("Z", [64, NIN], F32)
        Zb = Buf()
        ZT = [sb(f"ZT{i}", [128, 4, 64], F32) for i in range(3)]
        ZTb = Buf()
        gb = Buf()
        with ExitStack() as s0:
            xs_t = s0.enter_context(nc.sbuf_tensor("sx", [64, 1024], F32))
            xn = s0.enter_context(nc.sbuf_tensor("sxn", [64, 1024], BF16))
            junk = s0.enter_context(nc.sbuf_tensor("sjunk", [64, 1024], BF16))
            cl = s0.enter_context(nc.sbuf_tensor("scl", [64, 2], F32))
            hTs = s0.enter_context(nc.sbuf_tensor("shT", [128, 8, 64], BF16))
            wst = [s0.enter_context(nc.sbuf_tensor(f"swst{i}", [128, 512], F32)) for i in range(3)]
            wstb = bufs(3)
            wb = [s0.enter_context(nc.sbuf_tensor(f"swb{i}", [128, 512], BF16)) for i in range(3)]
            wbb = bufs(3)
            tps = s0.enter_context(nc.psum_tensor("stp", [128, 8, 128], BF16))
            pz = [s0.enter_context(nc.psum_tensor(f"spz{i}", [128, 512], F32)) for i in range(2)]
            pzb = bufs(2)
            pfh = [s0.enter_context(nc.psum_tensor(f"spfh{i}", [128, 512], F32)) for i in range(4)]
            pfhb = bufs(4)
            k.op("sp", lambda E: E.dma_start(out=xs_t[:], in_=dr["xs"][:, :]), w=[gb], dsem="sx")
            ACT(lambda E: E.activation(out=junk[:], in_=xs_t[:], func=AF.Square, accum_out=cl[:, 0:1]), [gb], [gb])
            rsqrt_col(k, cl[:, 0:1], gb, cl[:, 1:2], gb, 1024.0, EPS)
            DVE(lambda E: E.tensor_scalar(out=xn[:], in0=xs_t[:], scalar1=cl[:, 1:2], scalar2=None, op0=ALU.mult), [gb], [gb])
            for kc in range(8):
                PE(lambda E: E.transpose(out=tps[:, kc, 0:64], in_=xn[:, kc * 128:(kc + 1) * 128], identity=c["ident_b"][0:64, 0:64]),
                   [gb, cb], [gb])
            ACT(lambda E: E.activation(out=hTs[:], in_=tps[:, :, 0:64], func=AF.Copy), [gb], [gb])
            n = 0
            for cg in range(7):
                for kc in range(8):
                    s_ = n % 3
                    n += 1
                    k.op("sp", lambda E: E.dma_start(out=wst[s_][:], in_=dr["w_in"][kc * 128:(kc + 1) * 128, cg * 512:(cg + 1) * 512]),
                         w=[wstb[s_]], dsem=f"sw{s_}")
                    DVE(lambda E: E.tensor_scalar(out=wb[s_][:], in0=wst[s_][:], scalar1=c["an"][:, kc:kc + 1], scalar2=None,
                                                  op0=ALU.mult), [wstb[s_], cb], [wbb[s_]])
                    PE(lambda E: E.matmul(out=pz[cg % 2][0:64, :], lhsT=hTs[:, kc, :], rhs=wb[s_][:], start=(kc == 0), stop=(kc == 7)),
                       [gb, wbb[s_]], [pzb[cg % 2]])
                    if cg in (3, 4, 6):
                        for h in range(4):
                            PE(lambda E: E.matmul(out=pfh[h][:, 0:64], lhsT=wb[s_][:, h * 128:(h + 1) * 128], rhs=hTs[:, kc, :],
                                                  start=(kc == 0), stop=(kc == 7)), [gb, wbb[s_]], [pfhb[h]])
                ACT(lambda E: E.activation(out=Z[:, cg * 512:(cg + 1) * 512], in_=pz[cg % 2][0:64, :], func=AF.Copy),
                    [pzb[cg % 2]], [Zb])
                if cg in (3, 4, 6):
                    qi_ = {3: 0, 4: 1, 6: 2}[cg]
                    for h in range(4):
                        ACT(lambda E: E.activation(out=ZT[qi_][:, h, :], in_=pfh[h][:, 0:64], func=AF.Copy), [pfhb[h]], [ZTb])
            k.barrier()
        rtmp = sb("rtmp", [64, 512], F32)
        rope_inplace(k, Z[:, 0:1024].rearrange("p (g d) -> p g d", d=64), Zb, c["coss"][:, :], c["sins"][:, :], rtmp, gb, 64, 16)
        k.op("pool", lambda E: E.dma_start(out=dr["ks_out"][:, :], in_=Z[:, 512:1024]), r=[Zb], dsem="so2")
        k.op("pool", lambda E: E.dma_start(out=dr["vs_out"][:, :], in_=Z[:, 1024:1536]), r=[Zb], dsem="so2")
        if STOP <= 1:
            k.barrier()
            return smp
        qkb = sb("qkb", [64, 1024], BF16)
        Vhs = sb("Vhs", [64, 4, 128], BF16)
        tpb_ = ps("tpb", [128, 8, 128], BF16)
        ACT(lambda E: E.activation(out=qkb[:, 0:512], in_=Z[:, 0:512], func=AF.Copy, scale=0.125), [Zb], [gb])
        DVE(lambda E: E.tensor_copy(out=qkb[:, 512:1024], in_=Z[:, 512:1024]), [Zb], [gb])
        DVE(lambda E: E.memset(Vn[:, :, 128:130], 1.0), [], [gb])
        DVE(lambda E: E.tensor_copy(out=Vn[:, :, 0:128], in_=Z[:, 1024:1536].rearrange("p (h e) -> p h e", e=128)), [Zb], [gb])
        DVE(lambda E: E.tensor_copy(out=Vhs[:], in_=Z[:, 2560:3072].rearrange("p (h e) -> p h e", e=128)), [Zb], [gb])
        for i8 in range(8):
            PE(lambda E: E.transpose(out=tpb_[:, i8, 0:64], in_=qkb[:, i8 * 128:(i8 + 1) * 128], identity=c["ident_b"][0:64, 0:64]),
               [gb, cb], [gb])
        ACT(lambda E: E.activation(out=qkT[:], in_=tpb_[:, :, 0:64], func=AF.Copy), [gb], [gb])
        if STOP <= 1.3:
            k.barrier()
            return smp
        pf = ZT
        hs = {n_: sb("h_" + n_, [128, 4, 64], F32) for n_ in ("F", "G", "KT", "QS", "SG", "GC", "E1", "E", "D")}
        hbf = {n_: sb("hb_" + n_, [128, 4, 64], BF16) for n_ in ("Qb", "Kh", "Kb")}
        ACT(lambda E: E.activation(out=hs["F"][:], in_=pf[1][:], func=AF.Sigmoid), [gb, ZTb], [gb])
        ACT(lambda E: E.activation(out=hs["QS"][:], in_=pf[0][:], func=AF.Silu), [gb, ZTb], [gb])
        ACT(lambda E: E.activation(out=hs["SG"][:], in_=pf[2][:], func=AF.Silu), [gb, ZTb], [gb])
        for h in range(4):
            DVE(lambda E: E.tensor_scalar(out=hs["F"][:, h, :], in0=hs["F"][:, h, :], scalar1=c["oml_a"][:, h:h + 1],
                                          scalar2=c["lb_a"][:, h:h + 1], op0=ALU.mult, op1=ALU.add), [gb, cb], [gb])
        ACT(lambda E: E.activation(out=hs["GC"][:], in_=hs["F"][:], func=AF.Ln), [gb], [gb])
        DVE(lambda E: E.tensor_scalar(out=hs["KT"][:], in0=hs["F"][:], scalar1=-1.0, scalar2=1.0, op0=ALU.mult, op1=ALU.add), [gb], [gb])
        GC4 = hs["GC"][:].rearrange("p h (s t) -> p (h s) t", t=4)
        for t_ in range(1, 4):
            DVE(lambda E: E.tensor_tensor(out=GC4[:, :, t_:t_ + 1], in0=GC4[:, :, t_:t_ + 1], in1=GC4[:, :, t_ - 1:t_], op=ALU.add), [gb], [gb])
        ACT(lambda E: E.activation(out=hs["E1"][:], in_=hs["GC"][:], func=AF.Exp), [gb], [gb])
        DVE(lambda E: E.tensor_tensor(out=hbf["Qb"][:], in0=hs["QS"][:], in1=hs["E1"][:], op=ALU.mult), [gb], [gb])
        ACT(lambda E: E.activation(out=hs["E"][:], in_=hs["GC"][:], func=AF.Exp, scale=-1.0), [gb], [gb])
        DVE(lambda E: E.tensor_tensor(out=hbf["Kh"][:], in0=hs["KT"][:], in1=hs["E"][:], op=ALU.mult), [gb], [gb])
        D4 = hs["D"][:].rearrange("p h (s t) -> p (h s) t", t=4)
        DVE(lambda E: E.tensor_tensor(out=D4, in0=GC4[:, :, 3:4].to_broadcast([128, 64, 4]), in1=GC4, op=ALU.subtract), [gb], [gb])
        ACT(lambda E: E.activation(out=hs["E"][:], in_=hs["D"][:], func=AF.Exp), [gb], [gb])
        DVE(lambda E: E.tensor_tensor(out=hbf["Kb"][:], in0=hs["KT"][:], in1=hs["E"][:], op=ALU.mult), [gb], [gb])
        KbTok = sb("KbTok", [64, 4, 128], BF16)
        KbSel = sb("KbSel", [64, 4, 16, 128], BF16)
        for h in range(4):
            PE(lambda E: E.transpose(out=tpb_[0:64, h, :], in_=hbf["Kb"][:, h, :], identity=c["ident_b"][:]), [gb, cb], [gb])
        ACT(lambda E: E.activation(out=KbTok[:], in_=tpb_[0:64, 0:4, :], func=AF.Copy), [gb], [gb])
        for h in range(4):
            DVE(lambda E: E.tensor_tensor(out=KbSel[:, h, :, :], in0=KbTok[:, h, :].unsqueeze(1).to_broadcast([64, 16, 128]),
                                           in1=c["sel"][:].unsqueeze(2).to_broadcast([64, 16, 128]), op=ALU.mult), [gb, cb], [gb])
        pAs = ps("pAs", [128, 4, 64], F32)
        Am = sb("Am", [64, 4, 64], BF16)
        for h in range(4):
            PE(lambda E: E.matmul(out=pAs[0:64, h, :], lhsT=hbf["Kh"][:, h, :], rhs=hbf["Qb"][:, h, :], start=True, stop=True), [gb], [gb])
        DVE(lambda E: E.tensor_tensor(out=Am[:], in0=pAs[0:64, :, :], in1=c["bmask_f"][:].unsqueeze(1).to_broadcast([64, 4, 64]),
                                      op=ALU.mult), [gb, cb], [gb])
        if STOP <= 1.5:
            k.barrier()
            return smp
        pO = ps("pO", [128, 4, 64], F32)
        pOb = Buf()
        pUs = [ps(f"pUs{i}", [128, 4, 128], F32) for i in range(2)]
        pUsb = bufs(2)
        S0 = [sb(f"S0_{i}", [128, 4, 128], F32) for i in range(2)]
        S0b = bufs(2)
        S0bf = [sb(f"S0bf_{i}", [128, 4, 128], BF16) for i in range(2)]
        S0bfb = bufs(2)
        Sn = [sb(f"Sn_{i}", [128, 4, 128], F32) for i in range(2)]
        Snb = bufs(2)
        pOi = ps("pOi", [128, 4, 64], F32)
        Oin = sb("Oin", [128, 4, 64], F32)
        for h in range(4):
            PE(lambda E: E.matmul(out=pOi[:, h, :], lhsT=Vhs[:, h, :], rhs=Am[:, h, :], start=True, stop=True), [gb], [gb])
        ACT(lambda E: E.activation(out=Oin[:], in_=pOi[:], func=AF.Copy), [gb], [gb])
        for s_ in range(16):
            sl = s_ % 2
            k.op("sp", lambda E: E.dma_start(out=S0[sl][:], in_=dr["state"][s_].rearrange("h k v -> k h v")), w=[S0b[sl]], dsem=f"st{sl}")
            ACT(lambda E: E.activation(out=S0bf[sl][:], in_=S0[sl][:], func=AF.Copy), [S0b[sl]], [S0bfb[sl]])
            for h in range(4):
                PE(lambda E: E.matmul(out=pO[:, h, s_ * 4:(s_ + 1) * 4], lhsT=S0bf[sl][:, h, :], rhs=hbf["Qb"][:, h, s_ * 4:(s_ + 1) * 4],
                                      start=True, stop=True), [S0bfb[sl], gb], [pOb])
            for h in range(4):
                PE(lambda E: E.matmul(out=pUs[sl][:, h, :], lhsT=KbSel[:, h, s_, :], rhs=Vhs[:, h, :], start=True, stop=True),
                   [gb], [pUsb[sl]])
            for h in range(4):
                DVE(lambda E: E.scalar_tensor_tensor(out=Sn[sl][:, h, :], in0=S0[sl][:, h, :], scalar=hs["E1"][:, h, s_ * 4 + 3:s_ * 4 + 4],
                                                     in1=pUs[sl][:, h, :], op0=ALU.mult, op1=ALU.add), [S0b[sl], pUsb[sl], gb], [Snb[sl]])
            k.op("pool", lambda E: E.dma_start(out=dr["ss_out"][s_].rearrange("h k v -> k h v"), in_=Sn[sl][:]), r=[Snb[sl]], dsem=f"sso{sl}")
        if STOP <= 1.7:
            k.barrier()
            return smp
        SQ = sb("SQ", [128, 4, 64], BF16)
        pSs = pAs[:, 0, 0:4]
        DVE(lambda E: E.tensor_tensor(out=Oin[:], in0=Oin[:], in1=pO[:], op=ALU.add), [pOb, gb], [gb])
        ACT(lambda E: E.activation(out=SQ[:], in_=Oin[:], func=AF.Square), [gb], [gb])
        for h in range(4):
            PE(lambda E: E.matmul(out=pSs[0:64, h:h + 1], lhsT=SQ[:, h, :], rhs=c["ones_b"][:, 0:1], start=True, stop=True), [gb, cb], [gb])
        sc = sb("sc", [64, 4], F32)
        DVE(lambda E: E.tensor_reduce(out=sc[:, 0:1], in_=pSs[0:64, 0:4], axis=AX.X, op=ALU.add), [gb], [gb])
        rsqrt_col(k, sc[:, 0:1], gb, smp["rstdh"][0:64, 0:1], mb, 512.0, EPS)
        for h in range(4):
            DVE(lambda E: E.scalar_tensor_tensor(out=smp["mixT"][:, 4 + h, :], in0=Oin[:, h, :], scalar=c["hn_a"][:, h:h + 1],
                                                 in1=hs["SG"][:, h, :], op0=ALU.mult, op1=ALU.mult), [pOb, gb, cb], [mb])
        k.barrier()
    with ExitStack() as sa:
        def sb(name, shape, dt):
            return sa.enter_context(nc.sbuf_tensor("pa_" + name, shape, dt))

        def ps(name, shape, dt):
            return sa.enter_context(nc.psum_tensor("ppa_" + name, shape, dt))
        ACT = lambda fn, r, w: k.op("act", fn, r=r, w=w)
        DVE = lambda fn, r, w: k.op("dve", fn, r=r, w=w)
        POOL = lambda fn, r, w: k.op("pool", fn, r=r, w=w)
        PE = lambda fn, r, w: k.op("pe", fn, r=r, w=w)
        gb2 = Buf()
        pScm = [ps(f"pSc{i}", [128, 512], F32) for i in range(2)]
        pScb = Buf()
        pNm = [pScm[i][0:64, 0:256].rearrange("p (h t) -> p h t", t=64) for i in range(2)]
        PnT = sb("PnT", [64, 8, 64], BF16)
        Pn32 = sb("Pn32", [64, 8, 64], F32)
        for h in range(4):
            for cm in range(2):
                PE(lambda E: E.matmul(out=pNm[cm][:, h, :], lhsT=qkT[cm * 64:(cm + 1) * 64, 4 + h, :],
                                      rhs=qkT[cm * 64:(cm + 1) * 64, h, :], start=True, stop=True), [gb], [pScb])
        Pn4 = Pn32[:].rearrange("p (h c) t -> p h c t", c=2)
        for cm in range(2):
            ACT(lambda E: E.activation(out=Pn4[:, :, cm, :], in_=pNm[cm], func=AF.Exp), [pScb], [gb2])
        DVE(lambda E: E.tensor_tensor(out=PnT[:], in0=Pn32[:], in1=c["bmask_f"][:].unsqueeze(1).to_broadcast([64, 8, 64]), op=ALU.mult),
            [gb2, cb], [gb2])
        ptb = sb("ptb", [128, 256], I32)
        idxp = sb("idxp", [128, 256], I32)
        if STOP <= 2:
            k.barrier()
            return smp
        k.op("sp", lambda E: E.dma_start(out=ptb[:], in_=dr["pt"].partition_broadcast(128)), w=[gb2], dsem="spt")
        DVE(lambda E: E.tensor_scalar(out=idxp[:], in0=ptb[:], scalar1=128.0, scalar2=c["iota_p"][:, 0:1], op0=ALU.mult, op1=ALU.add),
            [gb2, cb], [gb2])
        NPG = 4
        OA = sb("OA", [4, 16, 8, 130], F32)
        OAb = Buf()
        pOs = ps("pOs", [128, 3, 512], F32)
        pOsb = Buf()
        lp = ExitStack()
        _sb_outer = sb
        sb = lambda name, shape, dt: lp.enter_context(nc.sbuf_tensor("pl_" + name, shape, dt))
        Kpg = [sb(f"Kpg{i}", [128, 512], F32) for i in range(NPG)]
        Kpgb = bufs(NPG)
        Vpg = [sb(f"Vpg{i}", [128, 512], F32) for i in range(NPG)]
        Vpgb = bufs(NPG)
        KTs = [sb(f"KTs{i}", [128, 4, 2048], BF16) for i in range(2)]
        KTsb = bufs(2)
        Vs = [sb(f"Vs{i}", [128, 16, 4, 130], BF16) for i in range(2)]
        Vsb = bufs(2)
        for i in range(2):
            POOL(lambda E: E.memset(Vs[i][:, :, :, 128:130], 1.0), [], [Vsb[i]])
        ptk = [ps("ptk0", [128, 4, 128], BF16)] * 2
        ptkb = [Buf()] * 2
        Kbf = [sb(f"Kbf{i}", [128, 512], BF16) for i in range(2)]
        Kbfb = bufs(2)
        Ps = [sb(f"Ps{i}", [128, 512], BF16) for i in range(2)]
        Psb = bufs(2)
        pgc = 0
        for s_ in range(16 if STOP > 3 else 0):
            sl = s_ % 2
            for j in range(NPAGE):
                g_ = pgc % NPG
                pgc += 1
                icol = idxp[:, s_ * 16 + j:s_ * 16 + j + 1]
                k.op("pool", lambda E: E.indirect_dma_start(out=Kpg[g_][:], out_offset=None, in_=dr["cache_k"][:, :],
                                                            in_offset=bass.IndirectOffsetOnAxis(ap=icol, axis=0)),
                     r=[gb2], w=[Kpgb[g_]], dsem=f"kg{g_}")
                k.op("pool", lambda E: E.indirect_dma_start(out=Vpg[g_][:], out_offset=None, in_=dr["cache_v"][:, :],
                                                            in_offset=bass.IndirectOffsetOnAxis(ap=icol, axis=0)),
                     r=[gb2], w=[Vpgb[g_]], dsem=f"vg{g_}")
                tk = j % 2
                DVE(lambda E: E.tensor_copy(out=Kbf[tk][:], in_=Kpg[g_][:]), [Kpgb[g_]], [Kbfb[tk]])
                for h in range(4):
                    PE(lambda E: E.transpose(out=ptk[tk][:, h, :], in_=Kbf[tk][:, h * 128:(h + 1) * 128], identity=c["ident_b"][:]),
                       [Kbfb[tk], cb], [ptkb[tk]])
                ACT(lambda E: E.activation(out=KTs[sl][:, :, j * 128:(j + 1) * 128], in_=ptk[tk][:], func=AF.Copy), [ptkb[tk]], [KTsb[sl]])
                ACT(lambda E: E.activation(out=Vs[sl][:, j, :, 0:128], in_=Vpg[g_][:].rearrange("p (h e) -> p h e", e=128), func=AF.Copy),
                    [Vpgb[g_]], [Vsb[sl]])
            for j in range(NPAGE):
                for h in range(4):
                    for cm in range(2):
                        col = (j * 4 + h) * 4
                        PE(lambda E: E.matmul(out=pScm[cm][:, col:col + 4], lhsT=KTs[sl][cm * 64:(cm + 1) * 64, h, j * 128:(j + 1) * 128],
                                              rhs=qkT[cm * 64:(cm + 1) * 64, h, s_ * 4:(s_ + 1) * 4], start=True, stop=True),
                           [KTsb[sl], gb], [pScb])
            for cm in range(2):
                ACT(lambda E: E.activation(out=Ps[sl][:, cm * 256:(cm + 1) * 256], in_=pScm[cm][:, 0:256], func=AF.Exp), [pScb], [Psb[sl]])
            for h in range(4):
                for cm in range(2):
                    hc = h * 2 + cm
                    bk, off = divmod(hc, 3)
                    for j in range(NPAGE):
                        col = cm * 256 + (j * 4 + h) * 4
                        PE(lambda E: E.matmul(out=pOs[0:4, bk, off * 130:off * 130 + 129], lhsT=Ps[sl][:, col:col + 4],
                                              rhs=Vs[sl][:, j, h, 0:129], start=(j == 0), stop=False), [Psb[sl], Vsb[sl]], [pOsb])
                    PE(lambda E: E.matmul(out=pOs[0:4, bk, off * 130:off * 130 + 129], lhsT=PnT[:, hc, s_ * 4:(s_ + 1) * 4],
                                          rhs=Vn[:, h, 0:129], start=False, stop=True), [gb2, gb], [pOsb])
            for bk in range(3):
                ncol = 3 if bk < 2 else 2
                DVE(lambda E: E.tensor_copy(out=OA[:, s_, bk * 3:bk * 3 + ncol, 0:129],
                                            in_=pOs[0:4, bk, 0:ncol * 130].rearrange("p (a b) -> p a b", b=130)[:, :, 0:129]), [pOsb], [OAb])
        k.barrier()
        lp.close()
        if STOP <= 4:
            return smp
        sb = _sb_outer
        rden = sb("rden", [4, 16, 8], F32)
        A1 = sb("A1", [4, 64, 128], F32)
        A2 = sb("A2", [4, 64, 128], F32)
        ss = sb("ss", [4, 64], F32)
        oan = sb("oan", [4, 64, 128], BF16)
        OA5 = OA[:].rearrange("p s (h c) e -> p (s h) c e", c=2)
        rd3 = rden[:].rearrange("p s (h c) -> p (s h) c", c=2)
        DVE(lambda E: E.reciprocal(out=rden[:], in_=OA[:, :, :, 128]), [OAb], [gb2])
        DVE(lambda E: E.tensor_scalar(out=rd3[:, :, 1:2], in0=rd3[:, :, 1:2], scalar1=c["neglam"][0:4, :], scalar2=None, op0=ALU.mult),
            [gb2, cb], [gb2])
        DVE(lambda E: E.tensor_tensor(out=A1[:], in0=OA5[:, :, 0, 0:128], in1=rd3[:, :, 0:1].to_broadcast([4, 64, 128]), op=ALU.mult),
            [OAb, gb2], [gb2])
        DVE(lambda E: E.tensor_tensor(out=A2[:], in0=OA5[:, :, 1, 0:128], in1=rd3[:, :, 1:2].to_broadcast([4, 64, 128]), op=ALU.mult),
            [OAb, gb2], [gb2])
        DVE(lambda E: E.tensor_tensor(out=A1[:], in0=A1[:], in1=A2[:], op=ALU.add), [gb2], [gb2])
        DVE(lambda E: E.tensor_tensor(out=A2[:], in0=A1[:], in1=A1[:], op=ALU.mult), [gb2], [gb2])
        DVE(lambda E: E.tensor_reduce(out=ss[:], in_=A2[:], axis=AX.X, op=ALU.add), [gb2], [gb2])
        rsqrt_col(k, ss[:], gb2, ss[:], gb2, 128.0, SUBLN_EPS)
        DVE(lambda E: E.tensor_tensor(out=oan[:], in0=A1[:], in1=ss[:].unsqueeze(2).to_broadcast([4, 64, 128]), op=ALU.mult), [gb2], [gb2])
        pT = ptk[0][:, :, 0:64]
        for s_ in range(16):
            for h in range(4):
                PE(lambda E: E.transpose(out=pT[:, h, s_ * 4:(s_ + 1) * 4], in_=oan[0:4, s_ * 4 + h, :], identity=c["ident_b"][0:4, 0:4]),
                   [gb2, cb], [gb2])
        ACT(lambda E: E.activation(out=smp["mixT"][:, 0:4, :], in_=pT, func=AF.Copy), [gb2], [mb])
        k.barrier()
    return smp


def build(L=8192, sample=True, debug=0, pool_rows=2560 * 128):
    nc = bass.Bass("TRN2", target_bir_lowering=False)
    NT = L // 128
    T2 = L // 4
    NT2 = T2 // 128
    dr = {}

    def din(name, shape, dt=F32):
        dr[name] = nc.dram_tensor(name, shape, dt, kind="ExternalInput").ap()

    def dout(name, shape, dt=F32):
        dr[name] = nc.dram_tensor(name, shape, dt, kind="ExternalOutput").ap()

    din("xp", [L if debug != 7 else 128, D])
    din("xp2", [T2, D])
    din("xs", [64, D])
    din("w_in_h", [D, 896])
    din("w_in", [D, NIN])
    din("w_out", [D, D])
    din("w_gate", [NFF, 128, 8, 128])
    din("w_up", [NFF, 128, 8, 128])
    din("w_down", [DFF, D])
    din("cache_k", [pool_rows, 512])
    din("cache_v", [pool_rows, 512])
    din("state", [16, 4, 128, 128])
    din("pt", [1, 256], I32)
    din("idx_mix", [128, 8], I32)
    din("idx_ssq", [128, 4], I32)
    for nm, shp in (("c_ident", [128, 128]), ("c_mask", [128, 128]), ("c_mask64", [128, 64]), ("c_bmask", [64, 64]),
                    ("c_sel", [64, 16]), ("c_cosp", [128, NT, 8]), ("c_sinp", [128, NT, 8]), ("c_coss", [64, 8]),
                    ("c_sins", [64, 8]), ("c_iota", [128, 1]), ("an", [128, 8]), ("fnc", [128, 8]), ("lamv", [1, 256]),
                    ("lbl_h", [128, 2]), ("lbl_a", [128, 8]), ("hn_h", [128, 1]), ("hn_a", [128, 4]), ("subln", [128, 1]),
                    ("fin", [1, 1024])):
        din(nm, shp)
    dout("y_p", [T2, D])
    dout("y_s", [64, D])
    dout("k_out", [L, 128])
    dout("v_out", [L, 128])
    dout("s_out", [128, 128])
    dout("ks_out", [64, 512])
    dout("vs_out", [64, 512])
    dout("ss_out", [16, 4, 128, 128])
    dr["xch"] = nc.dram_tensor("xch", [4 * 256, T2], BF16).ap()
    dr["xssq"] = nc.dram_tensor("xssq", [4 * 128, NT2], F32).ap()
    dr["xg"] = nc.dram_tensor("xg", [4 * 4 * 256, T2], BF16).ap()
    dr["xgs"] = nc.dram_tensor("xgs", [4 * 4 * 128, NT2], F32).ap()
    dr["xgb"] = Buf()
    dr["xgsb"] = Buf()
    with ExitStack() as es:
        block = es.enter_context(nc.Block())

        def body(_):
            k = KB(nc, es)
            c = load_consts(k, es, dr, L, True)
            if debug == 7:
                smp = phaseS(k, c, dr, es)
                k.op("pool", lambda E: E.dma_start(out=dr["y_s"][:, 0:512].bitcast(BF16).rearrange("t (a b) -> t a b", a=8)[:, :, 0:128],
                                                   in_=smp["mixT"][:].rearrange("p a b -> p a b")), r=[smp["b"]], dsem="dbg")
                k.barrier()
                return
            phase1(k, c, dr, L)
            k.barrier()
            groups = [[0, 1, 2, 3], [4, 5, 6, 7]]
            if debug != 8:
                for rr_ in range(4):
                    k.op("pool", lambda E: E.collective_compute(
                        "AllGather", ALU.bypass, replica_groups=groups,
                        ins=[dr["xch"][rr_ * 256:(rr_ + 1) * 256, :]], outs=[dr["xg"][rr_ * 1024:(rr_ + 1) * 1024, :]]),
                        w=[dr["xgb"]], dsem="cc1", inc=CC_INC)
                k.op("pool", lambda E: E.collective_compute("AllGather", ALU.bypass, replica_groups=groups,
                                                            ins=[dr["xssq"][:, :]], outs=[dr["xgs"][:, :]]),
                     w=[dr["xgsb"]], dsem="cc2", inc=CC_INC)
            smp = phaseS(k, c, dr, es) if sample else None
            phase2(k, c, dr, L, smp)
            k.barrier()

        block.sync(body)
    return nc


CC_INC = 1

def host_consts(L):
    NT = L // 128
    cst = {}
    cst["c_ident"] = np.eye(128, dtype=np.float32)
    p = np.arange(128)
    cst["c_mask"] = (p[:, None] <= p[None, :]).astype(np.float32)
    cst["c_mask64"] = ((p[:, None] % 64) <= np.arange(64)[None, :]).astype(np.float32)
    t = np.arange(64)
    cst["c_bmask"] = ((t[:, None] // 4 == t[None, :] // 4) & (t[:, None] <= t[None, :])).astype(np.float32)
    cst["c_sel"] = (t[:, None] // 4 == np.arange(16)[None, :]).astype(np.float32)
    half = 8
    inv_freq = (1.0 / (np.float32(ROPE_THETA) ** (np.arange(half, dtype=np.float32) * np.float32(2.0) / np.float32(16)))).astype(np.float32)
    pos = np.arange(L, dtype=np.float32)
    ang = (pos[:, None] * inv_freq[None, :]).astype(np.float32)
    cst["c_cosp"] = np.ascontiguousarray(np.cos(ang).astype(np.float32).reshape(NT, 128, 8).transpose(1, 0, 2))
    cst["c_sinp"] = np.ascontiguousarray(np.sin(ang).astype(np.float32).reshape(NT, 128, 8).transpose(1, 0, 2))
    poss = (PAST + (np.arange(64) % 4)).astype(np.float32)
    angs = (poss[:, None] * inv_freq[None, :]).astype(np.float32)
    cst["c_coss"] = np.cos(angs).astype(np.float32)
    cst["c_sins"] = np.sin(angs).astype(np.float32)
    cst["c_iota"] = np.arange(128, dtype=np.float32).reshape(128, 1)
    return cst


def make_in_maps(inp, L):
    cst = host_consts(L)
    f = lambda a: np.ascontiguousarray(np.asarray(a, dtype=np.float32))
    w_in = np.asarray(inp["w_in"])[0]
    segs = {"qa": 0, "ka": 1, "va": 2, "qh": 3, "fh": 4, "ih": 5, "gh": 6}
    order = ["qa", "ka", "va", "ih", "qh", "fh", "gh"]
    lb = np.asarray(inp["hgrn_lb_logits"])
    maps = []
    for cidx in range(8):
        b, hd = divmod(cidx, 4)
        m = dict(cst)
        m["xp"] = f(np.asarray(inp["x_prompt"])[b, :L])
        m["w_in_h"] = f(np.concatenate([w_in[:, segs[s] * 512 + hd * 128: segs[s] * 512 + (hd + 1) * 128] for s in order], axis=1))
        m["an"] = f(np.asarray(inp["attn_norm"])[0].reshape(8, 128).T)
        m["fnc"] = f(np.asarray(inp["ffn_norm"])[0].reshape(8, 128).T)
        m["lamv"] = f(np.concatenate([np.asarray(inp[n])[0] for n in ("lambda_q1", "lambda_k1", "lambda_q2", "lambda_k2")]).reshape(1, 256))
        m["lbl_h"] = f(lb[:, hd * 128:(hd + 1) * 128].T)
        m["lbl_a"] = f(lb.reshape(2, 4, 128).transpose(2, 0, 1).reshape(128, 8))
        hn = np.asarray(inp["hgrn_norm"])[0]
        m["hn_h"] = f(hn[hd * 128:(hd + 1) * 128].reshape(128, 1))
        m["hn_a"] = f(hn.reshape(4, 128).T)
        m["subln"] = f(np.asarray(inp["subln_w"])[0].reshape(128, 1))
        m["fin"] = f(np.asarray(inp["final_norm"]).reshape(1, 1024))
        T2 = L // 4
        m["xp2"] = f(np.asarray(inp["x_prompt"])[b, hd * T2:(hd + 1) * T2])
        m["xs"] = f(np.asarray(inp["x_sample"])[cidx * 16:(cidx + 1) * 16].reshape(64, D))
        m["w_in"] = f(w_in)
        m["w_out"] = f(np.asarray(inp["w_out"])[0])
        m["w_gate"] = f(np.asarray(inp["w_gate"])[0].reshape(8, 128, NFF, 128).transpose(2, 1, 0, 3))
        m["w_up"] = f(np.asarray(inp["w_up"])[0].reshape(8, 128, NFF, 128).transpose(2, 1, 0, 3))
        m["w_down"] = f(np.asarray(inp["w_down"])[0])
        m["cache_k"] = np.asarray(inp["cache_k"], dtype=np.float32).reshape(2560 * 128, 512)
        m["cache_v"] = np.asarray(inp["cache_v"], dtype=np.float32).reshape(2560 * 128, 512)
        m["state"] = f(np.asarray(inp["state_hgrn"])[0, cidx * 16:(cidx + 1) * 16])
        m["pt"] = np.ascontiguousarray(np.asarray(inp["page_table"], dtype=np.int32)[cidx * 16:(cidx + 1) * 16].reshape(1, 256))
        pp = np.arange(128, dtype=np.int32)
        r_ = hd
        m["idx_mix"] = np.ascontiguousarray(np.stack([r_ * 1024 + (kc % 4) * 256 + (kc // 4) * 128 + pp for kc in range(8)], axis=1).astype(np.int32))
        m["idx_ssq"] = np.ascontiguousarray(np.stack([h * 512 + r_ * 128 + pp for h in range(4)], axis=1).astype(np.int32))
        maps.append(m)
    return maps


_NC_CACHE = {}


def kernel(**inp):
    L = 8192
    if L not in _NC_CACHE:
        _NC_CACHE[L] = build(L=L)
    nc = _NC_CACHE[L]
    maps = make_in_maps(inp, L)
    res = run_bass_kernel_spmd(nc, maps, core_ids=list(range(8)))
    return assemble(res.results, L)


def assemble(R, L):
    T2 = L // 4
    y_p = np.zeros((2, L, D), np.float32)
    y_s = np.zeros((128, 4, D), np.float32)
    k_p = np.zeros((1, 2, L, 4, 2, 64), np.float32)
    v_p = np.zeros((1, 2, L, 4, 128), np.float32)
    s_p = np.zeros((1, 2, 4, 128, 128), np.float32)
    k_s = np.zeros((1, 128, 4, 4, 2, 64), np.float32)
    v_s = np.zeros((1, 128, 4, 4, 128), np.float32)
    s_s = np.zeros((1, 128, 4, 128, 128), np.float32)
    for cidx in range(8):
        b, hd = divmod(cidx, 4)
        r = R[cidx]
        y_p[b, hd * T2:(hd + 1) * T2] = np.asarray(r["y_p"])
        y_s[cidx * 16:(cidx + 1) * 16] = np.asarray(r["y_s"]).reshape(16, 4, D)
        k_p[0, b, :, hd] = np.asarray(r["k_out"]).reshape(L, 2, 64)
        v_p[0, b, :, hd] = np.asarray(r["v_out"])
        s_p[0, b, hd] = np.asarray(r["s_out"])
        k_s[0, cidx * 16:(cidx + 1) * 16] = np.asarray(r["ks_out"]).reshape(16, 4, 4, 2, 64)
        v_s[0, cidx * 16:(cidx + 1) * 16] = np.asarray(r["vs_out"]).reshape(16, 4, 4, 128)
        s_s[0, cidx * 16:(cidx + 1) * 16] = np.asarray(r["ss_out"])
    return (y_p, y_s, k_p, v_p, s_p, k_s, v_s, s_s)
```
